# Optimizing a Trainium2 kernel written in Bass

```python
import math
import jax, jax.numpy as jnp
from jax import lax
import numpy as np

D_MODEL = 2048
BATCH = 2
SEQ = 8192
DEPTH = 4

HEAD_DIM = 128
MIX_WIDTH = 2048
Q_BLOCK = 128
ROPE_THETA = 10000.0
EPS = 1e-6
MASKED = -1e30
FORCED = 1e9
NSA_HEADS = 8
NSA_KV_GROUPS = 2
NSA_CMP_LEN = 32
NSA_CMP_STRIDE = 16
NSA_CMP_HIDDEN = 256
NSA_SEL_LEN = 64
NSA_SEL_TOPN = 16
NSA_WINDOW = 512
DIFF_HEADS = 4
DIFF_QK_DIM = 64
DIFF_V_DIM = 128
DSA_HEADS = 12
DSA_KV_HEADS = 4
IDX_HEADS = 16
IDX_DIM = 64
DSA_TOPK_MAX = 256
MEM_HEADS = 4
MEM_LEN = 256

EVEN_SPLITS = (
    NSA_HEADS * HEAD_DIM,
    6 * NSA_KV_GROUPS * HEAD_DIM,
    NSA_HEADS * 3,
    NSA_HEADS * HEAD_DIM,
    DIFF_HEADS * 2 * DIFF_QK_DIM,
    DIFF_HEADS * 2 * DIFF_QK_DIM,
    DIFF_HEADS * DIFF_V_DIM,
    DIFF_HEADS * DIFF_V_DIM,
    MEM_HEADS * HEAD_DIM,
    MEM_HEADS * HEAD_DIM,
)
ODD_SPLITS = (
    DSA_HEADS * HEAD_DIM,
    DSA_KV_HEADS * HEAD_DIM,
    DSA_KV_HEADS * HEAD_DIM,
    IDX_HEADS * IDX_DIM,
    IDX_DIM,
    IDX_HEADS,
    DSA_HEADS * HEAD_DIM,
    MEM_HEADS * HEAD_DIM,
    MEM_HEADS * HEAD_DIM,
)
EVEN_IN = sum(EVEN_SPLITS)
ODD_IN = sum(ODD_SPLITS)

kernel_name = "hybrid_nsa_diff_dsa_memory_trunk"


def rms_norm(x, gain):
    xf = x.astype(jnp.float32)
    y = xf * lax.rsqrt(jnp.mean(xf * xf, axis=-1, keepdims=True) + EPS)
    return (y * gain.astype(jnp.float32)).astype(x.dtype)


def rope(x, pos):
    half = x.shape[-1] // 2
    inv = ROPE_THETA ** (-jnp.arange(half, dtype=jnp.float32) / half)
    ang = pos.astype(jnp.float32)[:, None] * inv[None, :]
    cos = jnp.cos(ang)[:, None, :]
    sin = jnp.sin(ang)[:, None, :]
    xf = x.astype(jnp.float32)
    x1, x2 = xf[..., :half], xf[..., half:]
    return jnp.concatenate([x1 * cos - x2 * sin, x2 * cos + x1 * sin], axis=-1).astype(x.dtype)


def masked_softmax(scores, mask):
    s = jnp.where(mask, scores.astype(jnp.float32), MASKED)
    m = jnp.max(s, axis=-1, keepdims=True)
    e = jnp.exp(s - m) * mask
    return e / jnp.maximum(jnp.sum(e, axis=-1, keepdims=True), 1e-30)


def split_cols(h, sizes):
    return jnp.split(h, np.cumsum(sizes)[:-1].tolist(), axis=-1)


def unblock(y):
    y = jnp.moveaxis(y, 0, 1)
    return y.reshape((y.shape[0], y.shape[1] * y.shape[2]) + y.shape[3:])


def nsa_attention(q, kv, gate_logits, pos, cmp_pos, cmp_w1, cmp_w2, qk_gain):
    B, S, H, Dh = q.shape
    G = NSA_KV_GROUPS
    R = H // G
    scale = Dh ** -0.5
    n_cmp = (S - NSA_CMP_LEN) // NSA_CMP_STRIDE + 1
    n_sel = S // NSA_SEL_LEN
    top_n = min(NSA_SEL_TOPN, n_sel)
    cmp_start = np.arange(n_cmp) * NSA_CMP_STRIDE
    cmp_last = jnp.asarray(cmp_start + NSA_CMP_LEN - 1, dtype=jnp.int32)
    cmp_gather = cmp_start[:, None] + np.arange(NSA_CMP_LEN)[None, :]
    sel_start = np.arange(n_sel) * NSA_SEL_LEN
    ov = np.clip(np.minimum(cmp_start[:, None] + NSA_CMP_LEN, sel_start[None, :] + NSA_SEL_LEN)
                 - np.maximum(cmp_start[:, None], sel_start[None, :]), 0, None) / NSA_CMP_LEN
    overlap = jnp.asarray(ov, dtype=jnp.float32)

    qg = rope(rms_norm(q, qk_gain[0]), pos).reshape(B, S, G, R, Dh)
    k_c_raw, v_c_raw, k_s, v_s, k_w, v_w = [kv[:, :, j] for j in range(6)]

    def compress(t, pe, w1, w2):
        blk = t[:, cmp_gather] + pe[:, None, :]
        blk = blk.transpose(0, 1, 3, 2, 4).reshape(B, n_cmp, G, NSA_CMP_LEN * Dh)
        hid = jax.nn.silu(jnp.einsum('bcgf,fh->bcgh', blk, w1))
        return jnp.einsum('bcgh,hd->bcgd', hid, w2)

    k_c = rope(rms_norm(compress(k_c_raw, cmp_pos[0], cmp_w1[0], cmp_w2[0]), qk_gain[1]), cmp_last)
    v_c = compress(v_c_raw, cmp_pos[1], cmp_w1[1], cmp_w2[1])
    k_s = rope(rms_norm(k_s, qk_gain[2]), pos)
    k_w = rope(rms_norm(k_w, qk_gain[3]), pos)

    def to_blocks(t):
        return t.reshape(B, n_sel, NSA_SEL_LEN, G, Dh).transpose(0, 3, 1, 2, 4)

    k_s_blk, v_s_blk = to_blocks(k_s), to_blocks(v_s)
    pad = ((0, 0), (NSA_WINDOW, 0), (0, 0), (0, 0))
    k_w_pad, v_w_pad = jnp.pad(k_w, pad), jnp.pad(v_w, pad)
    gather_blocks = jax.vmap(jax.vmap(lambda tb, ii: tb[ii]))
    sel_ids = jnp.arange(n_sel)
    sel_off = jnp.arange(NSA_SEL_LEN)
    win_off = jnp.arange(NSA_WINDOW + Q_BLOCK)
    vdt = v_c.dtype

    def block(bi):
        q0 = bi * Q_BLOCK
        t = q0 + jnp.arange(Q_BLOCK)
        qb = lax.dynamic_slice_in_dim(qg, q0, Q_BLOCK, 1)
        p_c = masked_softmax(jnp.einsum('btgrd,bcgd->bgrtc', qb, k_c) * scale,
                             cmp_last[None, :] <= t[:, None])
        o_c = jnp.einsum('bgrtc,bcgd->btgrd', p_c.astype(vdt), v_c)
        imp = jnp.einsum('bgrtc,cj->bgtj', p_c, overlap)
        cur = t // NSA_SEL_LEN
        visible = sel_ids[None, :] <= cur[:, None]
        forced = (sel_ids[None, :] == 0) | (sel_ids[None, :] >= cur[:, None] - 1)
        imp = jnp.where(visible, jnp.where(forced, FORCED, imp), MASKED)
        _, idx = lax.top_k(imp, top_n)
        k_sel = gather_blocks(k_s_blk, idx).reshape(B, G, Q_BLOCK, top_n * NSA_SEL_LEN, Dh)
        v_sel = gather_blocks(v_s_blk, idx).reshape(B, G, Q_BLOCK, top_n * NSA_SEL_LEN, Dh)
        tok = (idx[..., None] * NSA_SEL_LEN + sel_off).reshape(B, G, Q_BLOCK, top_n * NSA_SEL_LEN)
        mask_s = (tok <= t[None, None, :, None])[:, :, None]
        p_s = masked_softmax(jnp.einsum('btgrd,bgtkd->bgrtk', qb, k_sel) * scale, mask_s)
        o_s = jnp.einsum('bgrtk,bgtkd->btgrd', p_s.astype(vdt), v_sel)
        kw = lax.dynamic_slice_in_dim(k_w_pad, q0, NSA_WINDOW + Q_BLOCK, 1)
        vw = lax.dynamic_slice_in_dim(v_w_pad, q0, NSA_WINDOW + Q_BLOCK, 1)
        s_pos = q0 - NSA_WINDOW + win_off
        mask_w = ((s_pos[None, :] <= t[:, None]) & (s_pos[None, :] > t[:, None] - NSA_WINDOW)
                  & (s_pos[None, :] >= 0))
        p_w = masked_softmax(jnp.einsum('btgrd,bkgd->bgrtk', qb, kw) * scale, mask_w)
        o_w = jnp.einsum('bgrtk,bkgd->btgrd', p_w.astype(vdt), vw)
        g = jax.nn.sigmoid(lax.dynamic_slice_in_dim(gate_logits, q0, Q_BLOCK, 1).astype(jnp.float32))
        g = g.astype(vdt).reshape(B, Q_BLOCK, G, R, 3)
        return g[..., 0:1] * o_c + g[..., 1:2] * o_s + g[..., 2:3] * o_w

    out = unblock(lax.map(block, jnp.arange(S // Q_BLOCK)))
    return out.reshape(B, S, H * Dh)


def diff_attention(q, k, v, pos, qk_gain, lam_vecs, subln_gain, lambda_init):
    B, S, H, _, dq = q.shape
    q = rope(rms_norm(q, qk_gain[0]).reshape(B, S, 2 * H, dq), pos).reshape(B, S, H, 2, dq)
    k = rope(rms_norm(k, qk_gain[1]).reshape(B, S, 2 * H, dq), pos).reshape(B, S, H, 2, dq)
    lv = lam_vecs.astype(jnp.float32)
    lam = jnp.exp(jnp.sum(lv[0] * lv[1])) - jnp.exp(jnp.sum(lv[2] * lv[3])) + lambda_init
    key_pos = jnp.arange(S)

    def block(bi):
        q0 = bi * Q_BLOCK
        t = q0 + jnp.arange(Q_BLOCK)
        qb = lax.dynamic_slice_in_dim(q, q0, Q_BLOCK, 1)
        p = masked_softmax(jnp.einsum('bthcd,bshcd->bhcts', qb, k) * dq ** -0.5,
                           key_pos[None, :] <= t[:, None])
        a = p[:, :, 0] - lam * p[:, :, 1]
        return jnp.einsum('bhts,bshd->bthd', a.astype(v.dtype), v)

    o = unblock(lax.map(block, jnp.arange(S // Q_BLOCK)))
    o = rms_norm(o, subln_gain) * (1.0 - lambda_init)
    return o.reshape(B, S, H * v.shape[-1])


def dsa_attention(q, k, v, q_idx, k_idx, w_idx, pos, qk_gain):
    B, S, H, Dh = q.shape
    G = DSA_KV_HEADS
    R = H // G
    top_k = min(DSA_TOPK_MAX, S // 4)
    scale = Dh ** -0.5
    qg = rope(rms_norm(q, qk_gain[0]), pos).reshape(B, S, G, R, Dh)
    k = rope(rms_norm(k, qk_gain[1]), pos)
    q_idx = rope(q_idx, pos)
    k_idx = rope(k_idx[:, :, None, :], pos)[:, :, 0]
    w_idx = w_idx * (IDX_HEADS ** -0.5 * IDX_DIM ** -0.5)
    key_pos = jnp.arange(S)
    gather = jax.vmap(lambda tt, ii: tt[ii])

    def block(bi):
        q0 = bi * Q_BLOCK
        t = q0 + jnp.arange(Q_BLOCK)
        qb = lax.dynamic_slice_in_dim(qg, q0, Q_BLOCK, 1)
        qi = lax.dynamic_slice_in_dim(q_idx, q0, Q_BLOCK, 1)
        wi = lax.dynamic_slice_in_dim(w_idx, q0, Q_BLOCK, 1)
        rel = jax.nn.relu(jnp.einsum('bthd,bsd->bths', qi, k_idx)).astype(jnp.float32)
        score = jnp.einsum('bths,bth->bts', rel, wi.astype(jnp.float32))
        causal = key_pos[None, :] <= t[:, None]
        _, idx = lax.top_k(jnp.where(causal, score, MASKED), top_k)
        kg = gather(k, idx)
        vg = gather(v, idx)
        ok = (idx <= t[None, :, None])[:, None, None]
        p = masked_softmax(jnp.einsum('btgrd,btkgd->bgrtk', qb, kg) * scale, ok)
        return jnp.einsum('bgrtk,btkgd->btgrd', p.astype(v.dtype), vg)

    out = unblock(lax.map(block, jnp.arange(S // Q_BLOCK)))
    return out.reshape(B, S, H * Dh)


def memory_attention(q, mem_n, w_kv, qk_gain):
    B, S, H, Dh = q.shape
    kv = jnp.einsum('bmd,dc->bmc', mem_n, w_kv).reshape(B, mem_n.shape[1], 2, H, Dh)
    k = rms_norm(kv[:, :, 0], qk_gain[1])
    v = kv[:, :, 1]
    q = rms_norm(q, qk_gain[0])
    p = jax.nn.softmax((jnp.einsum('bshd,bmhd->bhsm', q, k) * Dh ** -0.5).astype(jnp.float32), axis=-1)
    return jnp.einsum('bhsm,bmhd->bshd', p.astype(v.dtype), v).reshape(B, S, H * Dh)


def setup_inputs(seed: int = 0) -> dict:
    key = jax.random.key(seed)
    ks = jax.random.split(key, 17)
    n_even = (DEPTH + 1) // 2
    n_odd = DEPTH // 2

    def nrm(k, shape, scale):
        return jax.random.normal(k, shape, jnp.float32) * scale

    def gain(k, shape):
        return 1.0 + 0.02 * jax.random.normal(k, shape, jnp.float32)

    return {
        "x": nrm(ks[0], (BATCH, SEQ, D_MODEL), 1.0),
        "mem": nrm(ks[1], (BATCH, MEM_LEN, D_MODEL), 1.0),
        "norm_gain": gain(ks[2], (DEPTH, D_MODEL)),
        "mem_norm_gain": gain(ks[3], (D_MODEL,)),
        "mem_w_kv": nrm(ks[4], (DEPTH, D_MODEL, 2 * MEM_HEADS * HEAD_DIM), D_MODEL ** -0.5),
        "mem_qk_gain": gain(ks[5], (DEPTH, 2, HEAD_DIM)),
        "w_out": nrm(ks[6], (DEPTH, MIX_WIDTH, D_MODEL), MIX_WIDTH ** -0.5),
        "even_w_in": nrm(ks[7], (n_even, D_MODEL, EVEN_IN), D_MODEL ** -0.5),
        "nsa_qk_gain": gain(ks[8], (n_even, 4, HEAD_DIM)),
        "nsa_cmp_pos": nrm(ks[9], (n_even, 2, NSA_CMP_LEN, HEAD_DIM), 0.3),
        "nsa_cmp_w1": nrm(ks[10], (n_even, 2, NSA_CMP_LEN * HEAD_DIM, NSA_CMP_HIDDEN), (NSA_CMP_LEN * HEAD_DIM) ** -0.5),
        "nsa_cmp_w2": nrm(ks[11], (n_even, 2, NSA_CMP_HIDDEN, HEAD_DIM), NSA_CMP_HIDDEN ** -0.5),
        "diff_qk_gain": gain(ks[12], (n_even, 2, DIFF_QK_DIM)),
        "diff_lambda": nrm(ks[13], (n_even, 4, DIFF_QK_DIM), 0.1),
        "diff_subln_gain": gain(ks[14], (n_even, DIFF_V_DIM)),
        "odd_w_in": nrm(ks[15], (n_odd, D_MODEL, ODD_IN), D_MODEL ** -0.5),
        "dsa_qk_gain": gain(ks[16], (n_odd, 2, HEAD_DIM)),
    }


def reference(x, mem, norm_gain, mem_norm_gain, mem_w_kv, mem_qk_gain, w_out, even_w_in,
              nsa_qk_gain, nsa_cmp_pos, nsa_cmp_w1, nsa_cmp_w2, diff_qk_gain, diff_lambda,
              diff_subln_gain, odd_w_in, dsa_qk_gain):
    B, S, _ = x.shape
    pos = jnp.arange(S)
    mem_n = rms_norm(mem, mem_norm_gain)
    for i in range(DEPTH):
        h = rms_norm(x, norm_gain[i])
        if i % 2 == 0:
            e = i // 2
            proj = jnp.einsum('bsd,dc->bsc', h, even_w_in[e])
            a_q, a_kv, a_g, a_z, b_q, b_k, b_v, b_z, m_q, m_z = split_cols(proj, EVEN_SPLITS)
            y_a = nsa_attention(a_q.reshape(B, S, NSA_HEADS, HEAD_DIM),
                                a_kv.reshape(B, S, 6, NSA_KV_GROUPS, HEAD_DIM),
                                a_g.reshape(B, S, NSA_HEADS, 3), pos,
                                nsa_cmp_pos[e], nsa_cmp_w1[e], nsa_cmp_w2[e], nsa_qk_gain[e])
            lambda_init = 0.8 - 0.6 * math.exp(-0.3 * i)
            y_b = diff_attention(b_q.reshape(B, S, DIFF_HEADS, 2, DIFF_QK_DIM),
                                 b_k.reshape(B, S, DIFF_HEADS, 2, DIFF_QK_DIM),
                                 b_v.reshape(B, S, DIFF_HEADS, DIFF_V_DIM), pos,
                                 diff_qk_gain[e], diff_lambda[e], diff_subln_gain[e], lambda_init)
            y_m = memory_attention(m_q.reshape(B, S, MEM_HEADS, HEAD_DIM), mem_n, mem_w_kv[i], mem_qk_gain[i])
            y = jnp.concatenate([y_a * jax.nn.silu(a_z), y_b * jax.nn.silu(b_z),
                                 y_m * jax.nn.silu(m_z)], axis=-1)
        else:
            o = i // 2
            proj = jnp.einsum('bsd,dc->bsc', h, odd_w_in[o])
            c_q, c_k, c_v, i_q, i_k, i_w, c_z, m_q, m_z = split_cols(proj, ODD_SPLITS)
            y_c = dsa_attention(c_q.reshape(B, S, DSA_HEADS, HEAD_DIM),
                                c_k.reshape(B, S, DSA_KV_HEADS, HEAD_DIM),
                                c_v.reshape(B, S, DSA_KV_HEADS, HEAD_DIM),
                                i_q.reshape(B, S, IDX_HEADS, IDX_DIM), i_k, i_w, pos, dsa_qk_gain[o])
            y_m = memory_attention(m_q.reshape(B, S, MEM_HEADS, HEAD_DIM), mem_n, mem_w_kv[i], mem_qk_gain[i])
            y = jnp.concatenate([y_c * jax.nn.silu(c_z), y_m * jax.nn.silu(m_z)], axis=-1)
        x = x + jnp.einsum('bsc,cd->bsd', y, w_out[i])
    return x
```

```python
import math
from contextlib import ExitStack
import numpy as np
import ml_dtypes
import concourse.bass as bass
import concourse.mybir as mybir
from concourse.bass_utils import run_bass_kernel_spmd

F32 = mybir.dt.float32
BF16 = mybir.dt.bfloat16
ALU = mybir.AluOpType
AF = mybir.ActivationFunctionType
AX = mybir.AxisListType
NPBF = ml_dtypes.bfloat16

D = 2048
KC = D // 128
EPS = 1e-6
NEG = -30000.0
ROPE_THETA = 10000.0

EVEN_SPLITS = (1024, 1536, 24, 1024, 512, 512, 512, 512, 512, 512)
ODD_SPLITS = (1536, 512, 512, 1024, 64, 16, 1536, 512, 512)


class Prog:
    def __init__(self, nc):
        self.nc = nc
        self.ins = []
        self.last_w = {}
        self.readers = {}
        self.pending = {}
        self.last_on = {}
        self.recent_dma = []
        self.recent_cc = []

    def barrier(self):
        deps = set(self.last_on.values()) | set(self.recent_dma[-12:]) | set(self.recent_cc[-4:])
        for e in ('pe', 'act', 'dve', 'pool', 'sp'):
            self.pending[e] = set(deps) | self.pending.get(e, set())

    def op(self, eng, fn, r=(), w=(), cc=False):
        i = len(self.ins)
        deps = set()
        if eng in self.pending:
            deps |= self.pending.pop(eng)
        self.last_on[eng] = i
        if eng == 'sp':
            self.recent_dma.append(i)
        if cc:
            self.recent_cc.append(i)
        for k in r:
            if k in self.last_w:
                deps.add(self.last_w[k])
        for k in w:
            if k in self.last_w:
                deps.add(self.last_w[k])
            deps.update(self.readers.get(k, ()))
        self.ins.append([eng, fn, deps, cc])
        for k in r:
            self.readers.setdefault(k, []).append(i)
        for k in w:
            self.last_w[k] = i
            self.readers[k] = []
        return i

    def mm(self, out, lhsT, rhs, start=True, stop=True, r=(), w=()):
        return self.op('pe', lambda e: e.matmul(out, lhsT, rhs, start=start, stop=stop), r, w)

    def tr(self, out, in_, ident, r=(), w=()):
        return self.op('pe', lambda e: e.transpose(out, in_, ident), r, w)

    def act(self, out, in_, func, r=(), w=(), **kw):
        return self.op('act', lambda e: e.activation(out, in_, func, **kw), r, w)

    def dma(self, out, in_, r=(), w=(), q='sp'):
        return self.op(q, lambda e: e.dma_start(out=out, in_=in_), r, w)

    def ts(self, eng, out, in0, s1, s2, op0, op1=None, r=(), w=(), **kw):
        if op1 is None:
            return self.op(eng, lambda e: e.tensor_scalar(out, in0, s1, None, op0, **kw), r, w)
        return self.op(eng, lambda e: e.tensor_scalar(out, in0, s1, s2, op0, op1, **kw), r, w)

    def tt(self, eng, out, in0, in1, op, r=(), w=()):
        return self.op(eng, lambda e: e.tensor_tensor(out, in0, in1, op), r, w)

    def stt(self, eng, out, in0, scalar, in1, op0, op1, r=(), w=()):
        return self.op(eng, lambda e: e.scalar_tensor_tensor(out, in0, scalar, in1, op0, op1), r, w)

    def cp(self, eng, out, in_, r=(), w=()):
        return self.op(eng, lambda e: e.tensor_copy(out, in_), r, w)

    def memset(self, eng, ap, val, r=(), w=()):
        return self.op(eng, lambda e: e.memset(ap, val), r, w)

    def emit(self, es):
        nc = self.nc
        ENGS = ['pe', 'act', 'dve', 'pool', 'sp', 'dq']
        ND = 12
        SEM_LIM = 30000
        n = len(self.ins)
        eng_of = [x[0] for x in self.ins]
        needed = [False] * n
        for i, (eng, fn, deps, cc) in enumerate(self.ins):
            if eng == 'pe':
                deps = {d for d in deps if eng_of[d] != 'pe'}
                self.ins[i][2] = deps
            for d in deps:
                needed[d] = True
        comp = [None] * n
        cnt = {e: 0 for e in ENGS}
        nsem_needed = {e: 1 for e in ENGS}
        dma_k = {'sp': 0, 'dq': 0}
        cc_k = 0
        NCC = 4
        for i, (eng, fn, deps, cc) in enumerate(self.ins):
            if cc:
                comp[i] = ('dcc', cc_k % NCC, 16 * (cc_k // NCC + 1))
                cc_k += 1
            elif eng in ('sp', 'dq'):
                k = dma_k[eng]
                dma_k[eng] += 1
                comp[i] = ('d' + eng, k % ND, 16 * (k // ND + 1))
            elif needed[i]:
                c = cnt[eng]
                cnt[eng] += 1
                comp[i] = (eng, c // SEM_LIM, c % SEM_LIM + 1)
                nsem_needed[eng] = c // SEM_LIM + 1
        sems = {}
        for e in ('pe', 'act', 'dve', 'pool'):
            for s in range(nsem_needed[e]):
                sems[(e, s)] = es.enter_context(nc.semaphore(f"s_{e}{s}"))
        for q in ('sp', 'dq'):
            if dma_k[q]:
                for s in range(ND):
                    sems[('d' + q, s)] = es.enter_context(nc.semaphore(f"s_{q}{s}"))
        if cc_k:
            for s in range(NCC):
                sems[('dcc', s)] = es.enter_context(nc.semaphore(f"s_cc{s}"))
        order = {e: [] for e in ENGS}
        for i in range(n):
            order[eng_of[i]].append(i)
        ins = self.ins
        final_dma = {q: dict() for q in ('sp', 'dq')}
        for i in range(n):
            if eng_of[i] in ('sp', 'dq'):
                c = comp[i]
                final_dma[eng_of[i]][(c[0], c[1])] = c[2]

        def run(engname, e):
            seen = {}
            dk = 0
            for i in order[engname]:
                eng, fn, deps, cc = ins[i]
                waits = {}
                for d in deps:
                    c = comp[d]
                    key = (c[0], c[1])
                    if waits.get(key, 0) < c[2]:
                        waits[key] = c[2]
                if engname in ('sp', 'dq') or cc:
                    c = comp[i]
                    if c[2] > 16:
                        key = (c[0], c[1])
                        if waits.get(key, 0) < c[2] - 16:
                            waits[key] = c[2] - 16
                for key, v in waits.items():
                    if seen.get(key, 0) >= v:
                        continue
                    seen[key] = v
                    e.wait_ge(sems[key], v)
                inst = fn(e)
                c = comp[i]
                if c is not None:
                    inst.then_inc(sems[(c[0], c[1])], 16 if (engname in ('sp', 'dq') or cc) else 1)
            if engname in ('sp', 'dq'):
                for key, v in final_dma[engname].items():
                    if seen.get(key, 0) < v:
                        e.wait_ge(sems[key], v)

        with nc.Block() as block:
            if order['sp']:
                @block.sync
                def _(e):
                    run('sp', e)
            if order['pe']:
                @block.tensor
                def _(e):
                    run('pe', e)
            if order['act'] or order['dq']:
                @block.scalar
                def _(e):
                    run('act', e)
            if order['dve']:
                @block.vector
                def _(e):
                    run('dve', e)
            if order['pool']:
                @block.gpsimd
                def _(e):
                    run('pool', e)


_UID = [0]


def _uname(name):
    _UID[0] += 1
    return f"{name}_{_UID[0]}"


class Pool:
    def __init__(self, nc, es, name, n, shape, dtype, psum=False):
        self.name = name
        name = _uname(name)
        self.n = n
        self.i = 0
        self.t = []
        for k in range(n):
            if psum:
                self.t.append(es.enter_context(nc.psum_tensor(f"{name}{k}", shape, dtype)))
            else:
                self.t.append(es.enter_context(nc.sbuf_tensor(f"{name}{k}", shape, dtype)))

    def next(self):
        k = self.i % self.n
        self.i += 1
        return self.t[k], (self.name, k)


def sb(nc, es, name, shape, dtype):
    return es.enter_context(nc.sbuf_tensor(_uname(name), shape, dtype))


class FX:
    def __init__(self, nc, P, S):
        self.nc, self.P, self.S = nc, P, S
        self.layer = 0
        self.cache = {}
        self.ext_shapes = {}

    def ext(self, name, shape, dtype, per_layer=False):
        nm = f"{name}_L{self.layer}" if per_layer else name
        if nm not in self.cache:
            self.cache[nm] = self.nc.dram_tensor(nm, shape, dtype, kind="ExternalInput").ap()
            self.ext_shapes[nm] = (name, per_layer, self.layer)
        return self.cache[nm]

    def internal(self, name, shape, dtype):
        nm = f"{name}_L{self.layer}"
        if nm not in self.cache:
            self.cache[nm] = self.nc.dram_tensor(nm, shape, dtype, kind="Internal").ap()
        return self.cache[nm]


def fm_map(parity, slot):
    if parity == 0:
        if slot < 8:
            return ('q', slot)
        if slot < 16:
            return ('k', slot - 8)
        if slot < 20:
            return ('q', 8 + slot - 16)
        if slot < 24:
            return ('k', 8 + slot - 20)
        return ('q', 12 + slot - 24)
    if slot < 12:
        return ('q', slot)
    if slot < 16:
        return ('k', slot - 12)
    if slot < 24:
        return ('q', 12 + slot - 16)
    if slot == 24:
        return ('k', 4)
    return ('q', 20 + slot - 25)


def tm_map(parity, dcol):
    if parity == 0:
        return {0: ('v', 0, 2), 256: ('v', 2, 2), 512: ('z', 0), 1024: ('z', 512), 1536: ('v', 4, 4),
                2048: ('z', 1024), 2560: ('z', 1536)}[dcol]
    return {0: ('v', 0, 4), 512: ('z', 0), 1024: ('z', 512), 1536: ('z', 1024), 2048: ('z', 1536)}[dcol]


NK = {0: 12, 1: 5}
NV = {0: 8, 1: 4}
NQS = {0: 16, 1: 24}


def own_positions(S, r):
    NT = S // 128
    NO = NT // 4
    pos = np.concatenate([np.arange(128) + (4 * j + r) * 128 for j in range(NO)])
    return pos


def rope_tables_fm(pos, dim):
    half = dim // 2
    inv = (ROPE_THETA ** (-np.arange(half, dtype=np.float32) / half)).astype(np.float32)
    ang = pos.astype(np.float32)[None, :] * inv[:, None]
    cos = np.cos(ang).astype(np.float32)
    sin = np.sin(ang).astype(np.float32)
    rows = np.arange(128) % half
    return cos[rows].astype(NPBF), sin[rows].astype(NPBF)


def rot_matrix_T(dim):
    half = dim // 2
    R = np.zeros((128, 128), np.float32)
    for m in range(128):
        blk = (m // dim) * dim
        w = m - blk
        if w < half:
            R[m, blk + w + half] = -1.0
        else:
            R[m, blk + w - half] = 1.0
    return R.T.copy().astype(NPBF)


def ones_block(dim):
    O = np.zeros((128, 128), np.float32)
    for m in range(128):
        blk = (m // dim) * dim
        O[blk:blk + dim, m] = 1.0 / dim
    return O.astype(NPBF)


def a_specs(parity):
    groups = []
    fm = [0]
    tmc = [0]
    tmf = [0]

    def fm_slot():
        fm[0] += 1
        return fm[0] - 1

    def add_fm_group(c0, nch, post, gain_idx, variant):
        for g0 in range(0, nch, 4):
            nn = min(4, nch - g0)
            items = [('fm', k * 128, post, gain_idx, variant, fm_slot()) for k in range(nn)]
            groups.append(dict(c0=c0 + g0 * 128, n=nn * 128, items=items))

    def add_tm(c0, ncols, func, dst='b'):
        for g0 in range(0, ncols, 512):
            nn = min(512, ncols - g0)
            if dst == 'b':
                dcol = tmc[0]
                tmc[0] += nn
            else:
                dcol = tmf[0]
                tmf[0] += nn
            groups.append(dict(c0=c0 + g0, n=nn, items=[('tm', 0, nn, func, dst, dcol)]))

    if parity == 0:
        c = 0
        add_fm_group(c, 8, 'nr', 0, 128); c += 1024
        add_fm_group(c, 4, 'raw', None, 128); c += 512
        groups.append(dict(c0=c, n=512, items=[('fm', 0, 'nr', 1, 128, fm_slot()), ('fm', 128, 'nr', 1, 128, fm_slot()),
                                               ('tm', 256, 256, 'copy', 'b', tmc[0])])); tmc[0] += 256; c += 512
        groups.append(dict(c0=c, n=512, items=[('fm', 0, 'nr', 2, 128, fm_slot()), ('fm', 128, 'nr', 2, 128, fm_slot()),
                                               ('tm', 256, 256, 'copy', 'b', tmc[0])])); tmc[0] += 256; c += 512
        add_tm(c, 24, 'sigmoid', 'f'); c += 24
        add_tm(c, 1024, 'silu'); c += 1024
        add_fm_group(c, 4, 'nr', 3, 64); c += 512
        add_fm_group(c, 4, 'nr', 4, 64); c += 512
        add_tm(c, 512, 'copy'); c += 512
        add_tm(c, 512, 'silu'); c += 512
        add_fm_group(c, 4, 'n', 5, 128); c += 512
        add_tm(c, 512, 'silu'); c += 512
        assert c == sum(EVEN_SPLITS)
    else:
        c = 0
        cq = c; c += 1536
        ck = c; c += 512
        cv = c; c += 512
        iq = c; c += 1024
        ik = c; c += 64
        iw = c; c += 16
        cz = c; c += 1536
        mq = c; c += 512
        mz = c; c += 512
        assert c == sum(ODD_SPLITS)
        groups.append(dict(c0=ik, n=80, items=[('tm', 64, 16, 'sign', 'f', 0), ('wrep',), ('fmrep', 0, 'r', None, 64, None)]))
        tmf[0] += 16
        add_fm_group(cq, 12, 'nr', 0, 128)
        add_fm_group(ck, 4, 'nr', 1, 128)
        add_tm(cv, 512, 'copy')
        for g0 in range(2):
            items = [('fm', k * 128, 'rw', None, 64, fm_slot(), g0 * 4 + k) for k in range(4)]
            groups.append(dict(c0=iq + g0 * 512, n=512, items=items))
        groups[0]['items'][2] = ('fmrep', 0, 'r', None, 64, fm_slot())
        add_tm(cz, 1536, 'silu')
        add_fm_group(mq, 4, 'n', 2, 128)
        add_tm(mz, 512, 'silu')
    return groups, fm[0], tmc[0], max(tmf[0], 1)


def build_A(parity, S, fx=None):
    NT = S // 128
    NO = NT // 4
    NOW = NO * 128
    NTG = NOW // 512
    groups, NFM, NTMC, NTMF = a_specs(parity)
    CIN = sum(EVEN_SPLITS) if parity == 0 else sum(ODD_SPLITS)
    NG = 6 if parity == 0 else 3
    if fx is None:
        nc = bass.Bass("TRN2", target_bir_lowering=False)
        x = nc.dram_tensor("x", [NOW, D], F32, kind="ExternalInput").ap()
        w_in = nc.dram_tensor("w_in", [D, CIN], F32, kind="ExternalInput").ap()
        ng = nc.dram_tensor("ng", [128, KC], F32, kind="ExternalInput").ap()
        gains = nc.dram_tensor("gains", [128, NG], F32, kind="ExternalInput").ap()
        tabs = nc.dram_tensor("tabs", [128, 4, NOW], BF16, kind="ExternalInput").ap()
        cmat = nc.dram_tensor("cmat", [128, 5, 128], BF16, kind="ExternalInput").ap()
        sel = nc.dram_tensor("sel", [16, 8, 128], BF16, kind="ExternalInput").ap() if parity == 1 else None
        fmo = nc.dram_tensor("fmo", [128, NFM, NOW], BF16, kind="ExternalOutput").ap()
        tmo = nc.dram_tensor("tmo", [NOW, NTMC], BF16, kind="ExternalOutput").ap()
        tmf = nc.dram_tensor("tmf", [NOW, NTMF], F32, kind="ExternalOutput").ap()
    else:
        nc = fx.nc
        x = fx.x_src
        w_in = fx.ext("w_in", [D, CIN], F32, True)
        ng = fx.ext("ng", [128, KC], F32, True)
        gains = fx.ext("gains", [128, NG], F32, True)
        tabs = fx.ext("tabs", [128, 4, NOW], BF16)
        cmat = fx.ext("cmat", [128, 5, 128], BF16)
        sel = fx.ext("sel", [16, 8, 128], BF16) if parity == 1 else None
        tmf = fx.internal("tmf", [NOW, NTMF], F32)
        fmq_d = fx.internal("fmq", [128, NQS[parity], NOW], BF16)
        kown_fm = fx.internal("kown_fm", [NK[parity] * 128, NOW], BF16)
        kown_v = fx.internal("kown_v", [NV[parity] * 128, NO * 129], BF16)
        tmz_d = fx.internal("tmz", [NOW, 2048], BF16)
        kown_fm_v = kown_fm.rearrange("(s d) n -> d s n", d=128)
        kown_v_v = kown_v.rearrange("(u p) (j d) -> p u j d", p=128, d=129)

    with ExitStack() as es:
        P = Prog(nc) if fx is None else fx.P
        hT = sb(nc, es, "hT", [128, KC, NOW], BF16)
        ng_t = sb(nc, es, "ng_t", [128, KC], F32)
        gains_t = sb(nc, es, "gains_t", [128, NG], F32)
        tabs_t = sb(nc, es, "tabs_t", [128, 4, NOW], BF16)
        cmat_t = sb(nc, es, "cmat_t", [128, 5, 128], BF16)
        eps_t = sb(nc, es, "eps_t", [128, 1], F32)
        xin = Pool(nc, es, "xin", 2, [128, D], F32)
        xs = Pool(nc, es, "xs", 1, [128, D], BF16)
        junk = sb(nc, es, "junk", [128, D], BF16)
        ssp = Pool(nc, es, "ss", 4, [128, 1], F32)
        wst = Pool(nc, es, "wst", 2, [128, KC // 2, 512], F32)
        wbf = Pool(nc, es, "wbf", 2, [128, KC, 512], BF16)
        psq = Pool(nc, es, "psq", 3, [128, 512], F32, psum=True)
        psb = Pool(nc, es, "psb", 2, [128, 512], F32, psum=True)
        psc = Pool(nc, es, "psc", 2, [128, 512], F32, psum=True)
        pst = Pool(nc, es, "pst", 1, [128, 8, 128], BF16, psum=True)
        sqb = Pool(nc, es, "sqb", 2, [128, 512], BF16)
        f1 = Pool(nc, es, "f1", 2, [128, 512], F32)
        f2 = Pool(nc, es, "f2", 2, [128, 512], F32)
        f3 = Pool(nc, es, "f3", 2, [128, 512], F32)
        qn = Pool(nc, es, "qn", 2, [128, 512], BF16)
        stg = Pool(nc, es, "stg", 5, [128, 512], BF16)
        stf = Pool(nc, es, "stf", 2, [128, 32], F32)
        stv = Pool(nc, es, "stv", 2, [128, 4, 129], BF16) if fx is not None else None
        if fx is not None:
            for t_ in stv.t:
                P.memset('pool', t_[:, :, 128:129], 1.0, w=[('stv', stv.t.index(t_))])
        wrep = sb(nc, es, "wrep", [128, KC, 128], BF16) if parity == 1 else None
        sel_t = sb(nc, es, "sel_t", [16, 8, 128], BF16) if parity == 1 else None
        aw16 = sb(nc, es, "aw16", [16, NOW], BF16) if parity == 1 else None
        if parity == 1:
            P.dma(sel_t[:], sel, w=['sel'])

        ident = cmat_t[:, 0, :]
        P.dma(ng_t[:], ng, w=['ng'])
        P.dma(gains_t[:], gains, w=['gains'])
        P.dma(tabs_t[:], tabs, w=['tabs'])
        P.dma(cmat_t[:], cmat, w=['cmat'])
        P.memset('dve', eps_t[:], EPS, w=['eps'])

        import os
        _p1 = os.environ.get('A_P1', 'full')
        for j in range(NO if not _p1.startswith('one') else 1):
            xt, xk = xin.next()
            P.dma(xt[:], x[j * 128:(j + 1) * 128, :], w=[xk])
            st, sk = ssp.next()
            P.act(junk[:], xt[:], AF.Square, r=[xk], w=['junk', sk], accum_out=st[:])
            s2, s2k = ssp.next()
            P.act(s2[:], st[:], AF.Sqrt, r=[sk, 'eps'], w=[s2k], scale=1.0 / D, bias=eps_t[:])
            P.op('dve', lambda e, a=s2: e.reciprocal(a[:], a[:]), r=[s2k], w=[s2k])
            xb, xbk = xs.next()
            P.ts('dve', xb[:], xt[:], s2[:], None, ALU.mult, r=[xk, s2k], w=[xbk])
            if _p1 == 'notr':
                continue
            for half in range(2 if _p1 != 'oneh' else 1):
                pt, ptk = pst.next()
                for q in range(8):
                    kc = half * 8 + q
                    P.tr(pt[:, q, :], xb[:, kc * 128:(kc + 1) * 128], ident, r=[xbk, 'cmat'], w=[ptk])
                for q in range(8):
                    kc = half * 8 + q
                    eng = 'dve'
                    if eng == 'pool':
                        P.act(hT[:, kc, j * 128:(j + 1) * 128], pt[:, q, :], AF.Copy, r=[ptk, 'ng'],
                              w=[('hT', j, kc)], scale=ng_t[:, kc:kc + 1])
                    else:
                        P.ts('dve', hT[:, kc, j * 128:(j + 1) * 128], pt[:, q, :], ng_t[:, kc:kc + 1], None,
                             ALU.mult, r=[ptk, 'ng'], w=[('hT', j, kc)])

        def hkeys(tiles):
            return [('hT', j, kc) for j in tiles for kc in range(KC)]

        def fm_unit(it, tg, wb, wbk, coff, post, gidx, slot, onesb, rotT, cosT, sinT):
            tiles = range(4 * tg, 4 * tg + 4)
            cols = slice(tg * 512, (tg + 1) * 512)
            pa, pak = psq.next()
            for kc in range(KC):
                if it[0] == 'fmrep':
                    lhsT = wrep[:, kc, :]
                    rk = [('wrep', kc)]
                else:
                    lhsT = wb[:, kc, coff:coff + 128]
                    rk = [(wbk, kc)]
                P.mm(pa[:], lhsT, hT[:, kc, cols], start=(kc == 0), stop=(kc == KC - 1),
                     r=rk + [('hT', j, kc) for j in tiles], w=[pak])
            so, sok = stg.next()
            if post == 'raw':
                P.act(so[:], pa[:], AF.Copy, r=[pak], w=[sok])
            elif post in ('nr', 'n'):
                sq, sqk = sqb.next()
                P.act(sq[:], pa[:], AF.Square, r=[pak], w=[sqk])
                yield
                pb, pbk = psb.next()
                P.mm(pb[:], onesb, sq[:], r=[sqk, 'cmat'], w=[pbk])
                sd, sdk = f1.next()
                P.act(sd[:], pb[:], AF.Sqrt, r=[pbk, 'eps'], w=[sdk], bias=eps_t[:])
                P.op('dve', lambda e, a=sd: e.reciprocal(a[:], a[:]), r=[sdk], w=[sdk])
                if post == 'n':
                    P.stt('dve', so[:], pa[:], gains_t[:, gidx:gidx + 1], sd[:], ALU.mult, ALU.mult,
                          r=[pak, sdk, 'gains'], w=[sok])
                else:
                    q_, qk = qn.next()
                    P.stt('dve', q_[:], pa[:], gains_t[:, gidx:gidx + 1], sd[:], ALU.mult, ALU.mult,
                          r=[pak, sdk, 'gains'], w=[qk])
                    yield
                    pc, pck = psc.next()
                    P.mm(pc[:], rotT, q_[:], r=[qk, 'cmat'], w=[pck])
                    t1, t1k = f2.next()
                    P.tt('pool', t1[:], q_[:], cosT[:, cols], ALU.mult, r=[qk, 'tabs'], w=[t1k])
                    t2, t2k = f3.next()
                    P.tt('dve', t2[:], pc[:], sinT[:, cols], ALU.mult, r=[pck, 'tabs'], w=[t2k])
                    P.tt('pool', so[:], t1[:], t2[:], ALU.add, r=[t1k, t2k], w=[sok])
            elif post == 'r':
                q_, qk = qn.next()
                P.act(q_[:], pa[:], AF.Copy, r=[pak], w=[qk])
                yield
                pc, pck = psc.next()
                P.mm(pc[:], rotT, q_[:], r=[qk, 'cmat'], w=[pck])
                t1, t1k = f2.next()
                P.tt('pool', t1[:], q_[:], cosT[:, cols], ALU.mult, r=[qk, 'tabs'], w=[t1k])
                t2, t2k = f3.next()
                P.tt('dve', t2[:], pc[:], sinT[:, cols], ALU.mult, r=[pck, 'tabs'], w=[t2k])
                P.tt('pool', so[:], t1[:], t2[:], ALU.add, r=[t1k, t2k], w=[sok])
            elif post == 'rw':
                cidx = it[6]
                q_, qk = qn.next()
                P.act(q_[:], pa[:], AF.Copy, r=[pak], w=[qk])
                yield
                pc, pck = psc.next()
                P.mm(pc[:], rotT, q_[:], r=[qk, 'cmat'], w=[pck])
                pw, pwk = psb.next()
                P.mm(pw[:], sel_t[:, cidx, :], aw16[:, cols], r=['sel', ('aw16', tg)], w=[pwk])
                t1, t1k = f2.next()
                P.tt('pool', t1[:], q_[:], cosT[:, cols], ALU.mult, r=[qk, 'tabs'], w=[t1k])
                t2, t2k = f3.next()
                P.tt('dve', t2[:], pc[:], sinT[:, cols], ALU.mult, r=[pck, 'tabs'], w=[t2k])
                P.tt('pool', t1[:], t1[:], t2[:], ALU.add, r=[t1k, t2k], w=[t1k])
                P.tt('dve', so[:], t1[:], pw[:], ALU.mult, r=[t1k, pwk], w=[sok])
            if fx is None:
                P.dma(fmo[:, slot, cols], so[:], r=[sok], w=[('fmo', slot, tg)])
            else:
                kind_, idx_ = fm_map(parity, slot)
                dst_ = fmq_d[:, idx_, cols] if kind_ == 'q' else kown_fm_v[:, idx_, cols]
                P.dma(dst_, so[:], r=[sok], w=[('fmo', slot, tg)])
            return
            yield

        fm_live = []

        def fm_push(g_):
            next(g_, None)
            for o_ in list(fm_live):
                try:
                    next(o_)
                except StopIteration:
                    fm_live.remove(o_)
            fm_live.append(g_)

        def fm_flush():
            while fm_live:
                for o_ in list(fm_live):
                    try:
                        next(o_)
                    except StopIteration:
                        fm_live.remove(o_)

        funcs = {'copy': AF.Copy, 'silu': AF.Silu, 'sigmoid': AF.Sigmoid, 'sign': AF.Sign}
        import os
        _lim = int(os.environ.get('A_GROUPS', '999'))
        for gi, g in enumerate(groups):
            if gi >= _lim:
                break
            n = g['n']
            wb, wbk = wbf.next()
            wv = w_in[:, g['c0']:g['c0'] + n].rearrange("(kc p) c -> p kc c", p=128)
            for hf in range(2):
                ws, wsk = wst.next()
                P.dma(ws[:, :, 0:n], wv[:, hf * 8:(hf + 1) * 8, :], w=[wsk])
                for q in range(8):
                    kc = hf * 8 + q
                    eng = ('dve', 'pool')[kc % 2]
                    P.cp(eng, wb[:, kc, 0:n], ws[:, q, 0:n], r=[wsk], w=[(wbk, kc)])
            wkeys = [(wbk, kc) for kc in range(KC)]
            for it in g['items']:
                if it[0] == 'wrep':
                    fm_flush()
                    for kc in range(KC):
                        for hh in range(2):
                            P.cp('pool', wrep[:, kc, hh * 64:(hh + 1) * 64], wb[:, kc, 0:64],
                                 r=[(wbk, kc)], w=[('wrep', kc)])
                    for tg in range(NTG):
                        cols = slice(tg * 512, (tg + 1) * 512)
                        pa, pak = psq.next()
                        for kc in range(KC):
                            P.mm(pa[0:16, :], wb[:, kc, 64:80], hT[:, kc, cols], start=(kc == 0), stop=(kc == KC - 1),
                                 r=[(wbk, kc)] + [('hT', j, kc) for j in range(4 * tg, 4 * tg + 4)], w=[pak])
                        P.act(aw16[:, cols], pa[0:16, :], AF.Abs, r=[pak], w=[('aw16', tg)], scale=1.0 / 32.0)
                    continue
                if it[0] in ('fm', 'fmrep'):
                    _, coff, post, gidx, variant, slot = it[:6]
                    vi = 0 if variant == 128 else 1
                    onesb = cmat_t[:, 1 + vi, :]
                    rotT = cmat_t[:, 3 + vi, :]
                    cosT = tabs_t[:, 2 * vi, :]
                    sinT = tabs_t[:, 2 * vi + 1, :]
                    for tg in range(NTG):
                        fm_push(fm_unit(it, tg, wb, wbk, coff, post, gidx, slot, onesb, rotT, cosT, sinT))
                elif it[0] == 'tm':
                    fm_flush()
                    _, coff, nn, func, dst, dcol = it
                    for j in range(NO):
                        pa, pak = psq.next()
                        for kc in range(KC):
                            P.mm(pa[:, 0:nn], hT[:, kc, j * 128:(j + 1) * 128], wb[:, kc, coff:coff + nn],
                                 start=(kc == 0), stop=(kc == KC - 1), r=[(wbk, kc), ('hT', j, kc)], w=[pak])
                        if dst == 'b' and fx is not None and tm_map(parity, dcol)[0] == 'v':
                            _, u0_, nu_ = tm_map(parity, dcol)
                            so, sok = stv.next()
                            P.act(so[:, 0:nu_, 0:128], pa[:, 0:nn].rearrange("p (u d) -> p u d", d=128), funcs[func],
                                  r=[pak], w=[sok])
                            P.dma(kown_v_v[:, u0_:u0_ + nu_, j, :], so[:, 0:nu_, :], r=[sok], w=[('tmo', dcol, j)])
                        elif dst == 'b':
                            so, sok = stg.next()
                            P.act(so[:, 0:nn], pa[:, 0:nn], funcs[func], r=[pak], w=[sok])
                            if fx is None:
                                P.dma(tmo[j * 128:(j + 1) * 128, dcol:dcol + nn], so[:, 0:nn], r=[sok], w=[('tmo', dcol, j)])
                            else:
                                zc_ = tm_map(parity, dcol)[1]
                                P.dma(tmz_d[j * 128:(j + 1) * 128, zc_:zc_ + nn], so[:, 0:nn], r=[sok], w=[('tmo', dcol, j)])
                        else:
                            so, sok = stf.next()
                            P.act(so[:, 0:nn], pa[:, 0:nn], funcs[func], r=[pak], w=[sok])
                            P.dma(tmf[j * 128:(j + 1) * 128, dcol:dcol + nn], so[:, 0:nn], r=[sok], w=[('tmf', dcol, j)])
        fm_flush()
        if fx is None:
            P.emit(es)
    return nc


def cmat_host():
    ident = np.eye(128, dtype=np.float32).astype(NPBF)
    return np.stack([ident, ones_block(128), ones_block(64), rot_matrix_T(128), rot_matrix_T(64)], axis=1)


def tabs_host(S, r):
    pos = own_positions(S, r)
    c128, s128 = rope_tables_fm(pos, 128)
    c64, s64 = rope_tables_fm(pos, 64)
    return np.stack([c128, s128, c64, s64], axis=1)


def geom(S):
    NT = S // 128
    NO = NT // 4
    NOW = NO * 128
    NC = NO // 4
    n_cmp = (S - 32) // 16 + 1
    NCT = (n_cmp + 127) // 128
    n_sel = S // 64
    return NT, NO, NOW, NC, n_cmp, NCT, n_sel


class BCtx:
    pass


def b_common(nc, es, P, S, parity, fx=None):
    NT, NO, NOW, NC, n_cmp, NCT, n_sel = geom(S)
    c = BCtx()
    c.S, c.NT, c.NO, c.NOW, c.NC = S, NT, NO, NOW, NC
    c.fx = fx
    NQ = 16 if parity == 0 else 24
    NZ = 2048
    if fx is None:
        dt = nc.dram_tensor
        c.ext = lambda name, shape, dtype, per_layer=False: dt(name, shape, dtype, kind="ExternalInput").ap()
        c.x = dt("x", [NOW, D], F32, kind="ExternalInput").ap()
        c.xo = dt("xo", [NOW, D], F32, kind="ExternalOutput").ap()
        c.fmq = dt("fmq", [128, NQ, NOW], BF16, kind="ExternalInput").ap()
        c.tmz = dt("tmz", [NOW, NZ], BF16, kind="ExternalInput").ap()
    else:
        c.ext = fx.ext
        c.x = fx.x_src
        c.xo = fx.x_dst
        c.fmq = fx.internal("fmq", [128, NQ, NOW], BF16)
        c.tmz = fx.internal("tmz", [NOW, NZ], BF16)
        c.tmf = fx.internal("tmf", [NOW, 24 if parity == 0 else 16], F32)
        c.kall_fm = fx.internal("kall_fm", [4 * NK[parity] * 128, NOW], BF16)
        c.kall_v = fx.internal("kall_v", [4 * NV[parity] * 128, NO * 129], BF16)
        c.kall_fm_v = c.kall_fm.rearrange("(r s d) (j p) -> d s j r p", r=4, d=128, p=128)
        c.kall_v_v = c.kall_v.rearrange("(r u p) (j d) -> p u j r d", r=4, p=128, d=129)
    c.w_out = c.ext("w_out", [D, D], F32, True)
    c.mem = c.ext("mem", [256, D], F32)
    c.w_kv = c.ext("w_kv", [D, 1024], F32, True)
    c.memg = c.ext("memg", [128, KC + 1], F32, True)
    c.cmat = c.ext("cmat", [128, 5, 128], BF16)
    c.dm = c.ext("dm", [128, 4, 128], BF16)

    def load_k(slot, kg_unfused):
        if fx is None:
            P.dma(c.bufK[:, 0:S], kg_unfused[:, slot, :], w=['bufK'])
        else:
            dv = c.bufK[:, 0:S].rearrange("d (j r p) -> d j r p", r=4, p=128)
            for r_ in range(4):
                P.dma(dv[:, :, r_, :], c.kall_fm_v[:, slot, :, r_, :], w=['bufK'])
    c.load_k = load_k

    def load_v(unit, vg_unfused):
        if fx is None:
            P.dma(c.bufV[:, 0:NT * 129], vg_unfused[unit], w=['bufV'])
        else:
            dv = c.bufV[:, 0:NT * 129].rearrange("p (j r d) -> p j r d", r=4, d=129)
            for r_ in range(4):
                P.dma(dv[:, :, r_, :], c.kall_v_v[:, unit, :, r_, :], w=['bufV'])
    c.load_v = load_v

    c.cmat_t = sb(nc, es, "cmat_t", [128, 5, 128], BF16)
    c.dm_t = sb(nc, es, "dm_t", [128, 4, 128], BF16)
    c.memg_t = sb(nc, es, "memg_t", [128, KC + 1], F32)
    c.eps_t = sb(nc, es, "eps_t", [128, 1], F32)
    c.y_all = sb(nc, es, "y_all", [128, NO, D], BF16)
    KCOLS = max(S, 8192)
    VCOLS = max(NT * 129, 8192)
    c.big = sb(nc, es, "big", [128, KCOLS + VCOLS], BF16)
    c.bufK = c.big[:, 0:KCOLS]
    c.bufV = c.big[:, KCOLS:KCOLS + VCOLS]
    c.qT = sb(nc, es, "qT", [128, 4, NOW], BF16)
    c.wst = Pool(nc, es, "wst", 1, [128, KC // 2, 512], F32)
    c.memT = c.bufV[:, 0:KC * 256].rearrange("p (k m) -> p k m", m=256)
    c.MKT = sb(nc, es, "MKT", [128, 4, 256], BF16)
    c.MVx = sb(nc, es, "MVx", [128, 2, 4, 129], BF16)
    c.stp = Pool(nc, es, "stp", 2, [128, 512], F32, psum=True)
    c.obank = [es.enter_context(nc.psum_tensor(_uname(f"ob{i}"), [128, 512], F32)) for i in range(4)]
    c.psm = Pool(nc, es, "psm", 1, [128, 512], F32, psum=True)
    c.pst = Pool(nc, es, "pst", 1, [128, 8, 128], BF16, psum=True)
    c.Ep = Pool(nc, es, "Ep", 3, [128, 512], BF16)
    c.f1 = Pool(nc, es, "f1", 2, [128, 512], F32)
    c.b1 = Pool(nc, es, "b1", 2, [128, 512], BF16)
    c.sm = Pool(nc, es, "sm", 8, [128, 4], F32)
    c.zp = Pool(nc, es, "zp", 3, [128, 512], BF16)

    def zt_for(j, col0, n):
        t, k = c.zp.next()
        P.dma(t[:, 0:n], c.tmz[j * 128:(j + 1) * 128, col0:col0 + n], w=[k])
        return t, k
    c.zt_for = zt_for
    c.xp = Pool(nc, es, "xp", 2, [128, 512], F32)
    c.ident = c.cmat_t[:, 0, :]
    c.ones128 = c.cmat_t[:, 1, :]
    c.ones64 = c.cmat_t[:, 2, :]
    c.rot128 = c.cmat_t[:, 3, :]

    P.dma(c.cmat_t[:], c.cmat, w=['cmat'])
    P.dma(c.dm_t[:], c.dm, w=['dm'])
    P.dma(c.memg_t[:], c.memg, w=['memg'])
    P.memset('dve', c.eps_t[:], EPS, w=['eps'])
    return c


def fm_norm(P, c, pa, pak, ncols, gain_ap, gain_key, out_ap, out_keys, ones=None):
    ones = c.ones128 if ones is None else ones
    sq, sqk = c.b1.next()
    P.act(sq[:, 0:ncols], pa[:, 0:ncols], AF.Square, r=[pak], w=[sqk])
    pb, pbk = c.psm.next()
    P.mm(pb[:, 0:ncols], ones, sq[:, 0:ncols], r=[sqk, 'cmat'], w=[pbk])
    sd, sdk = c.f1.next()
    P.act(sd[:, 0:ncols], pb[:, 0:ncols], AF.Sqrt, r=[pbk, 'eps'], w=[sdk], bias=c.eps_t[:])
    P.op('dve', lambda e: e.reciprocal(sd[:, 0:ncols], sd[:, 0:ncols]), r=[sdk], w=[sdk])
    P.stt('dve', out_ap, pa[:, 0:ncols], gain_ap, sd[:, 0:ncols], ALU.mult, ALU.mult,
          r=[pak, sdk, gain_key], w=out_keys)


def load_cast_w(P, c, dst, dst_key, w_dram_view, n):
    for hf in range(2):
        ws, wsk = c.wst.next()
        P.dma(ws[:, :, 0:n], w_dram_view[:, hf * 8:(hf + 1) * 8, :], w=[wsk])
        for q in range(8):
            kc = hf * 8 + q
            eng = ('dve', 'pool')[kc % 2]
            P.cp(eng, dst[:, kc, 0:n], ws[:, q, 0:n], r=[wsk], w=[dst_key])


def b_memory_kv(P, c):
    for mt in range(2):
        wst0, xk = c.wst.next()
        xt = wst0[:].rearrange("p a b -> p (a b)")[:, 0:D]
        P.dma(xt[:], c.mem[mt * 128:(mt + 1) * 128, :], w=[xk])
        xb, xbk = c.y_all[:, 0, :], ('y', 0)
        st, sk = c.sm.next()
        P.act(xb[:], xt[:], AF.Square, r=[xk], w=[xbk, sk], accum_out=st[:, 0:1])
        P.act(st[:, 1:2], st[:, 0:1], AF.Sqrt, r=[sk, 'eps'], w=[sk], scale=1.0 / D, bias=c.eps_t[:])
        P.op('dve', lambda e, a=st: e.reciprocal(a[:, 1:2], a[:, 1:2]), r=[sk], w=[sk])
        P.ts('dve', xb[:], xt[:], st[:, 1:2], None, ALU.mult, r=[xk, sk], w=[xbk])
        for half in range(2):
            pt, ptk = c.pst.next()
            for q in range(8):
                kc = half * 8 + q
                P.tr(pt[:, q, :], xb[:, kc * 128:(kc + 1) * 128], c.ident, r=[xbk, 'cmat'], w=[ptk])
            for q in range(8):
                kc = half * 8 + q
                P.ts('dve', c.memT[:, kc, mt * 128:(mt + 1) * 128], pt[:, q, :], c.memg_t[:, kc:kc + 1], None,
                     ALU.mult, r=[ptk, 'memg'], w=['bufV'])
    wv = c.w_kv.rearrange("(kc p) c -> p kc c", p=128)
    wk = c.bufK[:, 0:KC * 512].rearrange("p (k c) -> p k c", c=512)
    load_cast_w(P, c, wk, 'bufK', wv[:, :, 0:512], 512)
    for h in range(4):
        pa, pak = c.stp.next()
        for kc in range(KC):
            P.mm(pa[:, 0:256], wk[:, kc, h * 128:(h + 1) * 128], c.memT[:, kc, :], start=(kc == 0), stop=(kc == KC - 1),
                 r=['bufK', 'bufV'], w=[pak])
        fm_norm(P, c, pa, pak, 256, c.memg_t[:, KC:KC + 1], 'memg', c.MKT[:, h, :], ['MKT'])
    load_cast_w(P, c, wk, 'bufK', wv[:, :, 512:1024], 512)
    P.memset('pool', c.MVx[:, :, :, 128:129], 1.0, w=['MVx'])
    for mt in range(2):
        pa, pak = c.stp.next()
        for kc in range(KC):
            P.mm(pa[:], c.memT[:, kc, mt * 128:(mt + 1) * 128], wk[:, kc, :], start=(kc == 0), stop=(kc == KC - 1),
                 r=['bufK', 'bufV'], w=[pak])
        P.act(c.MVx[:, mt, :, 0:128], pa[:].rearrange("p (h d) -> p h d", d=128), AF.Copy, r=[pak], w=['MVx'])


def attend(P, c, steps, scale):
    prev = None

    def do_pv(st_):
        E, Ek = st_['E']
        for (ec0, oap, okey, rhs, rkeys, first, last) in st_['pv']:
            P.mm(oap, E[:, ec0:ec0 + 128], rhs, start=first, stop=last, r=[Ek] + rkeys, w=[okey])

    for st_ in steps:
        ps, psk = c.stp.next()
        c0, n = st_['c0'], st_['n']
        nm = len(st_['masks'])
        P.mm(ps[:, c0:c0 + n], st_['k'], st_['q'], start=True, stop=(nm == 0), r=st_['kkeys'] + st_['qkeys'], w=[psk])
        for mi, (mc0, mn, ml, mr, mk) in enumerate(st_['masks']):
            P.mm(ps[:, mc0:mc0 + mn], ml, mr, start=False, stop=(mi == nm - 1), r=mk, w=[psk])
        E, Ek = c.Ep.next()
        P.act(E[:, c0:c0 + n], ps[:, c0:c0 + n], AF.Exp, r=[psk], w=[Ek], scale=scale)
        st_['E'] = (E, Ek)
        if prev is not None:
            do_pv(prev)
        prev = st_
    if prev is not None:
        do_pv(prev)


def b_mem_attention(P, c, mq_slot0, ycol0, zcol0):
    NO, NC = c.NO, c.NC
    P.dma(c.qT[:], c.fmq[:, mq_slot0:mq_slot0 + 4, :], w=['qT'])
    scale = 128 ** -0.5
    for h in range(4):
        for ch in range(NC):
            steps = []
            for mt in range(2):
                pv = [(i * 128, c.obank[i][:, 0:129], ('ob', i), c.MVx[:, mt, h, :], ['MVx'], mt == 0, mt == 1)
                      for i in range(4)]
                steps.append(dict(k=c.MKT[:, h, mt * 128:(mt + 1) * 128], kkeys=['MKT'],
                                  q=c.qT[:, h, ch * 512:(ch + 1) * 512], qkeys=['qT'], c0=0, n=512, masks=[], pv=pv))
            attend(P, c, steps, scale)
            for i in range(4):
                j = ch * 4 + i
                zt, ztk = c.zt_for(j, zcol0 + h * 128, 128)
                s_, sk = c.sm.next()
                P.ts('dve', s_[:, 0:1], c.obank[i][:, 128:129], 1e-30, None, ALU.max, r=[('ob', i)], w=[sk])
                P.op('dve', lambda e, a=s_: e.reciprocal(a[:, 0:1], a[:, 0:1]), r=[sk], w=[sk])
                P.stt('dve', c.y_all[:, j, ycol0 + h * 128:ycol0 + (h + 1) * 128], c.obank[i][:, 0:128], s_[:, 0:1],
                      zt[:, 0:128], ALU.mult, ALU.mult,
                      r=[('ob', i), sk, ztk], w=[('y', j)])


def b_out_proj(P, c):
    NO = c.NO
    for j in range(NO):
        for half in range(2):
            pt, ptk = c.pst.next()
            for q in range(8):
                kc = half * 8 + q
                P.tr(pt[:, q, :], c.y_all[:, j, kc * 128:(kc + 1) * 128], c.ident, r=[('y', j), 'cmat'], w=[ptk])
            P.cp('dve', c.y_all[:, j, half * 1024:(half + 1) * 1024], pt[:].rearrange("p a b -> p (a b)"),
                 r=[ptk], w=[('y', j)])
    wv = c.w_out.rearrange("(kc p) c -> p kc c", p=128)
    wbs = [(c.bufK[:, 0:KC * 512].rearrange("p (k c) -> p k c", c=512), 'bufK'),
           (c.bufV[:, 0:KC * 512].rearrange("p (k c) -> p k c", c=512), 'bufV')]
    for cg in range(4):
        wb, wbk = wbs[cg % 2]
        load_cast_w(P, c, wb, wbk, wv[:, :, cg * 512:(cg + 1) * 512], 512)
        for j in range(NO):
            pa, pak = c.stp.next()
            for kc in range(KC):
                P.mm(pa[:], c.y_all[:, j, kc * 128:(kc + 1) * 128], wb[:, kc, :], start=(kc == 0), stop=(kc == KC - 1),
                     r=[('y', j), wbk], w=[pak])
            xt, xk = c.xp.next()
            P.dma(xt[:], c.x[j * 128:(j + 1) * 128, cg * 512:(cg + 1) * 512], w=[xk])
            P.tt('dve', xt[:], xt[:], pa[:], ALU.add, r=[xk, pak], w=[xk])
            P.dma(c.xo[j * 128:(j + 1) * 128, cg * 512:(cg + 1) * 512], xt[:], r=[xk], w=[('xo', j, cg)])


def build_B_even(S, fx=None):
    NT, NO, NOW, NC, n_cmp, NCT, n_sel = geom(S)
    NCP = NCT * 128
    nc = bass.Bass("TRN2", target_bir_lowering=False) if fx is None else fx.nc
    with ExitStack() as es:
        P = Prog(nc) if fx is None else fx.P
        c = b_common(nc, es, P, S, 0, fx)
        if fx is None:
            kg = c.ext("kg", [128, 12, S], BF16)
            vg = c.ext("vg", [8, 128, NT * 129], BF16)
            ag = c.ext("ag", [NOW, 24], F32)
        else:
            kg = vg = None
            ag = c.tmf
        cw1 = c.ext("cw1", [2, 4096, 256], F32, True)
        cw2 = c.ext("cw2", [128, 2, 2, 128], F32, True)
        peT = c.ext("peT", [128, 2, 32], BF16, True)
        ctab = c.ext("ctab", [128, 2, NCP], BF16)
        ovm = c.ext("ovm", [128, NCT, 128], BF16)
        expand = c.ext("expand", [128, 32, 128], BF16)
        wm = c.ext("wm", [128, 8, 128], BF16)
        cm = c.ext("cm", [NO, 128, NCT, 128], BF16)
        visfrc = c.ext("visfrc", [NO, 128, 2, 128], F32)
        gv = c.ext("gv", [128, 388], F32, True)

        cw2_f = sb(nc, es, "cw2_f", [128, 2, 2, 128], F32)
        cw2_t = sb(nc, es, "cw2_t", [128, 2, 2, 128], BF16)
        peT_t = sb(nc, es, "peT_t", [128, 2, 32], BF16)
        ctab_t = sb(nc, es, "ctab_t", [128, 2, NCP], BF16)
        ovm_t = sb(nc, es, "ovm_t", [128, NCT, 128], BF16)
        expand_t = sb(nc, es, "expand_t", [128, 32, 128], BF16)
        wm_t = sb(nc, es, "wm_t", [128, 8, 128], BF16)
        gv_t = sb(nc, es, "gv_t", [128, 388], F32)
        KCc = sb(nc, es, "KCc", [128, 2, NCP], BF16)
        VCx = sb(nc, es, "VCx", [128, 2, NCT, 257], BF16)
        hid = sb(nc, es, "hid", [128, 2, NCP], BF16)
        cbias = sb(nc, es, "cbias", [128, 2], F32)
        lam = sb(nc, es, "lam", [128, 4], F32)
        cmp_ = Pool(nc, es, "cmp", 2, [128, NCT, 128], BF16)
        vfp = Pool(nc, es, "vfp", 2, [128, 2, 128], F32)
        agp = Pool(nc, es, "agp", 2, [128, 24], F32)
        acc = Pool(nc, es, "acc", 1, [128, 512], F32)
        imp = Pool(nc, es, "imp", 2, [128, 128], F32)
        imp2 = Pool(nc, es, "imp2", 2, [128, 128], F32)
        m8 = Pool(nc, es, "m8", 2, [128, 16], F32)
        seln = Pool(nc, es, "seln", 2, [128, 128], BF16)
        selT = Pool(nc, es, "selT", 2, [128, 128], BF16)
        kwp = Pool(nc, es, "kwp", 1, [128, 8, 128], BF16)
        vwp = Pool(nc, es, "vwp", 1, [128, 8, 129], BF16)
        dtmp = Pool(nc, es, "dtmp", 2, [128, 4, 128], F32)

        for (t, src, k) in ((cw2_f, cw2, 'cw2f'), (peT_t, peT, 'peT'), (ctab_t, ctab, 'ctab'), (ovm_t, ovm, 'ovm'),
                            (expand_t, expand, 'expand'), (wm_t, wm, 'wm'), (gv_t, gv, 'gv')):
            P.dma(t[:], src, w=[k])
        P.cp('dve', cw2_t[:], cw2_f[:], r=['cw2f'], w=['cw2'])
        lv = gv_t[:, 130:130 + 256]
        j1, j1k = c.f1.next()
        P.tt('dve', j1[:, 0:64], lv[:, 0:64], lv[:, 64:128], ALU.mult, r=['gv'], w=[j1k])
        P.tt('dve', j1[:, 64:128], lv[:, 128:192], lv[:, 192:256], ALU.mult, r=['gv'], w=[j1k])
        P.op('dve', lambda e: e.reduce_sum(lam[:, 0:1], j1[:, 0:64], AX.X), r=[j1k], w=['lam'])
        P.op('dve', lambda e: e.reduce_sum(lam[:, 1:2], j1[:, 64:128], AX.X), r=[j1k], w=['lam'])
        P.act(lam[:, 0:2], lam[:, 0:2], AF.Exp, r=['lam'], w=['lam'])
        P.tt('dve', lam[:, 2:3], lam[:, 0:1], lam[:, 1:2], ALU.subtract, r=['lam'], w=['lam'])
        P.tt('dve', lam[:, 3:4], lam[:, 2:3], gv_t[:, 386:387], ALU.add, r=['lam', 'gv'], w=['lam'])
        P.ts('dve', lam[:, 3:4], lam[:, 3:4], -1.0, None, ALU.mult, r=['lam'], w=['lam'])

        b_memory_kv(P, c)

        P.memset('pool', hid[:], 0.0, w=['hid'])
        P.memset('pool', KCc[:], 0.0, w=['KCc'])
        P.memset('pool', VCx[:], 0.0, w=['VCx'])
        P.memset('pool', VCx[:, :, :, 256:257], 1.0, w=['VCx'])
        for g in range(2):
            for ct in range(NCT):
                P.cp('pool', VCx[:, g, ct, 128:256], ovm_t[:, ct, :], r=['ovm'], w=['VCx'])
        w1v = c.bufV[:, 0:32 * 256].rearrange("p (l h) -> p l h", h=256)
        for kv in range(2):
            wsrc = cw1[kv].rearrange("(l p) h -> p l h", p=128)
            for q4 in range(4):
                ws, wsk = c.wst.next()
                wsv = ws[:].rearrange("p a b -> p (a b)")[:, 0:8 * 256].rearrange("p (l h) -> p l h", h=256)
                P.dma(wsv, wsrc[:, q4 * 8:(q4 + 1) * 8, :], w=[wsk])
                P.cp('dve', w1v[:, q4 * 8:q4 * 8 + 4, :], wsv[:, 0:4, :], r=[wsk], w=['bufV'])
                P.cp('pool', w1v[:, q4 * 8 + 4:q4 * 8 + 8, :], wsv[:, 4:8, :], r=[wsk], w=['bufV'])
            for half in range(2):
                pb, pbk = c.psm.next()
                for l in range(32):
                    P.mm(pb[:, 0:1], w1v[:, l, half * 128:(half + 1) * 128], peT_t[:, kv, l:l + 1],
                         start=(l == 0), stop=(l == 31), r=['bufV', 'peT'], w=[pbk])
                P.cp('dve', cbias[:, half:half + 1], pb[:, 0:1], r=[pbk], w=['cbias'])
            for g in range(2):
                c.load_k(2 * kv + g, kg)
                for half in range(2):
                    pa, pak = c.stp.next()
                    for l in range(32):
                        P.mm(pa[:, 0:n_cmp], w1v[:, l, half * 128:(half + 1) * 128],
                             c.bufK[:, l:l + 16 * (n_cmp - 1) + 1:16], start=(l == 0), stop=(l == 31),
                             r=['bufV', 'bufK'], w=[pak])
                    P.act(hid[:, half, 0:n_cmp], pa[:, 0:n_cmp], AF.Silu, r=[pak, 'cbias'], w=['hid'],
                          bias=cbias[:, half:half + 1])
                if kv == 0:
                    pa, pak = c.stp.next()
                    for half in range(2):
                        P.mm(pa[:, 0:NCP], cw2_t[:, 0, half, :], hid[:, half, :], start=(half == 0), stop=(half == 1),
                             r=['cw2', 'hid'], w=[pak])
                    q_, qk = c.b1.next()
                    fm_norm(P, c, pa, pak, NCP, gv_t[:, 0:1], 'gv', q_[:, 0:NCP], [qk])
                    pc, pck = c.psm.next()
                    P.mm(pc[:, 0:NCP], c.rot128, q_[:, 0:NCP], r=[qk, 'cmat'], w=[pck])
                    t1, t1k = c.f1.next()
                    P.tt('pool', t1[:, 0:NCP], q_[:, 0:NCP], ctab_t[:, 0, :], ALU.mult, r=[qk, 'ctab'], w=[t1k])
                    t2, t2k = c.f1.next()
                    P.tt('dve', t2[:, 0:NCP], pc[:, 0:NCP], ctab_t[:, 1, :], ALU.mult, r=[pck, 'ctab'], w=[t2k])
                    P.tt('pool', KCc[:, g, :], t1[:, 0:NCP], t2[:, 0:NCP], ALU.add, r=[t1k, t2k], w=['KCc'])
                else:
                    for ct in range(NCT):
                        pa, pak = c.stp.next()
                        for half in range(2):
                            P.mm(pa[:, 0:128], hid[:, half, ct * 128:(ct + 1) * 128], cw2_t[:, 1, half, :],
                                 start=(half == 0), stop=(half == 1), r=['cw2', 'hid'], w=[pak])
                        P.act(VCx[:, g, ct, 0:128], pa[:, 0:128], AF.Copy, r=[pak], w=['VCx'])

        sc = 128 ** -0.5
        ob = c.obank
        for g in range(2):
            P.dma(c.qT[:], c.fmq[:, g * 4:(g + 1) * 4, :], w=['qT'])
            c.load_k(4 + g, kg)
            c.load_v(g, vg)
            vS = c.bufV[:, 0:NT * 129].rearrange("p (t d) -> p t d", d=129)
            for j in range(NO):
                qj = c.qT[:, :, j * 128:(j + 1) * 128]
                cmj, cmk = cmp_.next()
                P.dma(cmj[:], cm[j], w=[cmk])
                vf, vfk = vfp.next()
                P.dma(vf[:], visfrc[j], w=[vfk])
                agt, agk = agp.next()
                P.dma(agt[:], ag[j * 128:(j + 1) * 128, :], w=[agk])
                a_, ak = acc.next()
                steps = []
                for ct in range(NCT):
                    masks = [(h * 128, 128, c.ident, cmj[:, ct, :], ['cmat', cmk]) for h in range(4)]
                    pv = [(h * 128, ob[h][:, 0:257], ('ob', h), VCx[:, g, ct, :], ['VCx'], ct == 0, ct == NCT - 1)
                          for h in range(4)]
                    steps.append(dict(k=KCc[:, g, ct * 128:(ct + 1) * 128], kkeys=['KCc'], q=qj, qkeys=['qT'],
                                      c0=0, n=512, masks=masks, pv=pv))
                attend(P, c, steps, sc)
                im, imk = imp.next()
                for h in range(4):
                    s_, sk = c.sm.next()
                    P.ts('dve', s_[:, 0:1], ob[h][:, 256:257], 1e-30, None, ALU.max, r=[('ob', h)], w=[sk])
                    P.op('dve', lambda e, a=s_: e.reciprocal(a[:, 0:1], a[:, 0:1]), r=[sk], w=[sk])
                    P.tt('dve', s_[:, 1:2], s_[:, 0:1], agt[:, (g * 4 + h) * 3:(g * 4 + h) * 3 + 1], ALU.mult,
                         r=[sk, agk], w=[sk])
                    P.ts('dve', a_[:, h * 128:(h + 1) * 128], ob[h][:, 0:128], s_[:, 1:2], None, ALU.mult,
                         r=[('ob', h), sk], w=[ak])
                    if h == 0:
                        P.ts('dve', im[:], ob[h][:, 128:256], s_[:, 0:1], None, ALU.mult, r=[('ob', h), sk], w=[imk])
                    else:
                        P.stt('dve', im[:], ob[h][:, 128:256], s_[:, 0:1], im[:], ALU.mult, ALU.add,
                              r=[('ob', h), sk, imk], w=[imk])
                P.tt('dve', im[:], im[:], vf[:, 0, :], ALU.mult, r=[imk, vfk], w=[imk])
                P.tt('dve', im[:], im[:], vf[:, 1, :], ALU.add, r=[imk, vfk], w=[imk])
                mx, mxk = m8.next()
                P.op('dve', lambda e, a=mx, b=im: e.max(a[:, 0:8], b[:]), r=[imk], w=[mxk])
                i2, i2k = imp2.next()
                P.op('dve', lambda e, a=mx, b=im, o=i2: e.match_replace(o[:], a[:, 0:8], b[:], -3.0e38),
                     r=[imk, mxk], w=[i2k])
                P.op('dve', lambda e, a=mx, b=i2: e.max(a[:, 8:16], b[:]), r=[i2k], w=[mxk])
                sn, snk = seln.next()
                P.ts('dve', sn[:], im[:], mx[:, 15:16], NEG, ALU.is_lt, ALU.mult, r=[imk, mxk], w=[snk])
                pt, ptk = c.pst.next()
                P.tr(pt[:, 0, :], sn[:], c.ident, r=[snk, 'cmat'], w=[ptk])
                sT, sTk = selT.next()
                P.cp('dve', sT[:], pt[:, 0, :], r=[ptk], w=[sTk])
                u0 = 4 if j == 0 else 0
                kw, kwk = kwp.next()
                vw, vwk = vwp.next()
                t_lo = 4 * j - 4 + u0
                if fx is None:
                    P.dma(kw[:, u0:8, :], kg[:, 6 + g, t_lo * 128:(4 * j + 4) * 128].rearrange("p (u s) -> p u s", s=128), w=[kwk])
                    P.dma(vw[:, u0:8, :], vg[2 + g][:, t_lo * 129:(4 * j + 4) * 129].rearrange("p (u s) -> p u s", s=129), w=[vwk])
                else:
                    if j > 0:
                        P.dma(kw[:, 0:4, :], c.kall_fm_v[:, 6 + g, j - 1], w=[kwk])
                        P.dma(vw[:, 0:4, :], c.kall_v_v[:, 2 + g, j - 1], w=[vwk])
                    P.dma(kw[:, 4:8, :], c.kall_fm_v[:, 6 + g, j], w=[kwk])
                    P.dma(vw[:, 4:8, :], c.kall_v_v[:, 2 + g, j], w=[vwk])
                steps = []
                for u in range(u0, 8):
                    masks = [(h * 128, 128, c.ident, wm_t[:, u, :], ['cmat', 'wm']) for h in range(4)]
                    pv = [(h * 128, ob[h][:, 0:129], ('ob', h), vw[:, u, :], [vwk], u == u0, u == 7) for h in range(4)]
                    steps.append(dict(k=kw[:, u, :], kkeys=[kwk], q=qj, qkeys=['qT'], c0=0, n=512, masks=masks, pv=pv))
                attend(P, c, steps, sc)
                for h in range(4):
                    s_, sk = c.sm.next()
                    P.ts('dve', s_[:, 0:1], ob[h][:, 128:129], 1e-30, None, ALU.max, r=[('ob', h)], w=[sk])
                    P.op('dve', lambda e, a=s_: e.reciprocal(a[:, 0:1], a[:, 0:1]), r=[sk], w=[sk])
                    P.tt('dve', s_[:, 1:2], s_[:, 0:1], agt[:, (g * 4 + h) * 3 + 2:(g * 4 + h) * 3 + 3], ALU.mult,
                         r=[sk, agk], w=[sk])
                    P.stt('dve', a_[:, h * 128:(h + 1) * 128], ob[h][:, 0:128], s_[:, 1:2], a_[:, h * 128:(h + 1) * 128],
                          ALU.mult, ALU.add, r=[('ob', h), sk, ak], w=[ak])
                steps = []
                nk = 4 * j + 4
                for kt in range(nk):
                    hrows = slice((kt // 32) * 64, (kt // 32) * 64 + 64)
                    masks = [(h * 128, 128, expand_t[hrows, kt % 32, :], sT[hrows, :], ['expand', sTk]) for h in range(4)]
                    if kt >= 4 * j:
                        masks += [(h * 128, 128, c.ident, c.dm_t[:, kt - 4 * j, :], ['cmat', 'dm']) for h in range(4)]
                    pv = [(h * 128, ob[h][:, 0:129], ('ob', h), vS[:, kt, :], ['bufV'], kt == 0, kt == nk - 1)
                          for h in range(4)]
                    steps.append(dict(k=c.bufK[:, kt * 128:(kt + 1) * 128], kkeys=['bufK'], q=qj, qkeys=['qT'],
                                      c0=0, n=512, masks=masks, pv=pv))
                attend(P, c, steps, sc)
                zt, ztk = c.zt_for(j, g * 512, 512)
                for h in range(4):
                    s_, sk = c.sm.next()
                    P.ts('dve', s_[:, 0:1], ob[h][:, 128:129], 1e-30, None, ALU.max, r=[('ob', h)], w=[sk])
                    P.op('dve', lambda e, a=s_: e.reciprocal(a[:, 0:1], a[:, 0:1]), r=[sk], w=[sk])
                    P.tt('dve', s_[:, 1:2], s_[:, 0:1], agt[:, (g * 4 + h) * 3 + 1:(g * 4 + h) * 3 + 2], ALU.mult,
                         r=[sk, agk], w=[sk])
                    P.stt('dve', a_[:, h * 128:(h + 1) * 128], ob[h][:, 0:128], s_[:, 1:2], a_[:, h * 128:(h + 1) * 128],
                          ALU.mult, ALU.add, r=[('ob', h), sk, ak], w=[ak])
                P.tt('dve', c.y_all[:, j, g * 512:(g + 1) * 512], a_[:], zt[:], ALU.mult, r=[ak, ztk], w=[('y', j)])

        sc64 = 64 ** -0.5
        for h in range(4):
            P.dma(c.qT[:, 0, :], c.fmq[:, 8 + h, :], w=['qT'])
            c.load_k(8 + h, kg)
            c.load_v(4 + h, vg)
            vS = c.bufV[:, 0:NT * 129].rearrange("p (t d) -> p t d", d=129)
            for ch in range(NC):
                j0 = 4 * ch
                dts = []
                for m in range(2):
                    rows = slice(m * 64, (m + 1) * 64)
                    steps = []
                    nk = 4 * j0 + 16
                    for kt in range(nk):
                        a = 0 if kt < 4 * j0 else (kt - 4 * j0) // 4
                        masks = []
                        if kt >= 4 * j0:
                            u = kt - 4 * (j0 + a)
                            masks = [(a * 128, 128, c.ident, c.dm_t[:, u, :], ['cmat', 'dm'])]
                        pv = []
                        for i in range(a, 4):
                            last_kt = 4 * (j0 + i) + 3
                            pv.append((i * 128, ob[i][:, 0:129], ('ob', i), vS[:, kt, :], ['bufV'], kt == 0, kt == last_kt))
                        steps.append(dict(k=c.bufK[rows, kt * 128:(kt + 1) * 128], kkeys=['bufK'],
                                          q=c.qT[rows, 0, ch * 512 + a * 128:(ch + 1) * 512], qkeys=['qT'],
                                          c0=a * 128, n=512 - a * 128, masks=masks, pv=pv))
                    attend(P, c, steps, sc64)
                    d_, dk = dtmp.next()
                    dts.append((d_, dk))
                    for i in range(4):
                        s_, sk = c.sm.next()
                        P.ts('dve', s_[:, 0:1], ob[i][:, 128:129], 1e-30, None, ALU.max, r=[('ob', i)], w=[sk])
                        P.op('dve', lambda e, a=s_: e.reciprocal(a[:, 0:1], a[:, 0:1]), r=[sk], w=[sk])
                        if m == 1:
                            P.tt('dve', s_[:, 0:1], s_[:, 0:1], lam[:, 3:4], ALU.mult, r=[sk, 'lam'], w=[sk])
                        P.ts('dve', d_[:, i, :], ob[i][:, 0:128], s_[:, 0:1], None, ALU.mult, r=[('ob', i), sk], w=[dk])
                (d0, d0k), (d1, d1k) = dts
                P.tt('pool', d0[:], d0[:], d1[:], ALU.add, r=[d0k, d1k], w=[d0k])
                for i in range(4):
                    j = j0 + i
                    s_, sk = c.sm.next()
                    jb, jbk = c.b1.next()
                    P.act(jb[:, 0:128], d0[:, i, :], AF.Square, r=[d0k], w=[jbk, sk], accum_out=s_[:, 0:1])
                    P.act(s_[:, 1:2], s_[:, 0:1], AF.Sqrt, r=[sk, 'eps'], w=[sk], scale=1.0 / 128, bias=c.eps_t[:])
                    P.op('dve', lambda e, a=s_: e.reciprocal(a[:, 1:2], a[:, 1:2]), r=[sk], w=[sk])
                    zt, ztk = c.zt_for(j, 1024 + h * 128, 128)
                    t1, t1k = c.f1.next()
                    P.stt('dve', t1[:, 0:128], d0[:, i, :], s_[:, 1:2], gv_t[:, 2:130], ALU.mult, ALU.mult,
                          r=[d0k, sk, 'gv'], w=[t1k])
                    P.stt('dve', c.y_all[:, j, 1024 + h * 128:1024 + (h + 1) * 128], t1[:, 0:128], gv_t[:, 387:388],
                          zt[:, 0:128], ALU.mult, ALU.mult, r=[t1k, ztk, 'gv'], w=[('y', j)])

        b_mem_attention(P, c, 12, 1536, 1536)
        b_out_proj(P, c)
        if fx is None:
            P.emit(es)
    return nc


def mask_consts(S, r):
    NT, NO, NOW, NC, n_cmp, NCT, n_sel = geom(S)
    s = np.arange(128)[:, None]
    t = np.arange(128)[None, :]
    dm = np.zeros((128, 4, 128), np.float32)
    for u in range(4):
        if u == r:
            dm[:, u, :] = np.where(s <= t, 0.0, NEG)
        elif u > r:
            dm[:, u, :] = NEG
    wm = np.zeros((128, 8, 128), np.float32)
    for u in range(8):
        delta = (r + 4 - u) * 128 + t - s
        wm[:, u, :] = np.where((delta >= 0) & (delta < 512), 0.0, NEG)
    cm = np.zeros((NO, 128, NCT, 128), np.float32)
    vf = np.zeros((NO, 128, 2, 128), np.float32)
    jj = np.arange(128)[None, :]
    for j in range(NO):
        tg = (4 * j + r) * 128 + np.arange(128)
        for ct in range(NCT):
            cidx = ct * 128 + np.arange(128)
            vis = (cidx[:, None] < n_cmp) & ((16 * cidx[:, None] + 31) <= tg[None, :])
            cm[j, :, ct, :] = np.where(vis, 0.0, NEG)
        cur = (tg // 64)[:, None]
        visible = (jj <= cur) & (jj < n_sel)
        forced = (jj == 0) | (jj >= cur - 1)
        vf[j, :, 0, :] = np.where(visible & ~forced, 1.0, 0.0)
        vf[j, :, 1, :] = np.where(~visible, -1e30, np.where(forced, 1000.0 + jj, 0.0))
    return dm.astype(NPBF), wm.astype(NPBF), cm.astype(NPBF), vf


def even_consts(S):
    NT, NO, NOW, NC, n_cmp, NCT, n_sel = geom(S)
    NCP = NCT * 128
    cpos = 16 * np.arange(NCP) + 31
    cc, cs = rope_tables_fm(cpos, 128)
    ctab = np.stack([cc, cs], axis=1)
    cmp_start = np.arange(n_cmp) * 16
    sel_start = np.arange(n_sel) * 64
    ov = np.clip(np.minimum(cmp_start[:, None] + 32, sel_start[None, :] + 64)
                 - np.maximum(cmp_start[:, None], sel_start[None, :]), 0, None) / 32.0
    ovp = np.zeros((NCP, 128), np.float32)
    ovp[:n_cmp, :n_sel] = ov
    ovm = ovp.reshape(NCT, 128, 128).transpose(1, 0, 2).copy().astype(NPBF)
    expand = np.zeros((128, 32, 128), np.float32)
    for k in range(32):
        for s in range(128):
            expand[2 * k + s // 64, k, s] = 1.0
            expand[64 + 2 * k + s // 64, k, s] = 1.0
    return ctab, ovm, expand.astype(NPBF)


def fm_gather(fmos, slots, S):
    NT = S // 128
    NO = NT // 4
    out = np.zeros((128, len(slots), NT, 128), dtype=fmos[0].dtype)
    for r in range(4):
        v = fmos[r][:, slots, :].reshape(128, len(slots), NO, 128)
        out[:, :, r::4, :] = v
    return out.reshape(128, len(slots), S)


def tm_gather_vext(tmos, col0, S):
    NT = S // 128
    NO = NT // 4
    out = np.ones((128, NT, 129), dtype=tmos[0].dtype)
    for r in range(4):
        v = tmos[r][:, col0:col0 + 128].reshape(NO, 128, 128)
        out[:, r::4, 0:128] = v.transpose(1, 0, 2)
    return out.reshape(128, NT * 129)


def memg_host(mem_norm_gain, mem_k_gain):
    m = np.zeros((128, KC + 1), np.float32)
    m[:, :KC] = mem_norm_gain.reshape(KC, 128).T
    m[:, KC] = mem_k_gain
    return m


def a_inputs(parity, S, x_own, w_in, norm_gain, gain_cols, r):
    m = dict(x=np.ascontiguousarray(x_own), w_in=w_in, ng=np.ascontiguousarray(norm_gain.reshape(KC, 128).T),
             gains=np.ascontiguousarray(np.stack(gain_cols, axis=1).astype(np.float32)),
             tabs=tabs_host(S, r), cmat=cmat_host())
    if parity == 1:
        sel = np.zeros((16, 8, 128), np.float32)
        for c in range(8):
            for mm_ in range(128):
                sel[2 * c + mm_ // 64, c, mm_] = 1
        m['sel'] = sel.astype(NPBF)
    return m


def b_even_inputs(S, r, x_own, fmo_b, tmo_b, tmf_own, w_out, mem_b, mem_norm_gain, w_kv, mem_qk_gain,
                  nsa_qk_gain, cmp_pos, cmp_w1, cmp_w2, subln, lam_vecs, shared, lambda_init):
    fmo = fmo_b[r]
    tmo = tmo_b[r]
    dm, wm, cm, vf = mask_consts(S, r)
    gv = np.zeros((128, 388), np.float32)
    gv[:, 0] = nsa_qk_gain[1]
    gv[:, 2:130] = subln[None, :]
    gv[:, 130:386] = lam_vecs.reshape(1, 256)
    gv[:, 386] = lambda_init
    gv[:, 387] = 1.0 - lambda_init
    m = dict(x=np.ascontiguousarray(x_own), w_out=w_out, mem=mem_b, w_kv=w_kv,
             memg=memg_host(mem_norm_gain, mem_qk_gain[1]), cmat=cmat_host(), dm=dm,
             fmq=np.ascontiguousarray(np.concatenate([fmo[:, 0:8], fmo[:, 16:20], fmo[:, 24:28]], axis=1)),
             tmz=np.ascontiguousarray(np.concatenate([tmo[:, 512:1536], tmo[:, 2048:2560], tmo[:, 2560:3072]], axis=1)),
             kg=shared['kg'], vg=shared['vg'], ag=np.ascontiguousarray(tmf_own[:, :24]),
             cw1=cmp_w1, cw2=np.ascontiguousarray(cmp_w2.reshape(2, 2, 128, 128).transpose(2, 0, 1, 3)),
             peT=np.ascontiguousarray(cmp_pos.transpose(2, 0, 1)).astype(NPBF),
             ctab=shared['ctab'], ovm=shared['ovm'], expand=shared['expand'], wm=wm, cm=cm, visfrc=vf, gv=gv)
    return m


def b_even_shared(S, fmo_b, tmo_b):
    ctab, ovm, expand = even_consts(S)
    kg = fm_gather(fmo_b, [8, 9, 10, 11, 12, 13, 14, 15, 20, 21, 22, 23], S)
    vcols = [0, 128, 256, 384, 1536, 1664, 1792, 1920]
    vg = np.stack([tm_gather_vext(tmo_b, c0, S) for c0 in vcols], axis=0)
    return dict(kg=kg, vg=vg, ctab=ctab, ovm=ovm, expand=expand)


def build_B_odd(S, fx=None):
    NT, NO, NOW, NC, n_cmp, NCT, n_sel = geom(S)
    TOPK = min(256, S // 4)
    NIT = 16
    nc = bass.Bass("TRN2", target_bir_lowering=False) if fx is None else fx.nc
    with ExitStack() as es:
        P = Prog(nc) if fx is None else fx.P
        c = b_common(nc, es, P, S, 1, fx)
        if fx is None:
            kg = c.ext("kg", [128, 5, S], BF16)
            vg = c.ext("vg", [4, 128, NT * 129], BF16)
            sg = c.ext("sg", [NOW, 16], F32)
            mscr = nc.dram_tensor("mscr", [NO, 128, S], BF16, kind="Internal").ap()
        else:
            kg = vg = None
            sg = c.tmf
            mscr = fx.internal("mscr", [NO, 128, S], BF16)
        dmt = c.ext("dmt", [128, 512], F32)

        IKT = c.qT[:].rearrange("p a b -> p (a b)")[:, 0:S]
        dmt_t = sb(nc, es, "dmt_t", [128, 512], F32)
        mnp = Pool(nc, es, "mnp", 2, [128, S], BF16)
        iqp = Pool(nc, es, "iqp", 2, [128, 8, 128], BF16)
        sgp = Pool(nc, es, "sgp", 2, [128, 16], F32)
        rp = Pool(nc, es, "rp", 4, [128, 512], BF16)
        dgp = Pool(nc, es, "dgp", 2, [128, 16, 128], BF16)
        thr = Pool(nc, es, "thr", 2, [128, 8], F32)
        scr_bufs = [c.big[:, 0:2 * S].bitcast(F32)]
        scr_guard = [['bufK', 'bufV']]
        if NO >= 2:
            scr_bufs.append(c.y_all[:, 0:NO // 2, :].rearrange("p a b -> p (a b)")[:, 0:2 * S].bitcast(F32))
            scr_guard.append([('y', j_) for j_ in range(NO // 2)])

        P.dma(dmt_t[:], dmt, w=['dmt'])
        b_memory_kv(P, c)
        if fx is None:
            P.dma(IKT, kg[:, 4, :], w=['qT'])
        else:
            dv_ = IKT.rearrange("d (j r p) -> d j r p", r=4, p=128)
            for r_ in range(4):
                P.dma(dv_[:, :, r_, :], c.kall_fm_v[:, 4, :, r_, :], w=['qT'])
        for sb_, sg_ in zip(scr_bufs, scr_guard):
            P.memset('dve', sb_[:, 0:1], 0.0, w=sg_)

        psl = [(c.stp.t[0], ('stp', 0)), (c.stp.t[1], ('stp', 1)), (c.obank[0], ('ob', 0)), (c.obank[1], ('ob', 1))]
        accl = [(c.obank[2], ('ob', 2)), (c.obank[3], ('ob', 3))]
        psi = [0]
        acci = [0]
        for j in range(NO):
            L = (4 * j + 4) * 128
            sbi = j % len(scr_bufs)
            scr_t = scr_bufs[sbi]
            BB = scr_guard[sbi]
            iq, iqk = iqp.next()
            P.dma(iq[:], c.fmq[:, 12:20, j * 128:(j + 1) * 128], w=[iqk])
            sgt, sgk = sgp.next()
            P.dma(sgt[:], sg[j * 128:(j + 1) * 128, :], w=[sgk])
            dg, dgk = dgp.next()
            for hh in range(16):
                P.ts('pool', dg[:, hh, :], c.ident, sgt[:, hh:hh + 1], None, ALU.mult,
                     r=['cmat', sgk], w=[(dgk, hh)])
            for kc5 in range(L // 512):
                cols = slice(kc5 * 512, (kc5 + 1) * 512)
                acc_ps, acck = accl[acci[0] % 2]
                acci[0] += 1
                pend = []

                def do_acc(item):
                    hh_, R__, Rk_ = item
                    P.mm(acc_ps[:], dg[:, hh_, :], R__[:], start=(hh_ == 0), stop=(hh_ == 15),
                         r=[(dgk, hh_), Rk_], w=[acck])
                for hh in range(16):
                    rows = slice((hh % 2) * 64, (hh % 2) * 64 + 64)
                    ps, psk = psl[psi[0] % 4]
                    psi[0] += 1
                    P.mm(ps[:], iq[rows, hh // 2, :], IKT[rows, cols], r=[iqk, 'qT'], w=[psk])
                    R_, Rk = rp.next()
                    P.act(R_[:], ps[:], AF.Relu, r=[psk], w=[Rk])
                    pend.append((hh, R_, Rk))
                    if len(pend) > 2:
                        do_acc(pend.pop(0))
                while pend:
                    do_acc(pend.pop(0))
                P.act(scr_t[:, cols], acc_ps[:], AF.Copy, r=[acck] + BB, w=[('scr', sbi, kc5)])
            last = L // 512 - 1
            P.tt('dve', scr_t[:, L - 512:L], scr_t[:, L - 512:L], dmt_t[:], ALU.add, r=[('scr', sbi, last), 'dmt'] + BB,
                 w=[('scr', sbi, last)])
            skeys = [('scr', sbi, k) for k in range(L // 512)] + BB
            th, thk = thr.next()
            mn, mnk = mnp.next()
            P.op('dve', lambda e, a=th, LL=L, sc_=scr_t: e.reduce_max(a[:, 0:1], sc_[:, 0:LL], AX.X), r=skeys, w=[thk])
            P.ts('dve', th[:, 1:2], th[:, 0:1], -16.0, None, ALU.add, r=[thk], w=[thk])
            for k in range(NIT):
                ck = 8.0 / (2 ** k)
                P.ts('dve', th[:, 2:3], th[:, 1:2], ck, None, ALU.add, r=[thk], w=[thk])
                P.op('dve', lambda e, a=th, m_=mn, LL=L, sc_=scr_t: e.tensor_scalar(m_[:, 0:LL], sc_[:, 0:LL], a[:, 2:3], None,
                                                                      ALU.is_ge, ALU.add, accum_out=a[:, 3:4]),
                     r=skeys + [thk], w=[thk, mnk])
                P.ts('dve', th[:, 4:5], th[:, 3:4], float(TOPK) - 0.5, ck, ALU.is_ge, ALU.mult, r=[thk], w=[thk])
                P.tt('dve', th[:, 1:2], th[:, 1:2], th[:, 4:5], ALU.add, r=[thk], w=[thk])
            P.ts('dve', mn[:, 0:L], scr_t[:, 0:L], th[:, 1:2], NEG, ALU.is_lt, ALU.mult, r=skeys + [thk], w=[mnk])
            P.dma(mscr[j][:, 0:L], mn[:, 0:L], r=[mnk], w=[('mscr', j)])

        sc = 128 ** -0.5
        ob = c.obank
        for g in range(4):
            P.dma(c.qT[:, 0:3, :], c.fmq[:, 3 * g:3 * g + 3, :], w=['qT'])
            c.load_k(g, kg)
            c.load_v(g, vg)
            vS = c.bufV[:, 0:NT * 129].rearrange("p (t d) -> p t d", d=129)
            for j in range(NO):
                L = (4 * j + 4) * 128
                mn, mnk = mnp.next()
                P.dma(mn[:, 0:L], mscr[j][:, 0:L], r=[('mscr', j)], w=[mnk])
                qj = c.qT[:, 0:3, j * 128:(j + 1) * 128]
                steps = []
                nk = 4 * j + 4
                for kt in range(nk):
                    masks = [(h * 128, 128, mn[:, kt * 128:(kt + 1) * 128], c.ident, [mnk, 'cmat']) for h in range(3)]
                    pv = [(h * 128, ob[h][:, 0:129], ('ob', h), vS[:, kt, :], ['bufV'], kt == 0, kt == nk - 1)
                          for h in range(3)]
                    steps.append(dict(k=c.bufK[:, kt * 128:(kt + 1) * 128], kkeys=['bufK'], q=qj, qkeys=['qT'],
                                      c0=0, n=384, masks=masks, pv=pv))
                attend(P, c, steps, sc)
                zt, ztk = c.zt_for(j, g * 384, 384)
                for h in range(3):
                    s_, sk = c.sm.next()
                    P.ts('dve', s_[:, 0:1], ob[h][:, 128:129], 1e-30, None, ALU.max, r=[('ob', h)], w=[sk])
                    P.op('dve', lambda e, a=s_: e.reciprocal(a[:, 0:1], a[:, 0:1]), r=[sk], w=[sk])
                    col = (g * 3 + h) * 128
                    P.stt('dve', c.y_all[:, j, col:col + 128], ob[h][:, 0:128], s_[:, 0:1], zt[:, h * 128:(h + 1) * 128],
                          ALU.mult, ALU.mult, r=[('ob', h), sk, ztk], w=[('y', j)])

        b_mem_attention(P, c, 20, 1536, 1536)
        b_out_proj(P, c)
        if fx is None:
            P.emit(es)
    return nc


def b_odd_shared(S, fmo_b, tmo_b):
    kg = fm_gather(fmo_b, [12, 13, 14, 15, 24], S)
    vg = np.stack([tm_gather_vext(tmo_b, h * 128, S) for h in range(4)], axis=0)
    return dict(kg=kg, vg=vg)


def b_odd_inputs(S, r, x_own, fmo_b, tmo_b, tmf_own, w_out, mem_b, mem_norm_gain, w_kv, mem_qk_gain, shared):
    fmo = fmo_b[r]
    tmo = tmo_b[r]
    dm, wm, cm, vf = mask_consts(S, r)
    t = np.arange(128)[:, None]
    dmt = np.zeros((128, 4, 128), np.float32)
    s = np.arange(128)[None, :]
    for u in range(4):
        if u == r:
            dmt[:, u, :] = np.where(s <= t, 0.0, -1e30)
        elif u > r:
            dmt[:, u, :] = -1e30
    m = dict(x=np.ascontiguousarray(x_own), w_out=w_out, mem=mem_b, w_kv=w_kv,
             memg=memg_host(mem_norm_gain, mem_qk_gain[1]), cmat=cmat_host(), dm=dm,
             fmq=np.ascontiguousarray(np.concatenate([fmo[:, 0:12], fmo[:, 16:24], fmo[:, 25:29]], axis=1)),
             tmz=np.ascontiguousarray(tmo[:, 512:2560]),
             kg=shared['kg'], vg=shared['vg'], sg=np.ascontiguousarray(tmf_own[:, :16]),
             dmt=dmt.reshape(128, 512))
    return m


_PROGS = {}


def _prog(name, S):
    key = (name, S)
    if key not in _PROGS:
        if name == 'A0':
            _PROGS[key] = build_A(0, S)
        elif name == 'A1':
            _PROGS[key] = build_A(1, S)
        elif name == 'B0':
            _PROGS[key] = build_B_even(S)
        else:
            _PROGS[key] = build_B_odd(S)
    return _PROGS[key]


def run_layer(S, layer, xb, inp):
    parity = layer % 2
    e = layer // 2
    f32 = lambda a: np.ascontiguousarray(np.asarray(a, dtype=np.float32))
    mg = f32(inp['mem_qk_gain'][layer])
    if parity == 0:
        g = f32(inp['nsa_qk_gain'][e])
        dg = f32(inp['diff_qk_gain'][e])
        gcols = [g[0], g[2], g[3], np.tile(dg[0], 2), np.tile(dg[1], 2), mg[0]]
        w_in = f32(inp['even_w_in'][e])
    else:
        g = f32(inp['dsa_qk_gain'][e])
        gcols = [g[0], g[1], mg[0]]
        w_in = f32(inp['odd_w_in'][e])
    ng = f32(inp['norm_gain'][layer])
    owns = [[np.ascontiguousarray(xb[b][own_positions(S, r)]) for r in range(4)] for b in range(2)]
    maps = [a_inputs(parity, S, owns[cc // 4][cc % 4], w_in, ng, gcols, cc % 4) for cc in range(8)]
    resA = run_bass_kernel_spmd(_prog('A%d' % parity, S), maps, core_ids=list(range(8))).results
    w_out = f32(inp['w_out'][layer])
    w_kv = f32(inp['mem_w_kv'][layer])
    mng = f32(inp['mem_norm_gain'])
    maps = []
    for b in range(2):
        fmo_b = [np.asarray(resA[b * 4 + r]['fmo']) for r in range(4)]
        tmo_b = [np.asarray(resA[b * 4 + r]['tmo']) for r in range(4)]
        tmf_b = [np.asarray(resA[b * 4 + r]['tmf']) for r in range(4)]
        mem_b = f32(inp['mem'][b])
        if parity == 0:
            shared = b_even_shared(S, fmo_b, tmo_b)
            lambda_init = 0.8 - 0.6 * math.exp(-0.3 * layer)
            for r in range(4):
                maps.append(b_even_inputs(S, r, owns[b][r], fmo_b, tmo_b, tmf_b[r], w_out, mem_b, mng, w_kv, mg,
                                          f32(inp['nsa_qk_gain'][e]), f32(inp['nsa_cmp_pos'][e]), f32(inp['nsa_cmp_w1'][e]),
                                          f32(inp['nsa_cmp_w2'][e]), f32(inp['diff_subln_gain'][e]),
                                          f32(inp['diff_lambda'][e]), shared, lambda_init))
        else:
            shared = b_odd_shared(S, fmo_b, tmo_b)
            for r in range(4):
                maps.append(b_odd_inputs(S, r, owns[b][r], fmo_b, tmo_b, tmf_b[r], w_out, mem_b, mng, w_kv, mg, shared))
    resB = run_bass_kernel_spmd(_prog('B%d' % parity, S), maps, core_ids=list(range(8))).results
    out = []
    for b in range(2):
        xn = np.empty((S, D), np.float32)
        for r in range(4):
            xn[own_positions(S, r)] = np.asarray(resB[b * 4 + r]['xo'])
        out.append(xn)
    return out


def kernel_unfused(**inputs):
    x = np.asarray(inputs['x'], dtype=np.float32)
    S = x.shape[1]
    xb = [np.ascontiguousarray(x[b]) for b in range(x.shape[0])]
    for layer in range(4):
        xb = run_layer(S, layer, xb, inputs)
    return np.stack(xb, axis=0).astype(np.float32)


def kernel(**inputs):
    return kernel_fused(inputs, 4).astype(np.float32)


def build_fused(S, depth=4):
    NT, NO, NOW, NC, n_cmp, NCT, n_sel = geom(S)
    nc = bass.Bass("TRN2", target_bir_lowering=False)
    P = Prog(nc)
    fx = FX(nc, P, S)
    x_in = nc.dram_tensor("x", [NOW, D], F32, kind="ExternalInput").ap()
    out = nc.dram_tensor("out", [NOW, D], F32, kind="ExternalOutput").ap()
    xs = [x_in] + [nc.dram_tensor(f"xs{l}", [NOW, D], F32, kind="Internal").ap() for l in range(1, depth)] + [out]
    groups = [[0, 1, 2, 3], [4, 5, 6, 7]]
    ccsrc = nc.dram_tensor("ccsrc", [256, 2048], BF16, kind="Internal").ap()
    ccdst = nc.dram_tensor("ccdst", [1024, 2048], BF16, kind="Internal").ap()
    for layer in range(depth):
        parity = layer % 2
        fx.layer = layer
        fx.x_src = xs[layer]
        fx.x_dst = xs[layer + 1]
        build_A(parity, S, fx)
        P.barrier()
        kown_fm = fx.internal("kown_fm", [NK[parity] * 128, NOW], BF16)
        kall_fm = fx.internal("kall_fm", [4 * NK[parity] * 128, NOW], BF16)
        kown_v = fx.internal("kown_v", [NV[parity] * 128, NO * 129], BF16)
        kall_v = fx.internal("kall_v", [4 * NV[parity] * 128, NO * 129], BF16)
        def gather(src_rows, dst_view, R_, C_):
            sv = ccsrc.rearrange("a b -> (a b)")[0:R_ * C_].rearrange("(r c) -> r c", c=C_)
            dv = ccdst.rearrange("a b -> (a b)")[0:4 * R_ * C_].rearrange("(r c) -> r c", c=C_)
            P.dma(sv, src_rows, r=['ccdst'], w=['ccsrc'])
            P.op('pool', lambda e, a=sv, b=dv: e.collective_compute(
                "AllGather", ALU.bypass, replica_groups=groups, ins=[a], outs=[b]), r=['ccsrc'], w=['ccdst'])
            P.dma(dst_view, dv.rearrange("(k r) c -> k r c", k=4), r=['ccdst'], w=[('kall', layer)])
        nk_, nv_ = NK[parity], NV[parity]
        kall_fm_k = kall_fm.rearrange("(k sd) n -> k sd n", k=4)
        for s0 in range(0, nk_, 2):
            ns = min(2, nk_ - s0)
            gather(kown_fm[s0 * 128:(s0 + ns) * 128, :], kall_fm_k[:, s0 * 128:(s0 + ns) * 128, :], ns * 128, NOW)
        kall_v_k = kall_v.rearrange("(k up) n -> k up n", k=4)
        for u in range(nv_):
            gather(kown_v[u * 128:(u + 1) * 128, :], kall_v_k[:, u * 128:(u + 1) * 128, :], 128, NO * 129)
        P.barrier()
        if parity == 0:
            build_B_even(S, fx)
        else:
            build_B_odd(S, fx)
        P.barrier()
    with ExitStack() as es:
        P.emit(es)
    return nc, fx


def fused_inputs(S, inp, depth=4):
    f32 = lambda a: np.ascontiguousarray(np.asarray(a, dtype=np.float32))
    x = f32(inp['x'])
    ctab, ovm, expand = even_consts(S)
    sel = np.zeros((16, 8, 128), np.float32)
    for cc_ in range(8):
        for mm_ in range(128):
            sel[2 * cc_ + mm_ // 64, cc_, mm_] = 1
    shared = dict(cmat=cmat_host(), ctab=ctab, ovm=ovm, expand=expand, sel=sel.astype(NPBF))
    mng = f32(inp['mem_norm_gain'])
    for layer in range(depth):
        e = layer // 2
        L = f"_L{layer}"
        mg = f32(inp['mem_qk_gain'][layer])
        shared['w_out' + L] = f32(inp['w_out'][layer])
        shared['w_kv' + L] = f32(inp['mem_w_kv'][layer])
        shared['memg' + L] = memg_host(mng, mg[1])
        shared['ng' + L] = np.ascontiguousarray(f32(inp['norm_gain'][layer]).reshape(KC, 128).T)
        if layer % 2 == 0:
            g = f32(inp['nsa_qk_gain'][e])
            dg = f32(inp['diff_qk_gain'][e])
            gcols = [g[0], g[2], g[3], np.tile(dg[0], 2), np.tile(dg[1], 2), mg[0]]
            shared['w_in' + L] = f32(inp['even_w_in'][e])
            lambda_init = 0.8 - 0.6 * math.exp(-0.3 * layer)
            gv = np.zeros((128, 388), np.float32)
            gv[:, 0] = g[1]
            gv[:, 2:130] = f32(inp['diff_subln_gain'][e])[None, :]
            gv[:, 130:386] = f32(inp['diff_lambda'][e]).reshape(1, 256)
            gv[:, 386] = lambda_init
            gv[:, 387] = 1.0 - lambda_init
            shared['gv' + L] = gv
            shared['cw1' + L] = f32(inp['nsa_cmp_w1'][e])
            shared['cw2' + L] = np.ascontiguousarray(f32(inp['nsa_cmp_w2'][e]).reshape(2, 2, 128, 128).transpose(2, 0, 1, 3))
            shared['peT' + L] = np.ascontiguousarray(f32(inp['nsa_cmp_pos'][e]).transpose(2, 0, 1)).astype(NPBF)
        else:
            g = f32(inp['dsa_qk_gain'][e])
            gcols = [g[0], g[1], mg[0]]
            shared['w_in' + L] = f32(inp['odd_w_in'][e])
        shared['gains' + L] = np.ascontiguousarray(np.stack(gcols, axis=1).astype(np.float32))
    maps = []
    for core in range(8):
        b, r = core // 4, core % 4
        m = dict(shared)
        m['x'] = np.ascontiguousarray(x[b][own_positions(S, r)])
        m['mem'] = f32(inp['mem'][b])
        m['tabs'] = tabs_host(S, r)
        dm, wm, cm, vf = mask_consts(S, r)
        m.update(dm=dm, wm=wm, cm=cm, visfrc=vf)
        t = np.arange(128)[:, None]
        s_ = np.arange(128)[None, :]
        dmt = np.zeros((128, 4, 128), np.float32)
        for u in range(4):
            if u == r:
                dmt[:, u, :] = np.where(s_ <= t, 0.0, -1e30)
            elif u > r:
                dmt[:, u, :] = -1e30
        m['dmt'] = dmt.reshape(128, 512)
        maps.append(m)
    return maps


_FUSED = {}


def kernel_fused(inputs, depth=4):
    x = np.asarray(inputs['x'], dtype=np.float32)
    S = x.shape[1]
    import time as _t
    t0_ = _t.time()
    key = (S, depth)
    if key not in _FUSED:
        _FUSED[key] = build_fused(S, depth)
    nc, fx = _FUSED[key]
    print("[fused] build", round(_t.time() - t0_, 1), flush=True)
    maps = fused_inputs(S, inputs, depth)
    print("[fused] inputs", round(_t.time() - t0_, 1), flush=True)
    names = set(fx.cache.keys())
    maps = [{k: v for k, v in m.items() if k == 'x' or (k in fx.ext_shapes)} for m in maps]
    res = run_bass_kernel_spmd(nc, maps, core_ids=list(range(8))).results
    print("[fused] run", round(_t.time() - t0_, 1), flush=True)
    out = np.empty_like(x)
    for core in range(8):
        b, r = core // 4, core % 4
        out[b][own_positions(S, r)] = np.asarray(res[core]['out'])
    return out
```

```python
import math
from contextlib import ExitStack
import numpy as np
import ml_dtypes
import concourse.bass as bass
import concourse.mybir as mybir
from concourse.bass_utils import run_bass_kernel_spmd

F32 = mybir.dt.float32
BF16 = mybir.dt.bfloat16
ALU = mybir.AluOpType
AF = mybir.ActivationFunctionType
AX = mybir.AxisListType
NPBF = ml_dtypes.bfloat16

D = 2048
KC = D // 128
EPS = 1e-6
NEG = -30000.0
ROPE_THETA = 10000.0

EVEN_SPLITS = (1024, 1536, 24, 1024, 512, 512, 512, 512, 512, 512)
ODD_SPLITS = (1536, 512, 512, 1024, 64, 16, 1536, 512, 512)


class Prog:
    def __init__(self, nc):
        self.nc = nc
        self.ins = []
        self.last_w = {}
        self.readers = {}
        self.pending = {}
        self.last_on = {}
        self.recent_dma = []
        self.recent_cc = []

    def barrier(self):
        deps = set(self.last_on.values()) | set(self.recent_dma[-12:]) | set(self.recent_cc[-4:])
        for e in ('pe', 'act', 'dve', 'pool', 'sp'):
            self.pending[e] = set(deps) | self.pending.get(e, set())

    def op(self, eng, fn, r=(), w=(), cc=False):
        i = len(self.ins)
        deps = set()
        if eng in self.pending:
            deps |= self.pending.pop(eng)
        self.last_on[eng] = i
        if eng == 'sp':
            self.recent_dma.append(i)
        if cc:
            self.recent_cc.append(i)
        for k in r:
            if k in self.last_w:
                deps.add(self.last_w[k])
        for k in w:
            if k in self.last_w:
                deps.add(self.last_w[k])
            deps.update(self.readers.get(k, ()))
        self.ins.append([eng, fn, deps, cc])
        for k in r:
            self.readers.setdefault(k, []).append(i)
        for k in w:
            self.last_w[k] = i
            self.readers[k] = []
        return i

    def mm(self, out, lhsT, rhs, start=True, stop=True, r=(), w=()):
        return self.op('pe', lambda e: e.matmul(out, lhsT, rhs, start=start, stop=stop), r, w)

    def tr(self, out, in_, ident, r=(), w=()):
        return self.op('pe', lambda e: e.transpose(out, in_, ident), r, w)

    def act(self, out, in_, func, r=(), w=(), **kw):
        return self.op('act', lambda e: e.activation(out, in_, func, **kw), r, w)

    def dma(self, out, in_, r=(), w=(), q='sp'):
        return self.op(q, lambda e: e.dma_start(out=out, in_=in_), r, w)

    def ts(self, eng, out, in0, s1, s2, op0, op1=None, r=(), w=(), **kw):
        if op1 is None:
            return self.op(eng, lambda e: e.tensor_scalar(out, in0, s1, None, op0, **kw), r, w)
        return self.op(eng, lambda e: e.tensor_scalar(out, in0, s1, s2, op0, op1, **kw), r, w)

    def tt(self, eng, out, in0, in1, op, r=(), w=()):
        return self.op(eng, lambda e: e.tensor_tensor(out, in0, in1, op), r, w)

    def stt(self, eng, out, in0, scalar, in1, op0, op1, r=(), w=()):
        return self.op(eng, lambda e: e.scalar_tensor_tensor(out, in0, scalar, in1, op0, op1), r, w)

    def cp(self, eng, out, in_, r=(), w=()):
        return self.op(eng, lambda e: e.tensor_copy(out, in_), r, w)

    def memset(self, eng, ap, val, r=(), w=()):
        return self.op(eng, lambda e: e.memset(ap, val), r, w)

    def emit(self, es):
        nc = self.nc
        ENGS = ['pe', 'act', 'dve', 'pool', 'sp', 'dq']
        ND = 12
        SEM_LIM = 30000
        n = len(self.ins)
        eng_of = [x[0] for x in self.ins]
        needed = [False] * n
        for i, (eng, fn, deps, cc) in enumerate(self.ins):
            if eng == 'pe':
                deps = {d for d in deps if eng_of[d] != 'pe'}
                self.ins[i][2] = deps
            for d in deps:
                needed[d] = True
        comp = [None] * n
        cnt = {e: 0 for e in ENGS}
        nsem_needed = {e: 1 for e in ENGS}
        dma_k = {'sp': 0, 'dq': 0}
        cc_k = 0
        NCC = 4
        for i, (eng, fn, deps, cc) in enumerate(self.ins):
            if cc:
                comp[i] = ('dcc', cc_k % NCC, 16 * (cc_k // NCC + 1))
                cc_k += 1
            elif eng in ('sp', 'dq'):
                k = dma_k[eng]
                dma_k[eng] += 1
                comp[i] = ('d' + eng, k % ND, 16 * (k // ND + 1))
            elif needed[i]:
                c = cnt[eng]
                cnt[eng] += 1
                comp[i] = (eng, c // SEM_LIM, c % SEM_LIM + 1)
                nsem_needed[eng] = c // SEM_LIM + 1
        sems = {}
        for e in ('pe', 'act', 'dve', 'pool'):
            for s in range(nsem_needed[e]):
                sems[(e, s)] = es.enter_context(nc.semaphore(f"s_{e}{s}"))
        for q in ('sp', 'dq'):
            if dma_k[q]:
                for s in range(ND):
                    sems[('d' + q, s)] = es.enter_context(nc.semaphore(f"s_{q}{s}"))
        if cc_k:
            for s in range(NCC):
                sems[('dcc', s)] = es.enter_context(nc.semaphore(f"s_cc{s}"))
        order = {e: [] for e in ENGS}
        for i in range(n):
            order[eng_of[i]].append(i)
        ins = self.ins
        final_dma = {q: dict() for q in ('sp', 'dq')}
        for i in range(n):
            if eng_of[i] in ('sp', 'dq'):
                c = comp[i]
                final_dma[eng_of[i]][(c[0], c[1])] = c[2]

        def run(engname, e):
            seen = {}
            dk = 0
            for i in order[engname]:
                eng, fn, deps, cc = ins[i]
                waits = {}
                for d in deps:
                    c = comp[d]
                    key = (c[0], c[1])
                    if waits.get(key, 0) < c[2]:
                        waits[key] = c[2]
                if engname in ('sp', 'dq') or cc:
                    c = comp[i]
                    if c[2] > 16:
                        key = (c[0], c[1])
                        if waits.get(key, 0) < c[2] - 16:
                            waits[key] = c[2] - 16
                for key, v in waits.items():
                    if seen.get(key, 0) >= v:
                        continue
                    seen[key] = v
                    e.wait_ge(sems[key], v)
                inst = fn(e)
                c = comp[i]
                if c is not None:
                    inst.then_inc(sems[(c[0], c[1])], 16 if (engname in ('sp', 'dq') or cc) else 1)
            if engname in ('sp', 'dq'):
                for key, v in final_dma[engname].items():
                    if seen.get(key, 0) < v:
                        e.wait_ge(sems[key], v)

        with nc.Block() as block:
            if order['sp']:
                @block.sync
                def _(e):
                    run('sp', e)
            if order['pe']:
                @block.tensor
                def _(e):
                    run('pe', e)
            if order['act'] or order['dq']:
                @block.scalar
                def _(e):
                    run('act', e)
            if order['dve']:
                @block.vector
                def _(e):
                    run('dve', e)
            if order['pool']:
                @block.gpsimd
                def _(e):
                    run('pool', e)


_UID = [0]


def _uname(name):
    _UID[0] += 1
    return f"{name}_{_UID[0]}"


class Pool:
    def __init__(self, nc, es, name, n, shape, dtype, psum=False):
        self.name = name
        name = _uname(name)
        self.n = n
        self.i = 0
        self.t = []
        for k in range(n):
            if psum:
                self.t.append(es.enter_context(nc.psum_tensor(f"{name}{k}", shape, dtype)))
            else:
                self.t.append(es.enter_context(nc.sbuf_tensor(f"{name}{k}", shape, dtype)))

    def next(self):
        k = self.i % self.n
        self.i += 1
        return self.t[k], (self.name, k)


def sb(nc, es, name, shape, dtype):
    return es.enter_context(nc.sbuf_tensor(_uname(name), shape, dtype))


class FX:
    def __init__(self, nc, P, S):
        self.nc, self.P, self.S = nc, P, S
        self.layer = 0
        self.cache = {}
        self.ext_shapes = {}

    def ext(self, name, shape, dtype, per_layer=False):
        nm = f"{name}_L{self.layer}" if per_layer else name
        if nm not in self.cache:
            self.cache[nm] = self.nc.dram_tensor(nm, shape, dtype, kind="ExternalInput").ap()
            self.ext_shapes[nm] = (name, per_layer, self.layer)
        return self.cache[nm]

    def internal(self, name, shape, dtype):
        nm = f"{name}_L{self.layer}"
        if nm not in self.cache:
            self.cache[nm] = self.nc.dram_tensor(nm, shape, dtype, kind="Internal").ap()
        return self.cache[nm]


def fm_map(parity, slot):
    if parity == 0:
        if slot < 8:
            return ('q', slot)
        if slot < 16:
            return ('k', slot - 8)
        if slot < 20:
            return ('q', 8 + slot - 16)
        if slot < 24:
            return ('k', 8 + slot - 20)
        return ('q', 12 + slot - 24)
    if slot < 12:
        return ('q', slot)
    if slot < 16:
        return ('k', slot - 12)
    if slot < 24:
        return ('q', 12 + slot - 16)
    if slot == 24:
        return ('k', 4)
    return ('q', 20 + slot - 25)


def tm_map(parity, dcol):
    if parity == 0:
        return {0: ('v', 0, 2), 256: ('v', 2, 2), 512: ('z', 0), 1024: ('z', 512), 1536: ('v', 4, 4),
                2048: ('z', 1024), 2560: ('z', 1536)}[dcol]
    return {0: ('v', 0, 4), 512: ('z', 0), 1024: ('z', 512), 1536: ('z', 1024), 2048: ('z', 1536)}[dcol]


NK = {0: 12, 1: 5}
NV = {0: 8, 1: 4}
NQS = {0: 16, 1: 24}


def own_positions(S, r):
    NT = S // 128
    NO = NT // 4
    pos = np.concatenate([np.arange(128) + (4 * j + r) * 128 for j in range(NO)])
    return pos


def rope_tables_fm(pos, dim):
    half = dim // 2
    inv = (ROPE_THETA ** (-np.arange(half, dtype=np.float32) / half)).astype(np.float32)
    ang = pos.astype(np.float32)[None, :] * inv[:, None]
    cos = np.cos(ang).astype(np.float32)
    sin = np.sin(ang).astype(np.float32)
    rows = np.arange(128) % half
    return cos[rows].astype(NPBF), sin[rows].astype(NPBF)


def rot_matrix_T(dim):
    half = dim // 2
    R = np.zeros((128, 128), np.float32)
    for m in range(128):
        blk = (m // dim) * dim
        w = m - blk
        if w < half:
            R[m, blk + w + half] = -1.0
        else:
            R[m, blk + w - half] = 1.0
    return R.T.copy().astype(NPBF)


def ones_block(dim):
    O = np.zeros((128, 128), np.float32)
    for m in range(128):
        blk = (m // dim) * dim
        O[blk:blk + dim, m] = 1.0 / dim
    return O.astype(NPBF)


def a_specs(parity):
    groups = []
    fm = [0]
    tmc = [0]
    tmf = [0]

    def fm_slot():
        fm[0] += 1
        return fm[0] - 1

    def add_fm_group(c0, nch, post, gain_idx, variant):
        for g0 in range(0, nch, 4):
            nn = min(4, nch - g0)
            items = [('fm', k * 128, post, gain_idx, variant, fm_slot()) for k in range(nn)]
            groups.append(dict(c0=c0 + g0 * 128, n=nn * 128, items=items))

    def add_tm(c0, ncols, func, dst='b'):
        for g0 in range(0, ncols, 512):
            nn = min(512, ncols - g0)
            if dst == 'b':
                dcol = tmc[0]
                tmc[0] += nn
            else:
                dcol = tmf[0]
                tmf[0] += nn
            groups.append(dict(c0=c0 + g0, n=nn, items=[('tm', 0, nn, func, dst, dcol)]))

    if parity == 0:
        c = 0
        add_fm_group(c, 8, 'nr', 0, 128); c += 1024
        add_fm_group(c, 4, 'raw', None, 128); c += 512
        groups.append(dict(c0=c, n=512, items=[('fm', 0, 'nr', 1, 128, fm_slot()), ('fm', 128, 'nr', 1, 128, fm_slot()),
                                               ('tm', 256, 256, 'copy', 'b', tmc[0])])); tmc[0] += 256; c += 512
        groups.append(dict(c0=c, n=512, items=[('fm', 0, 'nr', 2, 128, fm_slot()), ('fm', 128, 'nr', 2, 128, fm_slot()),
                                               ('tm', 256, 256, 'copy', 'b', tmc[0])])); tmc[0] += 256; c += 512
        add_tm(c, 24, 'sigmoid', 'f'); c += 24
        add_tm(c, 1024, 'silu'); c += 1024
        add_fm_group(c, 4, 'nr', 3, 64); c += 512
        add_fm_group(c, 4, 'nr', 4, 64); c += 512
        add_tm(c, 512, 'copy'); c += 512
        add_tm(c, 512, 'silu'); c += 512
        add_fm_group(c, 4, 'n', 5, 128); c += 512
        add_tm(c, 512, 'silu'); c += 512
        assert c == sum(EVEN_SPLITS)
    else:
        c = 0
        cq = c; c += 1536
        ck = c; c += 512
        cv = c; c += 512
        iq = c; c += 1024
        ik = c; c += 64
        iw = c; c += 16
        cz = c; c += 1536
        mq = c; c += 512
        mz = c; c += 512
        assert c == sum(ODD_SPLITS)
        groups.append(dict(c0=ik, n=80, items=[('tm', 64, 16, 'sign', 'f', 0), ('wrep',), ('fmrep', 0, 'r', None, 64, None)]))
        tmf[0] += 16
        add_fm_group(cq, 12, 'nr', 0, 128)
        add_fm_group(ck, 4, 'nr', 1, 128)
        add_tm(cv, 512, 'copy')
        for g0 in range(2):
            items = [('fm', k * 128, 'rw', None, 64, fm_slot(), g0 * 4 + k) for k in range(4)]
            groups.append(dict(c0=iq + g0 * 512, n=512, items=items))
        groups[0]['items'][2] = ('fmrep', 0, 'r', None, 64, fm_slot())
        add_tm(cz, 1536, 'silu')
        add_fm_group(mq, 4, 'n', 2, 128)
        add_tm(mz, 512, 'silu')
    return groups, fm[0], tmc[0], max(tmf[0], 1)


def build_A(parity, S, fx=None):
    NT = S // 128
    NO = NT // 4
    NOW = NO * 128
    NTG = NOW // 512
    groups, NFM, NTMC, NTMF = a_specs(parity)
    CIN = sum(EVEN_SPLITS) if parity == 0 else sum(ODD_SPLITS)
    NG = 6 if parity == 0 else 3
    if fx is None:
        nc = bass.Bass("TRN2", target_bir_lowering=False)
        x = nc.dram_tensor("x", [NOW, D], F32, kind="ExternalInput").ap()
        w_in = nc.dram_tensor("w_in", [D, CIN], F32, kind="ExternalInput").ap()
        ng = nc.dram_tensor("ng", [128, KC], F32, kind="ExternalInput").ap()
        gains = nc.dram_tensor("gains", [128, NG], F32, kind="ExternalInput").ap()
        tabs = nc.dram_tensor("tabs", [128, 4, NOW], BF16, kind="ExternalInput").ap()
        cmat = nc.dram_tensor("cmat", [128, 5, 128], BF16, kind="ExternalInput").ap()
        sel = nc.dram_tensor("sel", [16, 8, 128], BF16, kind="ExternalInput").ap() if parity == 1 else None
        fmo = nc.dram_tensor("fmo", [128, NFM, NOW], BF16, kind="ExternalOutput").ap()
        tmo = nc.dram_tensor("tmo", [NOW, NTMC], BF16, kind="ExternalOutput").ap()
        tmf = nc.dram_tensor("tmf", [NOW, NTMF], F32, kind="ExternalOutput").ap()
    else:
        nc = fx.nc
        x = fx.x_src
        w_in = fx.ext("w_in", [D, CIN], F32, True)
        ng = fx.ext("ng", [128, KC], F32, True)
        gains = fx.ext("gains", [128, NG], F32, True)
        tabs = fx.ext("tabs", [128, 4, NOW], BF16)
        cmat = fx.ext("cmat", [128, 5, 128], BF16)
        sel = fx.ext("sel", [16, 8, 128], BF16) if parity == 1 else None
        tmf = fx.internal("tmf", [NOW, NTMF], F32)
        fmq_d = fx.internal("fmq", [128, NQS[parity], NOW], BF16)
        kown_fm = fx.internal("kown_fm", [NK[parity] * 128, NOW], BF16)
        kown_v = fx.internal("kown_v", [NV[parity] * 128, NO * 129], BF16)
        tmz_d = fx.internal("tmz", [NOW, 2048], BF16)
        kown_fm_v = kown_fm.rearrange("(s d) n -> d s n", d=128)
        kown_v_v = kown_v.rearrange("(u p) (j d) -> p u j d", p=128, d=129)

    with ExitStack() as es:
        P = Prog(nc) if fx is None else fx.P
        hT = sb(nc, es, "hT", [128, KC, NOW], BF16)
        ng_t = sb(nc, es, "ng_t", [128, KC], F32)
        gains_t = sb(nc, es, "gains_t", [128, NG], F32)
        tabs_t = sb(nc, es, "tabs_t", [128, 4, NOW], BF16)
        cmat_t = sb(nc, es, "cmat_t", [128, 5, 128], BF16)
        eps_t = sb(nc, es, "eps_t", [128, 1], F32)
        xin = Pool(nc, es, "xin", 2, [128, D], F32)
        xs = Pool(nc, es, "xs", 1, [128, D], BF16)
        junk = sb(nc, es, "junk", [128, D], BF16)
        ssp = Pool(nc, es, "ss", 4, [128, 1], F32)
        wst = Pool(nc, es, "wst", 2, [128, KC // 2, 512], F32)
        wbf = Pool(nc, es, "wbf", 2, [128, KC, 512], BF16)
        psq = Pool(nc, es, "psq", 3, [128, 512], F32, psum=True)
        psb = Pool(nc, es, "psb", 2, [128, 512], F32, psum=True)
        psc = Pool(nc, es, "psc", 2, [128, 512], F32, psum=True)
        pst = Pool(nc, es, "pst", 1, [128, 8, 128], BF16, psum=True)
        sqb = Pool(nc, es, "sqb", 2, [128, 512], BF16)
        f1 = Pool(nc, es, "f1", 2, [128, 512], F32)
        f2 = Pool(nc, es, "f2", 2, [128, 512], F32)
        f3 = Pool(nc, es, "f3", 2, [128, 512], F32)
        qn = Pool(nc, es, "qn", 2, [128, 512], BF16)
        stg = Pool(nc, es, "stg", 5, [128, 512], BF16)
        stf = Pool(nc, es, "stf", 2, [128, 32], F32)
        stv = Pool(nc, es, "stv", 2, [128, 4, 129], BF16) if fx is not None else None
        if fx is not None:
            for t_ in stv.t:
                P.memset('pool', t_[:, :, 128:129], 1.0, w=[('stv', stv.t.index(t_))])
        wrep = sb(nc, es, "wrep", [128, KC, 128], BF16) if parity == 1 else None
        sel_t = sb(nc, es, "sel_t", [16, 8, 128], BF16) if parity == 1 else None
        aw16 = sb(nc, es, "aw16", [16, NOW], BF16) if parity == 1 else None
        if parity == 1:
            P.dma(sel_t[:], sel, w=['sel'])

        ident = cmat_t[:, 0, :]
        P.dma(ng_t[:], ng, w=['ng'])
        P.dma(gains_t[:], gains, w=['gains'])
        P.dma(tabs_t[:], tabs, w=['tabs'])
        P.dma(cmat_t[:], cmat, w=['cmat'])
        P.memset('dve', eps_t[:], EPS, w=['eps'])

        import os
        _p1 = os.environ.get('A_P1', 'full')
        for j in range(NO if not _p1.startswith('one') else 1):
            xt, xk = xin.next()
            P.dma(xt[:], x[j * 128:(j + 1) * 128, :], w=[xk])
            st, sk = ssp.next()
            P.act(junk[:], xt[:], AF.Square, r=[xk], w=['junk', sk], accum_out=st[:])
            s2, s2k = ssp.next()
            P.act(s2[:], st[:], AF.Sqrt, r=[sk, 'eps'], w=[s2k], scale=1.0 / D, bias=eps_t[:])
            P.op('dve', lambda e, a=s2: e.reciprocal(a[:], a[:]), r=[s2k], w=[s2k])
            xb, xbk = xs.next()
            P.ts('dve', xb[:], xt[:], s2[:], None, ALU.mult, r=[xk, s2k], w=[xbk])
            if _p1 == 'notr':
                continue
            for half in range(2 if _p1 != 'oneh' else 1):
                pt, ptk = pst.next()
                for q in range(8):
                    kc = half * 8 + q
                    P.tr(pt[:, q, :], xb[:, kc * 128:(kc + 1) * 128], ident, r=[xbk, 'cmat'], w=[ptk])
                for q in range(8):
                    kc = half * 8 + q
                    eng = 'dve'
                    if eng == 'pool':
                        P.act(hT[:, kc, j * 128:(j + 1) * 128], pt[:, q, :], AF.Copy, r=[ptk, 'ng'],
                              w=[('hT', j, kc)], scale=ng_t[:, kc:kc + 1])
                    else:
                        P.ts('dve', hT[:, kc, j * 128:(j + 1) * 128], pt[:, q, :], ng_t[:, kc:kc + 1], None,
                             ALU.mult, r=[ptk, 'ng'], w=[('hT', j, kc)])

        def hkeys(tiles):
            return [('hT', j, kc) for j in tiles for kc in range(KC)]

        def fm_unit(it, tg, wb, wbk, coff, post, gidx, slot, onesb, rotT, cosT, sinT):
            tiles = range(4 * tg, 4 * tg + 4)
            cols = slice(tg * 512, (tg + 1) * 512)
            pa, pak = psq.next()
            for kc in range(KC):
                if it[0] == 'fmrep':
                    lhsT = wrep[:, kc, :]
                    rk = [('wrep', kc)]
                else:
                    lhsT = wb[:, kc, coff:coff + 128]
                    rk = [(wbk, kc)]
                P.mm(pa[:], lhsT, hT[:, kc, cols], start=(kc == 0), stop=(kc == KC - 1),
                     r=rk + [('hT', j, kc) for j in tiles], w=[pak])
            so, sok = stg.next()
            if post == 'raw':
                P.act(so[:], pa[:], AF.Copy, r=[pak], w=[sok])
            elif post in ('nr', 'n'):
                sq, sqk = sqb.next()
                P.act(sq[:], pa[:], AF.Square, r=[pak], w=[sqk])
                yield
                pb, pbk = psb.next()
                P.mm(pb[:], onesb, sq[:], r=[sqk, 'cmat'], w=[pbk])
                sd, sdk = f1.next()
                P.act(sd[:], pb[:], AF.Sqrt, r=[pbk, 'eps'], w=[sdk], bias=eps_t[:])
                P.op('dve', lambda e, a=sd: e.reciprocal(a[:], a[:]), r=[sdk], w=[sdk])
                if post == 'n':
                    P.stt('dve', so[:], pa[:], gains_t[:, gidx:gidx + 1], sd[:], ALU.mult, ALU.mult,
                          r=[pak, sdk, 'gains'], w=[sok])
                else:
                    q_, qk = qn.next()
                    P.stt('dve', q_[:], pa[:], gains_t[:, gidx:gidx + 1], sd[:], ALU.mult, ALU.mult,
                          r=[pak, sdk, 'gains'], w=[qk])
                    yield
                    pc, pck = psc.next()
                    P.mm(pc[:], rotT, q_[:], r=[qk, 'cmat'], w=[pck])
                    t1, t1k = f2.next()
                    P.tt('pool', t1[:], q_[:], cosT[:, cols], ALU.mult, r=[qk, 'tabs'], w=[t1k])
                    t2, t2k = f3.next()
                    P.tt('dve', t2[:], pc[:], sinT[:, cols], ALU.mult, r=[pck, 'tabs'], w=[t2k])
                    P.tt('pool', so[:], t1[:], t2[:], ALU.add, r=[t1k, t2k], w=[sok])
            elif post == 'r':
                q_, qk = qn.next()
                P.act(q_[:], pa[:], AF.Copy, r=[pak], w=[qk])
                yield
                pc, pck = psc.next()
                P.mm(pc[:], rotT, q_[:], r=[qk, 'cmat'], w=[pck])
                t1, t1k = f2.next()
                P.tt('pool', t1[:], q_[:], cosT[:, cols], ALU.mult, r=[qk, 'tabs'], w=[t1k])
                t2, t2k = f3.next()
                P.tt('dve', t2[:], pc[:], sinT[:, cols], ALU.mult, r=[pck, 'tabs'], w=[t2k])
                P.tt('pool', so[:], t1[:], t2[:], ALU.add, r=[t1k, t2k], w=[sok])
            elif post == 'rw':
                cidx = it[6]
                q_, qk = qn.next()
                P.act(q_[:], pa[:], AF.Copy, r=[pak], w=[qk])
                yield
                pc, pck = psc.next()
                P.mm(pc[:], rotT, q_[:], r=[qk, 'cmat'], w=[pck])
                pw, pwk = psb.next()
                P.mm(pw[:], sel_t[:, cidx, :], aw16[:, cols], r=['sel', ('aw16', tg)], w=[pwk])
                t1, t1k = f2.next()
                P.tt('pool', t1[:], q_[:], cosT[:, cols], ALU.mult, r=[qk, 'tabs'], w=[t1k])
                t2, t2k = f3.next()
                P.tt('dve', t2[:], pc[:], sinT[:, cols], ALU.mult, r=[pck, 'tabs'], w=[t2k])
                P.tt('pool', t1[:], t1[:], t2[:], ALU.add, r=[t1k, t2k], w=[t1k])
                P.tt('dve', so[:], t1[:], pw[:], ALU.mult, r=[t1k, pwk], w=[sok])
            if fx is None:
                P.dma(fmo[:, slot, cols], so[:], r=[sok], w=[('fmo', slot, tg)])
            else:
                kind_, idx_ = fm_map(parity, slot)
                dst_ = fmq_d[:, idx_, cols] if kind_ == 'q' else kown_fm_v[:, idx_, cols]
                P.dma(dst_, so[:], r=[sok], w=[('fmo', slot, tg)])
            return
            yield

        fm_live = []

        def fm_push(g_):
            next(g_, None)
            for o_ in list(fm_live):
                try:
                    next(o_)
                except StopIteration:
                    fm_live.remove(o_)
            fm_live.append(g_)

        def fm_flush():
            while fm_live:
                for o_ in list(fm_live):
                    try:
                        next(o_)
                    except StopIteration:
                        fm_live.remove(o_)

        funcs = {'copy': AF.Copy, 'silu': AF.Silu, 'sigmoid': AF.Sigmoid, 'sign': AF.Sign}
        import os
        _lim = int(os.environ.get('A_GROUPS', '999'))
        for gi, g in enumerate(groups):
            if gi >= _lim:
                break
            n = g['n']
            wb, wbk = wbf.next()
            wv = w_in[:, g['c0']:g['c0'] + n].rearrange("(kc p) c -> p kc c", p=128)
            for hf in range(2):
                ws, wsk = wst.next()
                P.dma(ws[:, :, 0:n], wv[:, hf * 8:(hf + 1) * 8, :], w=[wsk])
                for q in range(8):
                    kc = hf * 8 + q
                    eng = ('dve', 'pool')[kc % 2]
                    P.cp(eng, wb[:, kc, 0:n], ws[:, q, 0:n], r=[wsk], w=[(wbk, kc)])
            wkeys = [(wbk, kc) for kc in range(KC)]
            for it in g['items']:
                if it[0] == 'wrep':
                    fm_flush()
                    for kc in range(KC):
                        for hh in range(2):
                            P.cp('pool', wrep[:, kc, hh * 64:(hh + 1) * 64], wb[:, kc, 0:64],
                                 r=[(wbk, kc)], w=[('wrep', kc)])
                    for tg in range(NTG):
                        cols = slice(tg * 512, (tg + 1) * 512)
                        pa, pak = psq.next()
                        for kc in range(KC):
                            P.mm(pa[0:16, :], wb[:, kc, 64:80], hT[:, kc, cols], start=(kc == 0), stop=(kc == KC - 1),
                                 r=[(wbk, kc)] + [('hT', j, kc) for j in range(4 * tg, 4 * tg + 4)], w=[pak])
                        P.act(aw16[:, cols], pa[0:16, :], AF.Abs, r=[pak], w=[('aw16', tg)], scale=1.0 / 32.0)
                    continue
                if it[0] in ('fm', 'fmrep'):
                    _, coff, post, gidx, variant, slot = it[:6]
                    vi = 0 if variant == 128 else 1
                    onesb = cmat_t[:, 1 + vi, :]
                    rotT = cmat_t[:, 3 + vi, :]
                    cosT = tabs_t[:, 2 * vi, :]
                    sinT = tabs_t[:, 2 * vi + 1, :]
                    for tg in range(NTG):
                        fm_push(fm_unit(it, tg, wb, wbk, coff, post, gidx, slot, onesb, rotT, cosT, sinT))
                elif it[0] == 'tm':
                    fm_flush()
                    _, coff, nn, func, dst, dcol = it
                    for j in range(NO):
                        pa, pak = psq.next()
                        for kc in range(KC):
                            P.mm(pa[:, 0:nn], hT[:, kc, j * 128:(j + 1) * 128], wb[:, kc, coff:coff + nn],
                                 start=(kc == 0), stop=(kc == KC - 1), r=[(wbk, kc), ('hT', j, kc)], w=[pak])
                        if dst == 'b' and fx is not None and tm_map(parity, dcol)[0] == 'v':
                            _, u0_, nu_ = tm_map(parity, dcol)
                            so, sok = stv.next()
                            P.act(so[:, 0:nu_, 0:128], pa[:, 0:nn].rearrange("p (u d) -> p u d", d=128), funcs[func],
                                  r=[pak], w=[sok])
                            P.dma(kown_v_v[:, u0_:u0_ + nu_, j, :], so[:, 0:nu_, :], r=[sok], w=[('tmo', dcol, j)])
                        elif dst == 'b':
                            so, sok = stg.next()
                            P.act(so[:, 0:nn], pa[:, 0:nn], funcs[func], r=[pak], w=[sok])
                            if fx is None:
                                P.dma(tmo[j * 128:(j + 1) * 128, dcol:dcol + nn], so[:, 0:nn], r=[sok], w=[('tmo', dcol, j)])
                            else:
                                zc_ = tm_map(parity, dcol)[1]
                                P.dma(tmz_d[j * 128:(j + 1) * 128, zc_:zc_ + nn], so[:, 0:nn], r=[sok], w=[('tmo', dcol, j)])
                        else:
                            so, sok = stf.next()
                            P.act(so[:, 0:nn], pa[:, 0:nn], funcs[func], r=[pak], w=[sok])
                            P.dma(tmf[j * 128:(j + 1) * 128, dcol:dcol + nn], so[:, 0:nn], r=[sok], w=[('tmf', dcol, j)])
        fm_flush()
        if fx is None:
            P.emit(es)
    return nc


def cmat_host():
    ident = np.eye(128, dtype=np.float32).astype(NPBF)
    return np.stack([ident, ones_block(128), ones_block(64), rot_matrix_T(128), rot_matrix_T(64)], axis=1)


def tabs_host(S, r):
    pos = own_positions(S, r)
    c128, s128 = rope_tables_fm(pos, 128)
    c64, s64 = rope_tables_fm(pos, 64)
    return np.stack([c128, s128, c64, s64], axis=1)


def geom(S):
    NT = S // 128
    NO = NT // 4
    NOW = NO * 128
    NC = NO // 4
    n_cmp = (S - 32) // 16 + 1
    NCT = (n_cmp + 127) // 128
    n_sel = S // 64
    return NT, NO, NOW, NC, n_cmp, NCT, n_sel


class BCtx:
    pass


def b_common(nc, es, P, S, parity, fx=None):
    NT, NO, NOW, NC, n_cmp, NCT, n_sel = geom(S)
    c = BCtx()
    c.S, c.NT, c.NO, c.NOW, c.NC = S, NT, NO, NOW, NC
    c.fx = fx
    NQ = 16 if parity == 0 else 24
    NZ = 2048
    if fx is None:
        dt = nc.dram_tensor
        c.ext = lambda name, shape, dtype, per_layer=False: dt(name, shape, dtype, kind="ExternalInput").ap()
        c.x = dt("x", [NOW, D], F32, kind="ExternalInput").ap()
        c.xo = dt("xo", [NOW, D], F32, kind="ExternalOutput").ap()
        c.fmq = dt("fmq", [128, NQ, NOW], BF16, kind="ExternalInput").ap()
        c.tmz = dt("tmz", [NOW, NZ], BF16, kind="ExternalInput").ap()
    else:
        c.ext = fx.ext
        c.x = fx.x_src
        c.xo = fx.x_dst
        c.fmq = fx.internal("fmq", [128, NQ, NOW], BF16)
        c.tmz = fx.internal("tmz", [NOW, NZ], BF16)
        c.tmf = fx.internal("tmf", [NOW, 24 if parity == 0 else 16], F32)
        c.kall_fm = fx.internal("kall_fm", [4 * NK[parity] * 128, NOW], BF16)
        c.kall_v = fx.internal("kall_v", [4 * NV[parity] * 128, NO * 129], BF16)
        c.kall_fm_v = c.kall_fm.rearrange("(r s d) (j p) -> d s j r p", r=4, d=128, p=128)
        c.kall_v_v = c.kall_v.rearrange("(r u p) (j d) -> p u j r d", r=4, p=128, d=129)
    c.w_out = c.ext("w_out", [D, D], F32, True)
    c.mem = c.ext("mem", [256, D], F32)
    c.w_kv = c.ext("w_kv", [D, 1024], F32, True)
    c.memg = c.ext("memg", [128, KC + 1], F32, True)
    c.cmat = c.ext("cmat", [128, 5, 128], BF16)
    c.dm = c.ext("dm", [128, 4, 128], BF16)

    def load_k(slot, kg_unfused):
        if fx is None:
            P.dma(c.bufK[:, 0:S], kg_unfused[:, slot, :], w=['bufK'])
        else:
            dv = c.bufK[:, 0:S].rearrange("d (j r p) -> d j r p", r=4, p=128)
            for r_ in range(4):
                P.dma(dv[:, :, r_, :], c.kall_fm_v[:, slot, :, r_, :], w=['bufK'])
    c.load_k = load_k

    def load_v(unit, vg_unfused):
        if fx is None:
            P.dma(c.bufV[:, 0:NT * 129], vg_unfused[unit], w=['bufV'])
        else:
            dv = c.bufV[:, 0:NT * 129].rearrange("p (j r d) -> p j r d", r=4, d=129)
            for r_ in range(4):
                P.dma(dv[:, :, r_, :], c.kall_v_v[:, unit, :, r_, :], w=['bufV'])
    c.load_v = load_v

    c.cmat_t = sb(nc, es, "cmat_t", [128, 5, 128], BF16)
    c.dm_t = sb(nc, es, "dm_t", [128, 4, 128], BF16)
    c.memg_t = sb(nc, es, "memg_t", [128, KC + 1], F32)
    c.eps_t = sb(nc, es, "eps_t", [128, 1], F32)
    c.y_all = sb(nc, es, "y_all", [128, NO, D], BF16)
    KCOLS = max(S, 8192)
    VCOLS = max(NT * 129, 8192)
    c.big = sb(nc, es, "big", [128, KCOLS + VCOLS], BF16)
    c.bufK = c.big[:, 0:KCOLS]
    c.bufV = c.big[:, KCOLS:KCOLS + VCOLS]
    c.qT = sb(nc, es, "qT", [128, 4, NOW], BF16)
    c.wst = Pool(nc, es, "wst", 1, [128, KC // 2, 512], F32)
    c.memT = c.bufV[:, 0:KC * 256].rearrange("p (k m) -> p k m", m=256)
    c.MKT = sb(nc, es, "MKT", [128, 4, 256], BF16)
    c.MVx = sb(nc, es, "MVx", [128, 2, 4, 129], BF16)
    c.stp = Pool(nc, es, "stp", 2, [128, 512], F32, psum=True)
    c.obank = [es.enter_context(nc.psum_tensor(_uname(f"ob{i}"), [128, 512], F32)) for i in range(4)]
    c.psm = Pool(nc, es, "psm", 1, [128, 512], F32, psum=True)
    c.pst = Pool(nc, es, "pst", 1, [128, 8, 128], BF16, psum=True)
    c.Ep = Pool(nc, es, "Ep", 3, [128, 512], BF16)
    c.f1 = Pool(nc, es, "f1", 2, [128, 512], F32)
    c.b1 = Pool(nc, es, "b1", 2, [128, 512], BF16)
    c.sm = Pool(nc, es, "sm", 8, [128, 4], F32)
    c.zp = Pool(nc, es, "zp", 3, [128, 512], BF16)

    def zt_for(j, col0, n):
        t, k = c.zp.next()
        P.dma(t[:, 0:n], c.tmz[j * 128:(j + 1) * 128, col0:col0 + n], w=[k])
        return t, k
    c.zt_for = zt_for
    c.xp = Pool(nc, es, "xp", 2, [128, 512], F32)
    c.ident = c.cmat_t[:, 0, :]
    c.ones128 = c.cmat_t[:, 1, :]
    c.ones64 = c.cmat_t[:, 2, :]
    c.rot128 = c.cmat_t[:, 3, :]

    P.dma(c.cmat_t[:], c.cmat, w=['cmat'])
    P.dma(c.dm_t[:], c.dm, w=['dm'])
    P.dma(c.memg_t[:], c.memg, w=['memg'])
    P.memset('dve', c.eps_t[:], EPS, w=['eps'])
    return c


def fm_norm(P, c, pa, pak, ncols, gain_ap, gain_key, out_ap, out_keys, ones=None):
    ones = c.ones128 if ones is None else ones
    sq, sqk = c.b1.next()
    P.act(sq[:, 0:ncols], pa[:, 0:ncols], AF.Square, r=[pak], w=[sqk])
    pb, pbk = c.psm.next()
    P.mm(pb[:, 0:ncols], ones, sq[:, 0:ncols], r=[sqk, 'cmat'], w=[pbk])
    sd, sdk = c.f1.next()
    P.act(sd[:, 0:ncols], pb[:, 0:ncols], AF.Sqrt, r=[pbk, 'eps'], w=[sdk], bias=c.eps_t[:])
    P.op('dve', lambda e: e.reciprocal(sd[:, 0:ncols], sd[:, 0:ncols]), r=[sdk], w=[sdk])
    P.stt('dve', out_ap, pa[:, 0:ncols], gain_ap, sd[:, 0:ncols], ALU.mult, ALU.mult,
          r=[pak, sdk, gain_key], w=out_keys)


def load_cast_w(P, c, dst, dst_key, w_dram_view, n):
    for hf in range(2):
        ws, wsk = c.wst.next()
        P.dma(ws[:, :, 0:n], w_dram_view[:, hf * 8:(hf + 1) * 8, :], w=[wsk])
        for q in range(8):
            kc = hf * 8 + q
            eng = ('dve', 'pool')[kc % 2]
            P.cp(eng, dst[:, kc, 0:n], ws[:, q, 0:n], r=[wsk], w=[dst_key])


def b_memory_kv(P, c):
    for mt in range(2):
        wst0, xk = c.wst.next()
        xt = wst0[:].rearrange("p a b -> p (a b)")[:, 0:D]
        P.dma(xt[:], c.mem[mt * 128:(mt + 1) * 128, :], w=[xk])
        xb, xbk = c.y_all[:, 0, :], ('y', 0)
        st, sk = c.sm.next()
        P.act(xb[:], xt[:], AF.Square, r=[xk], w=[xbk, sk], accum_out=st[:, 0:1])
        P.act(st[:, 1:2], st[:, 0:1], AF.Sqrt, r=[sk, 'eps'], w=[sk], scale=1.0 / D, bias=c.eps_t[:])
        P.op('dve', lambda e, a=st: e.reciprocal(a[:, 1:2], a[:, 1:2]), r=[sk], w=[sk])
        P.ts('dve', xb[:], xt[:], st[:, 1:2], None, ALU.mult, r=[xk, sk], w=[xbk])
        for half in range(2):
            pt, ptk = c.pst.next()
            for q in range(8):
                kc = half * 8 + q
                P.tr(pt[:, q, :], xb[:, kc * 128:(kc + 1) * 128], c.ident, r=[xbk, 'cmat'], w=[ptk])
            for q in range(8):
                kc = half * 8 + q
                P.ts('dve', c.memT[:, kc, mt * 128:(mt + 1) * 128], pt[:, q, :], c.memg_t[:, kc:kc + 1], None,
                     ALU.mult, r=[ptk, 'memg'], w=['bufV'])
    wv = c.w_kv.rearrange("(kc p) c -> p kc c", p=128)
    wk = c.bufK[:, 0:KC * 512].rearrange("p (k c) -> p k c", c=512)
    load_cast_w(P, c, wk, 'bufK', wv[:, :, 0:512], 512)
    for h in range(4):
        pa, pak = c.stp.next()
        for kc in range(KC):
            P.mm(pa[:, 0:256], wk[:, kc, h * 128:(h + 1) * 128], c.memT[:, kc, :], start=(kc == 0), stop=(kc == KC - 1),
                 r=['bufK', 'bufV'], w=[pak])
        fm_norm(P, c, pa, pak, 256, c.memg_t[:, KC:KC + 1], 'memg', c.MKT[:, h, :], ['MKT'])
    load_cast_w(P, c, wk, 'bufK', wv[:, :, 512:1024], 512)
    P.memset('pool', c.MVx[:, :, :, 128:129], 1.0, w=['MVx'])
    for mt in range(2):
        pa, pak = c.stp.next()
        for kc in range(KC):
            P.mm(pa[:], c.memT[:, kc, mt * 128:(mt + 1) * 128], wk[:, kc, :], start=(kc == 0), stop=(kc == KC - 1),
                 r=['bufK', 'bufV'], w=[pak])
        P.act(c.MVx[:, mt, :, 0:128], pa[:].rearrange("p (h d) -> p h d", d=128), AF.Copy, r=[pak], w=['MVx'])


def attend(P, c, steps, scale):
    prev = None

    def do_pv(st_):
        E, Ek = st_['E']
        for (ec0, oap, okey, rhs, rkeys, first, last) in st_['pv']:
            P.mm(oap, E[:, ec0:ec0 + 128], rhs, start=first, stop=last, r=[Ek] + rkeys, w=[okey])

    for st_ in steps:
        ps, psk = c.stp.next()
        c0, n = st_['c0'], st_['n']
        nm = len(st_['masks'])
        P.mm(ps[:, c0:c0 + n], st_['k'], st_['q'], start=True, stop=(nm == 0), r=st_['kkeys'] + st_['qkeys'], w=[psk])
        for mi, (mc0, mn, ml, mr, mk) in enumerate(st_['masks']):
            P.mm(ps[:, mc0:mc0 + mn], ml, mr, start=False, stop=(mi == nm - 1), r=mk, w=[psk])
        E, Ek = c.Ep.next()
        P.act(E[:, c0:c0 + n], ps[:, c0:c0 + n], AF.Exp, r=[psk], w=[Ek], scale=scale)
        st_['E'] = (E, Ek)
        if prev is not None:
            do_pv(prev)
        prev = st_
    if prev is not None:
        do_pv(prev)


def b_mem_attention(P, c, mq_slot0, ycol0, zcol0):
    NO, NC = c.NO, c.NC
    P.dma(c.qT[:], c.fmq[:, mq_slot0:mq_slot0 + 4, :], w=['qT'])
    scale = 128 ** -0.5
    for h in range(4):
        for ch in range(NC):
            steps = []
            for mt in range(2):
                pv = [(i * 128, c.obank[i][:, 0:129], ('ob', i), c.MVx[:, mt, h, :], ['MVx'], mt == 0, mt == 1)
                      for i in range(4)]
                steps.append(dict(k=c.MKT[:, h, mt * 128:(mt + 1) * 128], kkeys=['MKT'],
                                  q=c.qT[:, h, ch * 512:(ch + 1) * 512], qkeys=['qT'], c0=0, n=512, masks=[], pv=pv))
            attend(P, c, steps, scale)
            for i in range(4):
                j = ch * 4 + i
                zt, ztk = c.zt_for(j, zcol0 + h * 128, 128)
                s_, sk = c.sm.next()
                P.ts('dve', s_[:, 0:1], c.obank[i][:, 128:129], 1e-30, None, ALU.max, r=[('ob', i)], w=[sk])
                P.op('dve', lambda e, a=s_: e.reciprocal(a[:, 0:1], a[:, 0:1]), r=[sk], w=[sk])
                P.stt('dve', c.y_all[:, j, ycol0 + h * 128:ycol0 + (h + 1) * 128], c.obank[i][:, 0:128], s_[:, 0:1],
                      zt[:, 0:128], ALU.mult, ALU.mult,
                      r=[('ob', i), sk, ztk], w=[('y', j)])


def b_out_proj(P, c):
    NO = c.NO
    for j in range(NO):
        for half in range(2):
            pt, ptk = c.pst.next()
            for q in range(8):
                kc = half * 8 + q
                P.tr(pt[:, q, :], c.y_all[:, j, kc * 128:(kc + 1) * 128], c.ident, r=[('y', j), 'cmat'], w=[ptk])
            P.cp('dve', c.y_all[:, j, half * 1024:(half + 1) * 1024], pt[:].rearrange("p a b -> p (a b)"),
                 r=[ptk], w=[('y', j)])
    wv = c.w_out.rearrange("(kc p) c -> p kc c", p=128)
    wbs = [(c.bufK[:, 0:KC * 512].rearrange("p (k c) -> p k c", c=512), 'bufK'),
           (c.bufV[:, 0:KC * 512].rearrange("p (k c) -> p k c", c=512), 'bufV')]
    for cg in range(4):
        wb, wbk = wbs[cg % 2]
        load_cast_w(P, c, wb, wbk, wv[:, :, cg * 512:(cg + 1) * 512], 512)
        for j in range(NO):
            pa, pak = c.stp.next()
            for kc in range(KC):
                P.mm(pa[:], c.y_all[:, j, kc * 128:(kc + 1) * 128], wb[:, kc, :], start=(kc == 0), stop=(kc == KC - 1),
                     r=[('y', j), wbk], w=[pak])
            xt, xk = c.xp.next()
            P.dma(xt[:], c.x[j * 128:(j + 1) * 128, cg * 512:(cg + 1) * 512], w=[xk])
            P.tt('dve', xt[:], xt[:], pa[:], ALU.add, r=[xk, pak], w=[xk])
            P.dma(c.xo[j * 128:(j + 1) * 128, cg * 512:(cg + 1) * 512], xt[:], r=[xk], w=[('xo', j, cg)])


def build_B_even(S, fx=None):
    NT, NO, NOW, NC, n_cmp, NCT, n_sel = geom(S)
    NCP = NCT * 128
    nc = bass.Bass("TRN2", target_bir_lowering=False) if fx is None else fx.nc
    with ExitStack() as es:
        P = Prog(nc) if fx is None else fx.P
        c = b_common(nc, es, P, S, 0, fx)
        if fx is None:
            kg = c.ext("kg", [128, 12, S], BF16)
            vg = c.ext("vg", [8, 128, NT * 129], BF16)
            ag = c.ext("ag", [NOW, 24], F32)
        else:
            kg = vg = None
            ag = c.tmf
        cw1 = c.ext("cw1", [2, 4096, 256], F32, True)
        cw2 = c.ext("cw2", [128, 2, 2, 128], F32, True)
        peT = c.ext("peT", [128, 2, 32], BF16, True)
        ctab = c.ext("ctab", [128, 2, NCP], BF16)
        ovm = c.ext("ovm", [128, NCT, 128], BF16)
        expand = c.ext("expand", [128, 32, 128], BF16)
        wm = c.ext("wm", [128, 8, 128], BF16)
        cm = c.ext("cm", [NO, 128, NCT, 128], BF16)
        visfrc = c.ext("visfrc", [NO, 128, 2, 128], F32)
        gv = c.ext("gv", [128, 388], F32, True)

        cw2_f = sb(nc, es, "cw2_f", [128, 2, 2, 128], F32)
        cw2_t = sb(nc, es, "cw2_t", [128, 2, 2, 128], BF16)
        peT_t = sb(nc, es, "peT_t", [128, 2, 32], BF16)
        ctab_t = sb(nc, es, "ctab_t", [128, 2, NCP], BF16)
        ovm_t = sb(nc, es, "ovm_t", [128, NCT, 128], BF16)
        expand_t = sb(nc, es, "expand_t", [128, 32, 128], BF16)
        wm_t = sb(nc, es, "wm_t", [128, 8, 128], BF16)
        gv_t = sb(nc, es, "gv_t", [128, 388], F32)
        KCc = sb(nc, es, "KCc", [128, 2, NCP], BF16)
        VCx = sb(nc, es, "VCx", [128, 2, NCT, 257], BF16)
        hid = sb(nc, es, "hid", [128, 2, NCP], BF16)
        cbias = sb(nc, es, "cbias", [128, 2], F32)
        lam = sb(nc, es, "lam", [128, 4], F32)
        cmp_ = Pool(nc, es, "cmp", 2, [128, NCT, 128], BF16)
        vfp = Pool(nc, es, "vfp", 2, [128, 2, 128], F32)
        agp = Pool(nc, es, "agp", 2, [128, 24], F32)
        acc = Pool(nc, es, "acc", 1, [128, 512], F32)
        imp = Pool(nc, es, "imp", 2, [128, 128], F32)
        imp2 = Pool(nc, es, "imp2", 2, [128, 128], F32)
        m8 = Pool(nc, es, "m8", 2, [128, 16], F32)
        seln = Pool(nc, es, "seln", 2, [128, 128], BF16)
        selT = Pool(nc, es, "selT", 2, [128, 128], BF16)
        kwp = Pool(nc, es, "kwp", 1, [128, 8, 128], BF16)
        vwp = Pool(nc, es, "vwp", 1, [128, 8, 129], BF16)
        dtmp = Pool(nc, es, "dtmp", 2, [128, 4, 128], F32)

        for (t, src, k) in ((cw2_f, cw2, 'cw2f'), (peT_t, peT, 'peT'), (ctab_t, ctab, 'ctab'), (ovm_t, ovm, 'ovm'),
                            (expand_t, expand, 'expand'), (wm_t, wm, 'wm'), (gv_t, gv, 'gv')):
            P.dma(t[:], src, w=[k])
        P.cp('dve', cw2_t[:], cw2_f[:], r=['cw2f'], w=['cw2'])
        lv = gv_t[:, 130:130 + 256]
        j1, j1k = c.f1.next()
        P.tt('dve', j1[:, 0:64], lv[:, 0:64], lv[:, 64:128], ALU.mult, r=['gv'], w=[j1k])
        P.tt('dve', j1[:, 64:128], lv[:, 128:192], lv[:, 192:256], ALU.mult, r=['gv'], w=[j1k])
        P.op('dve', lambda e: e.reduce_sum(lam[:, 0:1], j1[:, 0:64], AX.X), r=[j1k], w=['lam'])
        P.op('dve', lambda e: e.reduce_sum(lam[:, 1:2], j1[:, 64:128], AX.X), r=[j1k], w=['lam'])
        P.act(lam[:, 0:2], lam[:, 0:2], AF.Exp, r=['lam'], w=['lam'])
        P.tt('dve', lam[:, 2:3], lam[:, 0:1], lam[:, 1:2], ALU.subtract, r=['lam'], w=['lam'])
        P.tt('dve', lam[:, 3:4], lam[:, 2:3], gv_t[:, 386:387], ALU.add, r=['lam', 'gv'], w=['lam'])
        P.ts('dve', lam[:, 3:4], lam[:, 3:4], -1.0, None, ALU.mult, r=['lam'], w=['lam'])

        b_memory_kv(P, c)

        P.memset('pool', hid[:], 0.0, w=['hid'])
        P.memset('pool', KCc[:], 0.0, w=['KCc'])
        P.memset('pool', VCx[:], 0.0, w=['VCx'])
        P.memset('pool', VCx[:, :, :, 256:257], 1.0, w=['VCx'])
        for g in range(2):
            for ct in range(NCT):
                P.cp('pool', VCx[:, g, ct, 128:256], ovm_t[:, ct, :], r=['ovm'], w=['VCx'])
        w1v = c.bufV[:, 0:32 * 256].rearrange("p (l h) -> p l h", h=256)
        for kv in range(2):
            wsrc = cw1[kv].rearrange("(l p) h -> p l h", p=128)
            for q4 in range(4):
                ws, wsk = c.wst.next()
                wsv = ws[:].rearrange("p a b -> p (a b)")[:, 0:8 * 256].rearrange("p (l h) -> p l h", h=256)
                P.dma(wsv, wsrc[:, q4 * 8:(q4 + 1) * 8, :], w=[wsk])
                P.cp('dve', w1v[:, q4 * 8:q4 * 8 + 4, :], wsv[:, 0:4, :], r=[wsk], w=['bufV'])
                P.cp('pool', w1v[:, q4 * 8 + 4:q4 * 8 + 8, :], wsv[:, 4:8, :], r=[wsk], w=['bufV'])
            for half in range(2):
                pb, pbk = c.psm.next()
                for l in range(32):
                    P.mm(pb[:, 0:1], w1v[:, l, half * 128:(half + 1) * 128], peT_t[:, kv, l:l + 1],
                         start=(l == 0), stop=(l == 31), r=['bufV', 'peT'], w=[pbk])
                P.cp('dve', cbias[:, half:half + 1], pb[:, 0:1], r=[pbk], w=['cbias'])
            for g in range(2):
                c.load_k(2 * kv + g, kg)
                for half in range(2):
                    pa, pak = c.stp.next()
                    for l in range(32):
                        P.mm(pa[:, 0:n_cmp], w1v[:, l, half * 128:(half + 1) * 128],
                             c.bufK[:, l:l + 16 * (n_cmp - 1) + 1:16], start=(l == 0), stop=(l == 31),
                             r=['bufV', 'bufK'], w=[pak])
                    P.act(hid[:, half, 0:n_cmp], pa[:, 0:n_cmp], AF.Silu, r=[pak, 'cbias'], w=['hid'],
                          bias=cbias[:, half:half + 1])
                if kv == 0:
                    pa, pak = c.stp.next()
                    for half in range(2):
                        P.mm(pa[:, 0:NCP], cw2_t[:, 0, half, :], hid[:, half, :], start=(half == 0), stop=(half == 1),
                             r=['cw2', 'hid'], w=[pak])
                    q_, qk = c.b1.next()
                    fm_norm(P, c, pa, pak, NCP, gv_t[:, 0:1], 'gv', q_[:, 0:NCP], [qk])
                    pc, pck = c.psm.next()
                    P.mm(pc[:, 0:NCP], c.rot128, q_[:, 0:NCP], r=[qk, 'cmat'], w=[pck])
                    t1, t1k = c.f1.next()
                    P.tt('pool', t1[:, 0:NCP], q_[:, 0:NCP], ctab_t[:, 0, :], ALU.mult, r=[qk, 'ctab'], w=[t1k])
                    t2, t2k = c.f1.next()
                    P.tt('dve', t2[:, 0:NCP], pc[:, 0:NCP], ctab_t[:, 1, :], ALU.mult, r=[pck, 'ctab'], w=[t2k])
                    P.tt('pool', KCc[:, g, :], t1[:, 0:NCP], t2[:, 0:NCP], ALU.add, r=[t1k, t2k], w=['KCc'])
                else:
                    for ct in range(NCT):
                        pa, pak = c.stp.next()
                        for half in range(2):
                            P.mm(pa[:, 0:128], hid[:, half, ct * 128:(ct + 1) * 128], cw2_t[:, 1, half, :],
                                 start=(half == 0), stop=(half == 1), r=['cw2', 'hid'], w=[pak])
                        P.act(VCx[:, g, ct, 0:128], pa[:, 0:128], AF.Copy, r=[pak], w=['VCx'])

        sc = 128 ** -0.5
        ob = c.obank
        for g in range(2):
            P.dma(c.qT[:], c.fmq[:, g * 4:(g + 1) * 4, :], w=['qT'])
            c.load_k(4 + g, kg)
            c.load_v(g, vg)
            vS = c.bufV[:, 0:NT * 129].rearrange("p (t d) -> p t d", d=129)
            for j in range(NO):
                qj = c.qT[:, :, j * 128:(j + 1) * 128]
                cmj, cmk = cmp_.next()
                P.dma(cmj[:], cm[j], w=[cmk])
                vf, vfk = vfp.next()
                P.dma(vf[:], visfrc[j], w=[vfk])
                agt, agk = agp.next()
                P.dma(agt[:], ag[j * 128:(j + 1) * 128, :], w=[agk])
                a_, ak = acc.next()
                steps = []
                for ct in range(NCT):
                    masks = [(h * 128, 128, c.ident, cmj[:, ct, :], ['cmat', cmk]) for h in range(4)]
                    pv = [(h * 128, ob[h][:, 0:257], ('ob', h), VCx[:, g, ct, :], ['VCx'], ct == 0, ct == NCT - 1)
                          for h in range(4)]
                    steps.append(dict(k=KCc[:, g, ct * 128:(ct + 1) * 128], kkeys=['KCc'], q=qj, qkeys=['qT'],
                                      c0=0, n=512, masks=masks, pv=pv))
                attend(P, c, steps, sc)
                im, imk = imp.next()
                for h in range(4):
                    s_, sk = c.sm.next()
                    P.ts('dve', s_[:, 0:1], ob[h][:, 256:257], 1e-30, None, ALU.max, r=[('ob', h)], w=[sk])
                    P.op('dve', lambda e, a=s_: e.reciprocal(a[:, 0:1], a[:, 0:1]), r=[sk], w=[sk])
                    P.tt('dve', s_[:, 1:2], s_[:, 0:1], agt[:, (g * 4 + h) * 3:(g * 4 + h) * 3 + 1], ALU.mult,
                         r=[sk, agk], w=[sk])
                    P.ts('dve', a_[:, h * 128:(h + 1) * 128], ob[h][:, 0:128], s_[:, 1:2], None, ALU.mult,
                         r=[('ob', h), sk], w=[ak])
                    if h == 0:
                        P.ts('dve', im[:], ob[h][:, 128:256], s_[:, 0:1], None, ALU.mult, r=[('ob', h), sk], w=[imk])
                    else:
                        P.stt('dve', im[:], ob[h][:, 128:256], s_[:, 0:1], im[:], ALU.mult, ALU.add,
                              r=[('ob', h), sk, imk], w=[imk])
                P.tt('dve', im[:], im[:], vf[:, 0, :], ALU.mult, r=[imk, vfk], w=[imk])
                P.tt('dve', im[:], im[:], vf[:, 1, :], ALU.add, r=[imk, vfk], w=[imk])
                mx, mxk = m8.next()
                P.op('dve', lambda e, a=mx, b=im: e.max(a[:, 0:8], b[:]), r=[imk], w=[mxk])
                i2, i2k = imp2.next()
                P.op('dve', lambda e, a=mx, b=im, o=i2: e.match_replace(o[:], a[:, 0:8], b[:], -3.0e38),
                     r=[imk, mxk], w=[i2k])
                P.op('dve', lambda e, a=mx, b=i2: e.max(a[:, 8:16], b[:]), r=[i2k], w=[mxk])
                sn, snk = seln.next()
                P.ts('dve', sn[:], im[:], mx[:, 15:16], NEG, ALU.is_lt, ALU.mult, r=[imk, mxk], w=[snk])
                pt, ptk = c.pst.next()
                P.tr(pt[:, 0, :], sn[:], c.ident, r=[snk, 'cmat'], w=[ptk])
                sT, sTk = selT.next()
                P.cp('dve', sT[:], pt[:, 0, :], r=[ptk], w=[sTk])
                u0 = 4 if j == 0 else 0
                kw, kwk = kwp.next()
                vw, vwk = vwp.next()
                t_lo = 4 * j - 4 + u0
                if fx is None:
                    P.dma(kw[:, u0:8, :], kg[:, 6 + g, t_lo * 128:(4 * j + 4) * 128].rearrange("p (u s) -> p u s", s=128), w=[kwk])
                    P.dma(vw[:, u0:8, :], vg[2 + g][:, t_lo * 129:(4 * j + 4) * 129].rearrange("p (u s) -> p u s", s=129), w=[vwk])
                else:
                    if j > 0:
                        P.dma(kw[:, 0:4, :], c.kall_fm_v[:, 6 + g, j - 1], w=[kwk])
                        P.dma(vw[:, 0:4, :], c.kall_v_v[:, 2 + g, j - 1], w=[vwk])
                    P.dma(kw[:, 4:8, :], c.kall_fm_v[:, 6 + g, j], w=[kwk])
                    P.dma(vw[:, 4:8, :], c.kall_v_v[:, 2 + g, j], w=[vwk])
                steps = []
                for u in range(u0, 8):
                    masks = [(h * 128, 128, c.ident, wm_t[:, u, :], ['cmat', 'wm']) for h in range(4)]
                    pv = [(h * 128, ob[h][:, 0:129], ('ob', h), vw[:, u, :], [vwk], u == u0, u == 7) for h in range(4)]
                    steps.append(dict(k=kw[:, u, :], kkeys=[kwk], q=qj, qkeys=['qT'], c0=0, n=512, masks=masks, pv=pv))
                attend(P, c, steps, sc)
                for h in range(4):
                    s_, sk = c.sm.next()
                    P.ts('dve', s_[:, 0:1], ob[h][:, 128:129], 1e-30, None, ALU.max, r=[('ob', h)], w=[sk])
                    P.op('dve', lambda e, a=s_: e.reciprocal(a[:, 0:1], a[:, 0:1]), r=[sk], w=[sk])
                    P.tt('dve', s_[:, 1:2], s_[:, 0:1], agt[:, (g * 4 + h) * 3 + 2:(g * 4 + h) * 3 + 3], ALU.mult,
                         r=[sk, agk], w=[sk])
                    P.stt('dve', a_[:, h * 128:(h + 1) * 128], ob[h][:, 0:128], s_[:, 1:2], a_[:, h * 128:(h + 1) * 128],
                          ALU.mult, ALU.add, r=[('ob', h), sk, ak], w=[ak])
                steps = []
                nk = 4 * j + 4
                for kt in range(nk):
                    hrows = slice((kt // 32) * 64, (kt // 32) * 64 + 64)
                    masks = [(h * 128, 128, expand_t[hrows, kt % 32, :], sT[hrows, :], ['expand', sTk]) for h in range(4)]
                    if kt >= 4 * j:
                        masks += [(h * 128, 128, c.ident, c.dm_t[:, kt - 4 * j, :], ['cmat', 'dm']) for h in range(4)]
                    pv = [(h * 128, ob[h][:, 0:129], ('ob', h), vS[:, kt, :], ['bufV'], kt == 0, kt == nk - 1)
                          for h in range(4)]
                    steps.append(dict(k=c.bufK[:, kt * 128:(kt + 1) * 128], kkeys=['bufK'], q=qj, qkeys=['qT'],
                                      c0=0, n=512, masks=masks, pv=pv))
                attend(P, c, steps, sc)
                zt, ztk = c.zt_for(j, g * 512, 512)
                for h in range(4):
                    s_, sk = c.sm.next()
                    P.ts('dve', s_[:, 0:1], ob[h][:, 128:129], 1e-30, None, ALU.max, r=[('ob', h)], w=[sk])
                    P.op('dve', lambda e, a=s_: e.reciprocal(a[:, 0:1], a[:, 0:1]), r=[sk], w=[sk])
                    P.tt('dve', s_[:, 1:2], s_[:, 0:1], agt[:, (g * 4 + h) * 3 + 1:(g * 4 + h) * 3 + 2], ALU.mult,
                         r=[sk, agk], w=[sk])
                    P.stt('dve', a_[:, h * 128:(h + 1) * 128], ob[h][:, 0:128], s_[:, 1:2], a_[:, h * 128:(h + 1) * 128],
                          ALU.mult, ALU.add, r=[('ob', h), sk, ak], w=[ak])
                P.tt('dve', c.y_all[:, j, g * 512:(g + 1) * 512], a_[:], zt[:], ALU.mult, r=[ak, ztk], w=[('y', j)])

        sc64 = 64 ** -0.5
        for h in range(4):
            P.dma(c.qT[:, 0, :], c.fmq[:, 8 + h, :], w=['qT'])
            c.load_k(8 + h, kg)
            c.load_v(4 + h, vg)
            vS = c.bufV[:, 0:NT * 129].rearrange("p (t d) -> p t d", d=129)
            for ch in range(NC):
                j0 = 4 * ch
                dts = []
                for m in range(2):
                    rows = slice(m * 64, (m + 1) * 64)
                    steps = []
                    nk = 4 * j0 + 16
                    for kt in range(nk):
                        a = 0 if kt < 4 * j0 else (kt - 4 * j0) // 4
                        masks = []
                        if kt >= 4 * j0:
                            u = kt - 4 * (j0 + a)
                            masks = [(a * 128, 128, c.ident, c.dm_t[:, u, :], ['cmat', 'dm'])]
                        pv = []
                        for i in range(a, 4):
                            last_kt = 4 * (j0 + i) + 3
                            pv.append((i * 128, ob[i][:, 0:129], ('ob', i), vS[:, kt, :], ['bufV'], kt == 0, kt == last_kt))
                        steps.append(dict(k=c.bufK[rows, kt * 128:(kt + 1) * 128], kkeys=['bufK'],
                                          q=c.qT[rows, 0, ch * 512 + a * 128:(ch + 1) * 512], qkeys=['qT'],
                                          c0=a * 128, n=512 - a * 128, masks=masks, pv=pv))
                    attend(P, c, steps, sc64)
                    d_, dk = dtmp.next()
                    dts.append((d_, dk))
                    for i in range(4):
                        s_, sk = c.sm.next()
                        P.ts('dve', s_[:, 0:1], ob[i][:, 128:129], 1e-30, None, ALU.max, r=[('ob', i)], w=[sk])
                        P.op('dve', lambda e, a=s_: e.reciprocal(a[:, 0:1], a[:, 0:1]), r=[sk], w=[sk])
                        if m == 1:
                            P.tt('dve', s_[:, 0:1], s_[:, 0:1], lam[:, 3:4], ALU.mult, r=[sk, 'lam'], w=[sk])
                        P.ts('dve', d_[:, i, :], ob[i][:, 0:128], s_[:, 0:1], None, ALU.mult, r=[('ob', i), sk], w=[dk])
                (d0, d0k), (d1, d1k) = dts
                P.tt('pool', d0[:], d0[:], d1[:], ALU.add, r=[d0k, d1k], w=[d0k])
                for i in range(4):
                    j = j0 + i
                    s_, sk = c.sm.next()
                    jb, jbk = c.b1.next()
                    P.act(jb[:, 0:128], d0[:, i, :], AF.Square, r=[d0k], w=[jbk, sk], accum_out=s_[:, 0:1])
                    P.act(s_[:, 1:2], s_[:, 0:1], AF.Sqrt, r=[sk, 'eps'], w=[sk], scale=1.0 / 128, bias=c.eps_t[:])
                    P.op('dve', lambda e, a=s_: e.reciprocal(a[:, 1:2], a[:, 1:2]), r=[sk], w=[sk])
                    zt, ztk = c.zt_for(j, 1024 + h * 128, 128)
                    t1, t1k = c.f1.next()
                    P.stt('dve', t1[:, 0:128], d0[:, i, :], s_[:, 1:2], gv_t[:, 2:130], ALU.mult, ALU.mult,
                          r=[d0k, sk, 'gv'], w=[t1k])
                    P.stt('dve', c.y_all[:, j, 1024 + h * 128:1024 + (h + 1) * 128], t1[:, 0:128], gv_t[:, 387:388],
                          zt[:, 0:128], ALU.mult, ALU.mult, r=[t1k, ztk, 'gv'], w=[('y', j)])

        b_mem_attention(P, c, 12, 1536, 1536)
        b_out_proj(P, c)
        if fx is None:
            P.emit(es)
    return nc


def mask_consts(S, r):
    NT, NO, NOW, NC, n_cmp, NCT, n_sel = geom(S)
    s = np.arange(128)[:, None]
    t = np.arange(128)[None, :]
    dm = np.zeros((128, 4, 128), np.float32)
    for u in range(4):
        if u == r:
            dm[:, u, :] = np.where(s <= t, 0.0, NEG)
        elif u > r:
            dm[:, u, :] = NEG
    wm = np.zeros((128, 8, 128), np.float32)
    for u in range(8):
        delta = (r + 4 - u) * 128 + t - s
        wm[:, u, :] = np.where((delta >= 0) & (delta < 512), 0.0, NEG)
    cm = np.zeros((NO, 128, NCT, 128), np.float32)
    vf = np.zeros((NO, 128, 2, 128), np.float32)
    jj = np.arange(128)[None, :]
    for j in range(NO):
        tg = (4 * j + r) * 128 + np.arange(128)
        for ct in range(NCT):
            cidx = ct * 128 + np.arange(128)
            vis = (cidx[:, None] < n_cmp) & ((16 * cidx[:, None] + 31) <= tg[None, :])
            cm[j, :, ct, :] = np.where(vis, 0.0, NEG)
        cur = (tg // 64)[:, None]
        visible = (jj <= cur) & (jj < n_sel)
        forced = (jj == 0) | (jj >= cur - 1)
        vf[j, :, 0, :] = np.where(visible & ~forced, 1.0, 0.0)
        vf[j, :, 1, :] = np.where(~visible, -1e30, np.where(forced, 1000.0 + jj, 0.0))
    return dm.astype(NPBF), wm.astype(NPBF), cm.astype(NPBF), vf


def even_consts(S):
    NT, NO, NOW, NC, n_cmp, NCT, n_sel = geom(S)
    NCP = NCT * 128
    cpos = 16 * np.arange(NCP) + 31
    cc, cs = rope_tables_fm(cpos, 128)
    ctab = np.stack([cc, cs], axis=1)
    cmp_start = np.arange(n_cmp) * 16
    sel_start = np.arange(n_sel) * 64
    ov = np.clip(np.minimum(cmp_start[:, None] + 32, sel_start[None, :] + 64)
                 - np.maximum(cmp_start[:, None], sel_start[None, :]), 0, None) / 32.0
    ovp = np.zeros((NCP, 128), np.float32)
    ovp[:n_cmp, :n_sel] = ov
    ovm = ovp.reshape(NCT, 128, 128).transpose(1, 0, 2).copy().astype(NPBF)
    expand = np.zeros((128, 32, 128), np.float32)
    for k in range(32):
        for s in range(128):
            expand[2 * k + s // 64, k, s] = 1.0
            expand[64 + 2 * k + s // 64, k, s] = 1.0
    return ctab, ovm, expand.astype(NPBF)


def fm_gather(fmos, slots, S):
    NT = S // 128
    NO = NT // 4
    out = np.zeros((128, len(slots), NT, 128), dtype=fmos[0].dtype)
    for r in range(4):
        v = fmos[r][:, slots, :].reshape(128, len(slots), NO, 128)
        out[:, :, r::4, :] = v
    return out.reshape(128, len(slots), S)


def tm_gather_vext(tmos, col0, S):
    NT = S // 128
    NO = NT // 4
    out = np.ones((128, NT, 129), dtype=tmos[0].dtype)
    for r in range(4):
        v = tmos[r][:, col0:col0 + 128].reshape(NO, 128, 128)
        out[:, r::4, 0:128] = v.transpose(1, 0, 2)
    return out.reshape(128, NT * 129)


def memg_host(mem_norm_gain, mem_k_gain):
    m = np.zeros((128, KC + 1), np.float32)
    m[:, :KC] = mem_norm_gain.reshape(KC, 128).T
    m[:, KC] = mem_k_gain
    return m


def a_inputs(parity, S, x_own, w_in, norm_gain, gain_cols, r):
    m = dict(x=np.ascontiguousarray(x_own), w_in=w_in, ng=np.ascontiguousarray(norm_gain.reshape(KC, 128).T),
             gains=np.ascontiguousarray(np.stack(gain_cols, axis=1).astype(np.float32)),
             tabs=tabs_host(S, r), cmat=cmat_host())
    if parity == 1:
        sel = np.zeros((16, 8, 128), np.float32)
        for c in range(8):
            for mm_ in range(128):
                sel[2 * c + mm_ // 64, c, mm_] = 1
        m['sel'] = sel.astype(NPBF)
    return m


def b_even_inputs(S, r, x_own, fmo_b, tmo_b, tmf_own, w_out, mem_b, mem_norm_gain, w_kv, mem_qk_gain,
                  nsa_qk_gain, cmp_pos, cmp_w1, cmp_w2, subln, lam_vecs, shared, lambda_init):
    fmo = fmo_b[r]
    tmo = tmo_b[r]
    dm, wm, cm, vf = mask_consts(S, r)
    gv = np.zeros((128, 388), np.float32)
    gv[:, 0] = nsa_qk_gain[1]
    gv[:, 2:130] = subln[None, :]
    gv[:, 130:386] = lam_vecs.reshape(1, 256)
    gv[:, 386] = lambda_init
    gv[:, 387] = 1.0 - lambda_init
    m = dict(x=np.ascontiguousarray(x_own), w_out=w_out, mem=mem_b, w_kv=w_kv,
             memg=memg_host(mem_norm_gain, mem_qk_gain[1]), cmat=cmat_host(), dm=dm,
             fmq=np.ascontiguousarray(np.concatenate([fmo[:, 0:8], fmo[:, 16:20], fmo[:, 24:28]], axis=1)),
             tmz=np.ascontiguousarray(np.concatenate([tmo[:, 512:1536], tmo[:, 2048:2560], tmo[:, 2560:3072]], axis=1)),
             kg=shared['kg'], vg=shared['vg'], ag=np.ascontiguousarray(tmf_own[:, :24]),
             cw1=cmp_w1, cw2=np.ascontiguousarray(cmp_w2.reshape(2, 2, 128, 128).transpose(2, 0, 1, 3)),
             peT=np.ascontiguousarray(cmp_pos.transpose(2, 0, 1)).astype(NPBF),
             ctab=shared['ctab'], ovm=shared['ovm'], expand=shared['expand'], wm=wm, cm=cm, visfrc=vf, gv=gv)
    return m


def b_even_shared(S, fmo_b, tmo_b):
    ctab, ovm, expand = even_consts(S)
    kg = fm_gather(fmo_b, [8, 9, 10, 11, 12, 13, 14, 15, 20, 21, 22, 23], S)
    vcols = [0, 128, 256, 384, 1536, 1664, 1792, 1920]
    vg = np.stack([tm_gather_vext(tmo_b, c0, S) for c0 in vcols], axis=0)
    return dict(kg=kg, vg=vg, ctab=ctab, ovm=ovm, expand=expand)


def build_B_odd(S, fx=None):
    NT, NO, NOW, NC, n_cmp, NCT, n_sel = geom(S)
    TOPK = min(256, S // 4)
    NIT = 16
    nc = bass.Bass("TRN2", target_bir_lowering=False) if fx is None else fx.nc
    with ExitStack() as es:
        P = Prog(nc) if fx is None else fx.P
        c = b_common(nc, es, P, S, 1, fx)
        if fx is None:
            kg = c.ext("kg", [128, 5, S], BF16)
            vg = c.ext("vg", [4, 128, NT * 129], BF16)
            sg = c.ext("sg", [NOW, 16], F32)
            mscr = nc.dram_tensor("mscr", [NO, 128, S], BF16, kind="Internal").ap()
        else:
            kg = vg = None
            sg = c.tmf
            mscr = fx.internal("mscr", [NO, 128, S], BF16)
        dmt = c.ext("dmt", [128, 512], F32)

        IKT = c.qT[:].rearrange("p a b -> p (a b)")[:, 0:S]
        dmt_t = sb(nc, es, "dmt_t", [128, 512], F32)
        mnp = Pool(nc, es, "mnp", 2, [128, S], BF16)
        iqp = Pool(nc, es, "iqp", 2, [128, 8, 128], BF16)
        sgp = Pool(nc, es, "sgp", 2, [128, 16], F32)
        rp = Pool(nc, es, "rp", 7, [128, 512], BF16)
        dgp = Pool(nc, es, "dgp", 2, [128, 16, 128], BF16)
        thr = Pool(nc, es, "thr", 2, [128, 8], F32)
        scr_bufs = [c.big[:, 0:2 * S].bitcast(F32)]
        scr_guard = [['bufK', 'bufV']]
        if NO >= 2:
            scr_bufs.append(c.y_all[:, 0:NO // 2, :].rearrange("p a b -> p (a b)")[:, 0:2 * S].bitcast(F32))
            scr_guard.append([('y', j_) for j_ in range(NO // 2)])

        P.dma(dmt_t[:], dmt, w=['dmt'])
        b_memory_kv(P, c)
        if fx is None:
            P.dma(IKT, kg[:, 4, :], w=['qT'])
        else:
            dv_ = IKT.rearrange("d (j r p) -> d j r p", r=4, p=128)
            for r_ in range(4):
                P.dma(dv_[:, :, r_, :], c.kall_fm_v[:, 4, :, r_, :], w=['qT'])
        for sb_, sg_ in zip(scr_bufs, scr_guard):
            P.memset('dve', sb_[:, 0:1], 0.0, w=sg_)

        psl = [(c.stp.t[0], ('stp', 0)), (c.stp.t[1], ('stp', 1)), (c.obank[0], ('ob', 0)), (c.obank[1], ('ob', 1)), (c.psm.t[0], ('psm', 0))]
        accl = [(c.obank[2], ('ob', 2)), (c.obank[3], ('ob', 3))]
        psi = [0]
        acci = [0]
        for j in range(NO):
            L = (4 * j + 4) * 128
            sbi = j % len(scr_bufs)
            scr_t = scr_bufs[sbi]
            BB = scr_guard[sbi]
            iq, iqk = iqp.next()
            P.dma(iq[:], c.fmq[:, 12:20, j * 128:(j + 1) * 128], w=[iqk])
            sgt, sgk = sgp.next()
            P.dma(sgt[:], sg[j * 128:(j + 1) * 128, :], w=[sgk])
            dg, dgk = dgp.next()
            for hh in range(16):
                P.ts('pool', dg[:, hh, :], c.ident, sgt[:, hh:hh + 1], None, ALU.mult,
                     r=['cmat', sgk], w=[(dgk, hh)])
            pend = []

            def do_item(item):
                hh_, R__, Rk_, acc_ps_, acck_, cols_, kc5_ = item
                P.mm(acc_ps_[:], dg[:, hh_, :], R__[:], start=(hh_ == 0), stop=(hh_ == 15),
                     r=[(dgk, hh_), Rk_], w=[acck_])
                if hh_ == 15:
                    P.act(scr_t[:, cols_], acc_ps_[:], AF.Copy, r=[acck_] + BB, w=[('scr', sbi, kc5_)])
            for kc5 in range(L // 512):
                cols = slice(kc5 * 512, (kc5 + 1) * 512)
                acc_ps, acck = accl[acci[0] % 2]
                acci[0] += 1
                for hh in range(16):
                    rows = slice((hh % 2) * 64, (hh % 2) * 64 + 64)
                    ps, psk = psl[psi[0] % len(psl)]
                    psi[0] += 1
                    P.mm(ps[:], iq[rows, hh // 2, :], IKT[rows, cols], r=[iqk, 'qT'], w=[psk])
                    R_, Rk = rp.next()
                    P.act(R_[:], ps[:], AF.Relu, r=[psk], w=[Rk])
                    pend.append((hh, R_, Rk, acc_ps, acck, cols, kc5))
                    if len(pend) > 4:
                        do_item(pend.pop(0))
            while pend:
                do_item(pend.pop(0))
            last = L // 512 - 1
            P.tt('dve', scr_t[:, L - 512:L], scr_t[:, L - 512:L], dmt_t[:], ALU.add, r=[('scr', sbi, last), 'dmt'] + BB,
                 w=[('scr', sbi, last)])
            skeys = [('scr', sbi, k) for k in range(L // 512)] + BB
            th, thk = thr.next()
            mn, mnk = mnp.next()
            P.op('dve', lambda e, a=th, LL=L, sc_=scr_t: e.reduce_max(a[:, 0:1], sc_[:, 0:LL], AX.X), r=skeys, w=[thk])
            P.ts('dve', th[:, 1:2], th[:, 0:1], -16.0, None, ALU.add, r=[thk], w=[thk])
            for k in range(NIT):
                ck = 8.0 / (2 ** k)
                P.ts('dve', th[:, 2:3], th[:, 1:2], ck, None, ALU.add, r=[thk], w=[thk])
                P.op('dve', lambda e, a=th, m_=mn, LL=L, sc_=scr_t: e.tensor_scalar(m_[:, 0:LL], sc_[:, 0:LL], a[:, 2:3], None,
                                                                      ALU.is_ge, ALU.add, accum_out=a[:, 3:4]),
                     r=skeys + [thk], w=[thk, mnk])
                P.ts('dve', th[:, 4:5], th[:, 3:4], float(TOPK) - 0.5, ck, ALU.is_ge, ALU.mult, r=[thk], w=[thk])
                P.tt('dve', th[:, 1:2], th[:, 1:2], th[:, 4:5], ALU.add, r=[thk], w=[thk])
            P.ts('dve', mn[:, 0:L], scr_t[:, 0:L], th[:, 1:2], NEG, ALU.is_lt, ALU.mult, r=skeys + [thk], w=[mnk])
            P.dma(mscr[j][:, 0:L], mn[:, 0:L], r=[mnk], w=[('mscr', j)])

        sc = 128 ** -0.5
        ob = c.obank
        for g in range(4):
            P.dma(c.qT[:, 0:3, :], c.fmq[:, 3 * g:3 * g + 3, :], w=['qT'])
            c.load_k(g, kg)
            c.load_v(g, vg)
            vS = c.bufV[:, 0:NT * 129].rearrange("p (t d) -> p t d", d=129)
            for j in range(NO):
                L = (4 * j + 4) * 128
                mn, mnk = mnp.next()
                P.dma(mn[:, 0:L], mscr[j][:, 0:L], r=[('mscr', j)], w=[mnk])
                qj = c.qT[:, 0:3, j * 128:(j + 1) * 128]
                steps = []
                nk = 4 * j + 4
                for kt in range(nk):
                    masks = [(h * 128, 128, mn[:, kt * 128:(kt + 1) * 128], c.ident, [mnk, 'cmat']) for h in range(3)]
                    pv = [(h * 128, ob[h][:, 0:129], ('ob', h), vS[:, kt, :], ['bufV'], kt == 0, kt == nk - 1)
                          for h in range(3)]
                    steps.append(dict(k=c.bufK[:, kt * 128:(kt + 1) * 128], kkeys=['bufK'], q=qj, qkeys=['qT'],
                                      c0=0, n=384, masks=masks, pv=pv))
                attend(P, c, steps, sc)
                zt, ztk = c.zt_for(j, g * 384, 384)
                for h in range(3):
                    s_, sk = c.sm.next()
                    P.ts('dve', s_[:, 0:1], ob[h][:, 128:129], 1e-30, None, ALU.max, r=[('ob', h)], w=[sk])
                    P.op('dve', lambda e, a=s_: e.reciprocal(a[:, 0:1], a[:, 0:1]), r=[sk], w=[sk])
                    col = (g * 3 + h) * 128
                    P.stt('dve', c.y_all[:, j, col:col + 128], ob[h][:, 0:128], s_[:, 0:1], zt[:, h * 128:(h + 1) * 128],
                          ALU.mult, ALU.mult, r=[('ob', h), sk, ztk], w=[('y', j)])

        b_mem_attention(P, c, 20, 1536, 1536)
        b_out_proj(P, c)
        if fx is None:
            P.emit(es)
    return nc


def b_odd_shared(S, fmo_b, tmo_b):
    kg = fm_gather(fmo_b, [12, 13, 14, 15, 24], S)
    vg = np.stack([tm_gather_vext(tmo_b, h * 128, S) for h in range(4)], axis=0)
    return dict(kg=kg, vg=vg)


def b_odd_inputs(S, r, x_own, fmo_b, tmo_b, tmf_own, w_out, mem_b, mem_norm_gain, w_kv, mem_qk_gain, shared):
    fmo = fmo_b[r]
    tmo = tmo_b[r]
    dm, wm, cm, vf = mask_consts(S, r)
    t = np.arange(128)[:, None]
    dmt = np.zeros((128, 4, 128), np.float32)
    s = np.arange(128)[None, :]
    for u in range(4):
        if u == r:
            dmt[:, u, :] = np.where(s <= t, 0.0, -1e30)
        elif u > r:
            dmt[:, u, :] = -1e30
    m = dict(x=np.ascontiguousarray(x_own), w_out=w_out, mem=mem_b, w_kv=w_kv,
             memg=memg_host(mem_norm_gain, mem_qk_gain[1]), cmat=cmat_host(), dm=dm,
             fmq=np.ascontiguousarray(np.concatenate([fmo[:, 0:12], fmo[:, 16:24], fmo[:, 25:29]], axis=1)),
             tmz=np.ascontiguousarray(tmo[:, 512:2560]),
             kg=shared['kg'], vg=shared['vg'], sg=np.ascontiguousarray(tmf_own[:, :16]),
             dmt=dmt.reshape(128, 512))
    return m


_PROGS = {}


def _prog(name, S):
    key = (name, S)
    if key not in _PROGS:
        if name == 'A0':
            _PROGS[key] = build_A(0, S)
        elif name == 'A1':
            _PROGS[key] = build_A(1, S)
        elif name == 'B0':
            _PROGS[key] = build_B_even(S)
        else:
            _PROGS[key] = build_B_odd(S)
    return _PROGS[key]


def run_layer(S, layer, xb, inp):
    parity = layer % 2
    e = layer // 2
    f32 = lambda a: np.ascontiguousarray(np.asarray(a, dtype=np.float32))
    mg = f32(inp['mem_qk_gain'][layer])
    if parity == 0:
        g = f32(inp['nsa_qk_gain'][e])
        dg = f32(inp['diff_qk_gain'][e])
        gcols = [g[0], g[2], g[3], np.tile(dg[0], 2), np.tile(dg[1], 2), mg[0]]
        w_in = f32(inp['even_w_in'][e])
    else:
        g = f32(inp['dsa_qk_gain'][e])
        gcols = [g[0], g[1], mg[0]]
        w_in = f32(inp['odd_w_in'][e])
    ng = f32(inp['norm_gain'][layer])
    owns = [[np.ascontiguousarray(xb[b][own_positions(S, r)]) for r in range(4)] for b in range(2)]
    maps = [a_inputs(parity, S, owns[cc // 4][cc % 4], w_in, ng, gcols, cc % 4) for cc in range(8)]
    resA = run_bass_kernel_spmd(_prog('A%d' % parity, S), maps, core_ids=list(range(8))).results
    w_out = f32(inp['w_out'][layer])
    w_kv = f32(inp['mem_w_kv'][layer])
    mng = f32(inp['mem_norm_gain'])
    maps = []
    for b in range(2):
        fmo_b = [np.asarray(resA[b * 4 + r]['fmo']) for r in range(4)]
        tmo_b = [np.asarray(resA[b * 4 + r]['tmo']) for r in range(4)]
        tmf_b = [np.asarray(resA[b * 4 + r]['tmf']) for r in range(4)]
        mem_b = f32(inp['mem'][b])
        if parity == 0:
            shared = b_even_shared(S, fmo_b, tmo_b)
            lambda_init = 0.8 - 0.6 * math.exp(-0.3 * layer)
            for r in range(4):
                maps.append(b_even_inputs(S, r, owns[b][r], fmo_b, tmo_b, tmf_b[r], w_out, mem_b, mng, w_kv, mg,
                                          f32(inp['nsa_qk_gain'][e]), f32(inp['nsa_cmp_pos'][e]), f32(inp['nsa_cmp_w1'][e]),
                                          f32(inp['nsa_cmp_w2'][e]), f32(inp['diff_subln_gain'][e]),
                                          f32(inp['diff_lambda'][e]), shared, lambda_init))
        else:
            shared = b_odd_shared(S, fmo_b, tmo_b)
            for r in range(4):
                maps.append(b_odd_inputs(S, r, owns[b][r], fmo_b, tmo_b, tmf_b[r], w_out, mem_b, mng, w_kv, mg, shared))
    resB = run_bass_kernel_spmd(_prog('B%d' % parity, S), maps, core_ids=list(range(8))).results
    out = []
    for b in range(2):
        xn = np.empty((S, D), np.float32)
        for r in range(4):
            xn[own_positions(S, r)] = np.asarray(resB[b * 4 + r]['xo'])
        out.append(xn)
    return out


def kernel_unfused(**inputs):
    x = np.asarray(inputs['x'], dtype=np.float32)
    S = x.shape[1]
    xb = [np.ascontiguousarray(x[b]) for b in range(x.shape[0])]
    for layer in range(4):
        xb = run_layer(S, layer, xb, inputs)
    return np.stack(xb, axis=0).astype(np.float32)


def kernel(**inputs):
    return kernel_fused(inputs, 4).astype(np.float32)


def build_fused(S, depth=4):
    NT, NO, NOW, NC, n_cmp, NCT, n_sel = geom(S)
    nc = bass.Bass("TRN2", target_bir_lowering=False)
    P = Prog(nc)
    fx = FX(nc, P, S)
    x_in = nc.dram_tensor("x", [NOW, D], F32, kind="ExternalInput").ap()
    out = nc.dram_tensor("out", [NOW, D], F32, kind="ExternalOutput").ap()
    xs = [x_in] + [nc.dram_tensor(f"xs{l}", [NOW, D], F32, kind="Internal").ap() for l in range(1, depth)] + [out]
    groups = [[0, 1, 2, 3], [4, 5, 6, 7]]
    ccsrc = nc.dram_tensor("ccsrc", [256, 2048], BF16, kind="Internal").ap()
    ccdst = nc.dram_tensor("ccdst", [1024, 2048], BF16, kind="Internal").ap()
    for layer in range(depth):
        parity = layer % 2
        fx.layer = layer
        fx.x_src = xs[layer]
        fx.x_dst = xs[layer + 1]
        build_A(parity, S, fx)
        P.barrier()
        kown_fm = fx.internal("kown_fm", [NK[parity] * 128, NOW], BF16)
        kall_fm = fx.internal("kall_fm", [4 * NK[parity] * 128, NOW], BF16)
        kown_v = fx.internal("kown_v", [NV[parity] * 128, NO * 129], BF16)
        kall_v = fx.internal("kall_v", [4 * NV[parity] * 128, NO * 129], BF16)
        def gather(src_rows, dst_view, R_, C_):
            sv = ccsrc.rearrange("a b -> (a b)")[0:R_ * C_].rearrange("(r c) -> r c", c=C_)
            dv = ccdst.rearrange("a b -> (a b)")[0:4 * R_ * C_].rearrange("(r c) -> r c", c=C_)
            P.dma(sv, src_rows, r=['ccdst'], w=['ccsrc'])
            P.op('pool', lambda e, a=sv, b=dv: e.collective_compute(
                "AllGather", ALU.bypass, replica_groups=groups, ins=[a], outs=[b]), r=['ccsrc'], w=['ccdst'])
            P.dma(dst_view, dv.rearrange("(k r) c -> k r c", k=4), r=['ccdst'], w=[('kall', layer)])
        nk_, nv_ = NK[parity], NV[parity]
        kall_fm_k = kall_fm.rearrange("(k sd) n -> k sd n", k=4)
        for s0 in range(0, nk_, 2):
            ns = min(2, nk_ - s0)
            gather(kown_fm[s0 * 128:(s0 + ns) * 128, :], kall_fm_k[:, s0 * 128:(s0 + ns) * 128, :], ns * 128, NOW)
        kall_v_k = kall_v.rearrange("(k up) n -> k up n", k=4)
        for u in range(nv_):
            gather(kown_v[u * 128:(u + 1) * 128, :], kall_v_k[:, u * 128:(u + 1) * 128, :], 128, NO * 129)
        P.barrier()
        if parity == 0:
            build_B_even(S, fx)
        else:
            build_B_odd(S, fx)
        P.barrier()
    with ExitStack() as es:
        P.emit(es)
    return nc, fx


def fused_inputs(S, inp, depth=4):
    f32 = lambda a: np.ascontiguousarray(np.asarray(a, dtype=np.float32))
    x = f32(inp['x'])
    ctab, ovm, expand = even_consts(S)
    sel = np.zeros((16, 8, 128), np.float32)
    for cc_ in range(8):
        for mm_ in range(128):
            sel[2 * cc_ + mm_ // 64, cc_, mm_] = 1
    shared = dict(cmat=cmat_host(), ctab=ctab, ovm=ovm, expand=expand, sel=sel.astype(NPBF))
    mng = f32(inp['mem_norm_gain'])
    for layer in range(depth):
        e = layer // 2
        L = f"_L{layer}"
        mg = f32(inp['mem_qk_gain'][layer])
        shared['w_out' + L] = f32(inp['w_out'][layer])
        shared['w_kv' + L] = f32(inp['mem_w_kv'][layer])
        shared['memg' + L] = memg_host(mng, mg[1])
        shared['ng' + L] = np.ascontiguousarray(f32(inp['norm_gain'][layer]).reshape(KC, 128).T)
        if layer % 2 == 0:
            g = f32(inp['nsa_qk_gain'][e])
            dg = f32(inp['diff_qk_gain'][e])
            gcols = [g[0], g[2], g[3], np.tile(dg[0], 2), np.tile(dg[1], 2), mg[0]]
            shared['w_in' + L] = f32(inp['even_w_in'][e])
            lambda_init = 0.8 - 0.6 * math.exp(-0.3 * layer)
            gv = np.zeros((128, 388), np.float32)
            gv[:, 0] = g[1]
            gv[:, 2:130] = f32(inp['diff_subln_gain'][e])[None, :]
            gv[:, 130:386] = f32(inp['diff_lambda'][e]).reshape(1, 256)
            gv[:, 386] = lambda_init
            gv[:, 387] = 1.0 - lambda_init
            shared['gv' + L] = gv
            shared['cw1' + L] = f32(inp['nsa_cmp_w1'][e])
            shared['cw2' + L] = np.ascontiguousarray(f32(inp['nsa_cmp_w2'][e]).reshape(2, 2, 128, 128).transpose(2, 0, 1, 3))
            shared['peT' + L] = np.ascontiguousarray(f32(inp['nsa_cmp_pos'][e]).transpose(2, 0, 1)).astype(NPBF)
        else:
            g = f32(inp['dsa_qk_gain'][e])
            gcols = [g[0], g[1], mg[0]]
            shared['w_in' + L] = f32(inp['odd_w_in'][e])
        shared['gains' + L] = np.ascontiguousarray(np.stack(gcols, axis=1).astype(np.float32))
    maps = []
    for core in range(8):
        b, r = core // 4, core % 4
        m = dict(shared)
        m['x'] = np.ascontiguousarray(x[b][own_positions(S, r)])
        m['mem'] = f32(inp['mem'][b])
        m['tabs'] = tabs_host(S, r)
        dm, wm, cm, vf = mask_consts(S, r)
        m.update(dm=dm, wm=wm, cm=cm, visfrc=vf)
        t = np.arange(128)[:, None]
        s_ = np.arange(128)[None, :]
        dmt = np.zeros((128, 4, 128), np.float32)
        for u in range(4):
            if u == r:
                dmt[:, u, :] = np.where(s_ <= t, 0.0, -1e30)
            elif u > r:
                dmt[:, u, :] = -1e30
        m['dmt'] = dmt.reshape(128, 512)
        maps.append(m)
    return maps


_FUSED = {}


def kernel_fused(inputs, depth=4):
    x = np.asarray(inputs['x'], dtype=np.float32)
    S = x.shape[1]
    import time as _t
    t0_ = _t.time()
    key = (S, depth)
    if key not in _FUSED:
        _FUSED[key] = build_fused(S, depth)
    nc, fx = _FUSED[key]
    print("[fused] build", round(_t.time() - t0_, 1), flush=True)
    maps = fused_inputs(S, inputs, depth)
    print("[fused] inputs", round(_t.time() - t0_, 1), flush=True)
    names = set(fx.cache.keys())
    maps = [{k: v for k, v in m.items() if k == 'x' or (k in fx.ext_shapes)} for m in maps]
    res = run_bass_kernel_spmd(nc, maps, core_ids=list(range(8))).results
    print("[fused] run", round(_t.time() - t0_, 1), flush=True)
    out = np.empty_like(x)
    for core in range(8):
        b, r = core // 4, core % 4
        out[b][own_positions(S, r)] = np.asarray(res[core]['out'])
    return out
```

```python
import math
from contextlib import ExitStack
import numpy as np
import ml_dtypes
import concourse.bass as bass
import concourse.mybir as mybir
from concourse.bass_utils import run_bass_kernel_spmd

F32 = mybir.dt.float32
BF16 = mybir.dt.bfloat16
ALU = mybir.AluOpType
AF = mybir.ActivationFunctionType
AX = mybir.AxisListType
NPBF = ml_dtypes.bfloat16

D = 2048
KC = D // 128
EPS = 1e-6
NEG = -30000.0
ROPE_THETA = 10000.0

EVEN_SPLITS = (1024, 1536, 24, 1024, 512, 512, 512, 512, 512, 512)
ODD_SPLITS = (1536, 512, 512, 1024, 64, 16, 1536, 512, 512)


class Prog:
    def __init__(self, nc):
        self.nc = nc
        self.ins = []
        self.last_w = {}
        self.readers = {}
        self.pending = {}
        self.last_on = {}
        self.recent_dma = []
        self.recent_cc = []

    def barrier(self):
        deps = set(self.last_on.values()) | set(self.recent_dma[-12:]) | set(self.recent_cc[-4:])
        for e in ('pe', 'act', 'dve', 'pool', 'sp'):
            self.pending[e] = set(deps) | self.pending.get(e, set())

    def op(self, eng, fn, r=(), w=(), cc=False):
        i = len(self.ins)
        deps = set()
        if eng in self.pending:
            deps |= self.pending.pop(eng)
        self.last_on[eng] = i
        if eng == 'sp':
            self.recent_dma.append(i)
        if cc:
            self.recent_cc.append(i)
        for k in r:
            if k in self.last_w:
                deps.add(self.last_w[k])
        for k in w:
            if k in self.last_w:
                deps.add(self.last_w[k])
            deps.update(self.readers.get(k, ()))
        self.ins.append([eng, fn, deps, cc])
        for k in r:
            self.readers.setdefault(k, []).append(i)
        for k in w:
            self.last_w[k] = i
            self.readers[k] = []
        return i

    def mm(self, out, lhsT, rhs, start=True, stop=True, r=(), w=()):
        return self.op('pe', lambda e: e.matmul(out, lhsT, rhs, start=start, stop=stop), r, w)

    def tr(self, out, in_, ident, r=(), w=()):
        return self.op('pe', lambda e: e.transpose(out, in_, ident), r, w)

    def act(self, out, in_, func, r=(), w=(), **kw):
        return self.op('act', lambda e: e.activation(out, in_, func, **kw), r, w)

    def dma(self, out, in_, r=(), w=(), q='sp'):
        return self.op(q, lambda e: e.dma_start(out=out, in_=in_), r, w)

    def ts(self, eng, out, in0, s1, s2, op0, op1=None, r=(), w=(), **kw):
        if op1 is None:
            return self.op(eng, lambda e: e.tensor_scalar(out, in0, s1, None, op0, **kw), r, w)
        return self.op(eng, lambda e: e.tensor_scalar(out, in0, s1, s2, op0, op1, **kw), r, w)

    def tt(self, eng, out, in0, in1, op, r=(), w=()):
        return self.op(eng, lambda e: e.tensor_tensor(out, in0, in1, op), r, w)

    def stt(self, eng, out, in0, scalar, in1, op0, op1, r=(), w=()):
        return self.op(eng, lambda e: e.scalar_tensor_tensor(out, in0, scalar, in1, op0, op1), r, w)

    def cp(self, eng, out, in_, r=(), w=()):
        return self.op(eng, lambda e: e.tensor_copy(out, in_), r, w)

    def memset(self, eng, ap, val, r=(), w=()):
        return self.op(eng, lambda e: e.memset(ap, val), r, w)

    def emit(self, es):
        nc = self.nc
        ENGS = ['pe', 'act', 'dve', 'pool', 'sp', 'dq']
        ND = 12
        SEM_LIM = 30000
        n = len(self.ins)
        eng_of = [x[0] for x in self.ins]
        needed = [False] * n
        for i, (eng, fn, deps, cc) in enumerate(self.ins):
            if eng == 'pe':
                deps = {d for d in deps if eng_of[d] != 'pe'}
                self.ins[i][2] = deps
            for d in deps:
                needed[d] = True
        comp = [None] * n
        cnt = {e: 0 for e in ENGS}
        nsem_needed = {e: 1 for e in ENGS}
        dma_k = {'sp': 0, 'dq': 0}
        cc_k = 0
        NCC = 4
        for i, (eng, fn, deps, cc) in enumerate(self.ins):
            if cc:
                comp[i] = ('dcc', cc_k % NCC, 16 * (cc_k // NCC + 1))
                cc_k += 1
            elif eng in ('sp', 'dq'):
                k = dma_k[eng]
                dma_k[eng] += 1
                comp[i] = ('d' + eng, k % ND, 16 * (k // ND + 1))
            elif needed[i]:
                c = cnt[eng]
                cnt[eng] += 1
                comp[i] = (eng, c // SEM_LIM, c % SEM_LIM + 1)
                nsem_needed[eng] = c // SEM_LIM + 1
        sems = {}
        for e in ('pe', 'act', 'dve', 'pool'):
            for s in range(nsem_needed[e]):
                sems[(e, s)] = es.enter_context(nc.semaphore(f"s_{e}{s}"))
        for q in ('sp', 'dq'):
            if dma_k[q]:
                for s in range(ND):
                    sems[('d' + q, s)] = es.enter_context(nc.semaphore(f"s_{q}{s}"))
        if cc_k:
            for s in range(NCC):
                sems[('dcc', s)] = es.enter_context(nc.semaphore(f"s_cc{s}"))
        order = {e: [] for e in ENGS}
        for i in range(n):
            order[eng_of[i]].append(i)
        ins = self.ins
        final_dma = {q: dict() for q in ('sp', 'dq')}
        for i in range(n):
            if eng_of[i] in ('sp', 'dq'):
                c = comp[i]
                final_dma[eng_of[i]][(c[0], c[1])] = c[2]

        def run(engname, e):
            seen = {}
            dk = 0
            for i in order[engname]:
                eng, fn, deps, cc = ins[i]
                waits = {}
                for d in deps:
                    c = comp[d]
                    key = (c[0], c[1])
                    if waits.get(key, 0) < c[2]:
                        waits[key] = c[2]
                if engname in ('sp', 'dq') or cc:
                    c = comp[i]
                    if c[2] > 16:
                        key = (c[0], c[1])
                        if waits.get(key, 0) < c[2] - 16:
                            waits[key] = c[2] - 16
                for key, v in waits.items():
                    if seen.get(key, 0) >= v:
                        continue
                    seen[key] = v
                    e.wait_ge(sems[key], v)
                inst = fn(e)
                c = comp[i]
                if c is not None:
                    inst.then_inc(sems[(c[0], c[1])], 16 if (engname in ('sp', 'dq') or cc) else 1)
            if engname in ('sp', 'dq'):
                for key, v in final_dma[engname].items():
                    if seen.get(key, 0) < v:
                        e.wait_ge(sems[key], v)

        with nc.Block() as block:
            if order['sp']:
                @block.sync
                def _(e):
                    run('sp', e)
            if order['pe']:
                @block.tensor
                def _(e):
                    run('pe', e)
            if order['act'] or order['dq']:
                @block.scalar
                def _(e):
                    run('act', e)
            if order['dve']:
                @block.vector
                def _(e):
                    run('dve', e)
            if order['pool']:
                @block.gpsimd
                def _(e):
                    run('pool', e)


_UID = [0]


def _uname(name):
    _UID[0] += 1
    return f"{name}_{_UID[0]}"


class Pool:
    def __init__(self, nc, es, name, n, shape, dtype, psum=False):
        self.name = name
        name = _uname(name)
        self.n = n
        self.i = 0
        self.t = []
        for k in range(n):
            if psum:
                self.t.append(es.enter_context(nc.psum_tensor(f"{name}{k}", shape, dtype)))
            else:
                self.t.append(es.enter_context(nc.sbuf_tensor(f"{name}{k}", shape, dtype)))

    def next(self):
        k = self.i % self.n
        self.i += 1
        return self.t[k], (self.name, k)


def sb(nc, es, name, shape, dtype):
    return es.enter_context(nc.sbuf_tensor(_uname(name), shape, dtype))


class FX:
    def __init__(self, nc, P, S):
        self.nc, self.P, self.S = nc, P, S
        self.layer = 0
        self.cache = {}
        self.ext_shapes = {}

    def ext(self, name, shape, dtype, per_layer=False):
        nm = f"{name}_L{self.layer}" if per_layer else name
        if nm not in self.cache:
            self.cache[nm] = self.nc.dram_tensor(nm, shape, dtype, kind="ExternalInput").ap()
            self.ext_shapes[nm] = (name, per_layer, self.layer)
        return self.cache[nm]

    def internal(self, name, shape, dtype):
        nm = f"{name}_L{self.layer}"
        if nm not in self.cache:
            self.cache[nm] = self.nc.dram_tensor(nm, shape, dtype, kind="Internal").ap()
        return self.cache[nm]


def fm_map(parity, slot):
    if parity == 0:
        if slot < 8:
            return ('q', slot)
        if slot < 16:
            return ('k', slot - 8)
        if slot < 20:
            return ('q', 8 + slot - 16)
        if slot < 24:
            return ('k', 8 + slot - 20)
        return ('q', 12 + slot - 24)
    if slot < 12:
        return ('q', slot)
    if slot < 16:
        return ('k', slot - 12)
    if slot < 24:
        return ('q', 12 + slot - 16)
    if slot == 24:
        return ('k', 4)
    return ('q', 20 + slot - 25)


def tm_map(parity, dcol):
    if parity == 0:
        return {0: ('v', 0, 2), 256: ('v', 2, 2), 512: ('z', 0), 1024: ('z', 512), 1536: ('v', 4, 4),
                2048: ('z', 1024), 2560: ('z', 1536)}[dcol]
    return {0: ('v', 0, 4), 512: ('z', 0), 1024: ('z', 512), 1536: ('z', 1024), 2048: ('z', 1536)}[dcol]


NK = {0: 12, 1: 5}
NV = {0: 8, 1: 4}
NQS = {0: 16, 1: 24}


def own_positions(S, r):
    NT = S // 128
    NO = NT // 4
    pos = np.concatenate([np.arange(128) + (4 * j + r) * 128 for j in range(NO)])
    return pos


def rope_tables_fm(pos, dim):
    half = dim // 2
    inv = (ROPE_THETA ** (-np.arange(half, dtype=np.float32) / half)).astype(np.float32)
    ang = pos.astype(np.float32)[None, :] * inv[:, None]
    cos = np.cos(ang).astype(np.float32)
    sin = np.sin(ang).astype(np.float32)
    rows = np.arange(128) % half
    return cos[rows].astype(NPBF), sin[rows].astype(NPBF)


def rot_matrix_T(dim):
    half = dim // 2
    R = np.zeros((128, 128), np.float32)
    for m in range(128):
        blk = (m // dim) * dim
        w = m - blk
        if w < half:
            R[m, blk + w + half] = -1.0
        else:
            R[m, blk + w - half] = 1.0
    return R.T.copy().astype(NPBF)


def ones_block(dim):
    O = np.zeros((128, 128), np.float32)
    for m in range(128):
        blk = (m // dim) * dim
        O[blk:blk + dim, m] = 1.0 / dim
    return O.astype(NPBF)


def a_specs(parity):
    groups = []
    fm = [0]
    tmc = [0]
    tmf = [0]

    def fm_slot():
        fm[0] += 1
        return fm[0] - 1

    def add_fm_group(c0, nch, post, gain_idx, variant):
        for g0 in range(0, nch, 4):
            nn = min(4, nch - g0)
            items = [('fm', k * 128, post, gain_idx, variant, fm_slot()) for k in range(nn)]
            groups.append(dict(c0=c0 + g0 * 128, n=nn * 128, items=items))

    def add_tm(c0, ncols, func, dst='b'):
        for g0 in range(0, ncols, 512):
            nn = min(512, ncols - g0)
            if dst == 'b':
                dcol = tmc[0]
                tmc[0] += nn
            else:
                dcol = tmf[0]
                tmf[0] += nn
            groups.append(dict(c0=c0 + g0, n=nn, items=[('tm', 0, nn, func, dst, dcol)]))

    if parity == 0:
        c = 0
        add_fm_group(c, 8, 'nr', 0, 128); c += 1024
        add_fm_group(c, 4, 'raw', None, 128); c += 512
        groups.append(dict(c0=c, n=512, items=[('fm', 0, 'nr', 1, 128, fm_slot()), ('fm', 128, 'nr', 1, 128, fm_slot()),
                                               ('tm', 256, 256, 'copy', 'b', tmc[0])])); tmc[0] += 256; c += 512
        groups.append(dict(c0=c, n=512, items=[('fm', 0, 'nr', 2, 128, fm_slot()), ('fm', 128, 'nr', 2, 128, fm_slot()),
                                               ('tm', 256, 256, 'copy', 'b', tmc[0])])); tmc[0] += 256; c += 512
        add_tm(c, 24, 'sigmoid', 'f'); c += 24
        add_tm(c, 1024, 'silu'); c += 1024
        add_fm_group(c, 4, 'nr', 3, 64); c += 512
        add_fm_group(c, 4, 'nr', 4, 64); c += 512
        add_tm(c, 512, 'copy'); c += 512
        add_tm(c, 512, 'silu'); c += 512
        add_fm_group(c, 4, 'n', 5, 128); c += 512
        add_tm(c, 512, 'silu'); c += 512
        assert c == sum(EVEN_SPLITS)
    else:
        c = 0
        cq = c; c += 1536
        ck = c; c += 512
        cv = c; c += 512
        iq = c; c += 1024
        ik = c; c += 64
        iw = c; c += 16
        cz = c; c += 1536
        mq = c; c += 512
        mz = c; c += 512
        assert c == sum(ODD_SPLITS)
        groups.append(dict(c0=ik, n=80, items=[('tm', 64, 16, 'sign', 'f', 0), ('wrep',), ('fmrep', 0, 'r', None, 64, None)]))
        tmf[0] += 16
        add_fm_group(cq, 12, 'nr', 0, 128)
        add_fm_group(ck, 4, 'nr', 1, 128)
        add_tm(cv, 512, 'copy')
        for g0 in range(2):
            items = [('fm', k * 128, 'rw', None, 64, fm_slot(), g0 * 4 + k) for k in range(4)]
            groups.append(dict(c0=iq + g0 * 512, n=512, items=items))
        groups[0]['items'][2] = ('fmrep', 0, 'r', None, 64, fm_slot())
        add_tm(cz, 1536, 'silu')
        add_fm_group(mq, 4, 'n', 2, 128)
        add_tm(mz, 512, 'silu')
    return groups, fm[0], tmc[0], max(tmf[0], 1)


def build_A(parity, S, fx=None):
    NT = S // 128
    NO = NT // 4
    NOW = NO * 128
    NTG = NOW // 512
    groups, NFM, NTMC, NTMF = a_specs(parity)
    CIN = sum(EVEN_SPLITS) if parity == 0 else sum(ODD_SPLITS)
    NG = 6 if parity == 0 else 3
    if fx is None:
        nc = bass.Bass("TRN2", target_bir_lowering=False)
        x = nc.dram_tensor("x", [NOW, D], F32, kind="ExternalInput").ap()
        w_in = nc.dram_tensor("w_in", [D, CIN], F32, kind="ExternalInput").ap()
        ng = nc.dram_tensor("ng", [128, KC], F32, kind="ExternalInput").ap()
        gains = nc.dram_tensor("gains", [128, NG], F32, kind="ExternalInput").ap()
        tabs = nc.dram_tensor("tabs", [128, 4, NOW], BF16, kind="ExternalInput").ap()
        cmat = nc.dram_tensor("cmat", [128, 5, 128], BF16, kind="ExternalInput").ap()
        sel = nc.dram_tensor("sel", [16, 8, 128], BF16, kind="ExternalInput").ap() if parity == 1 else None
        fmo = nc.dram_tensor("fmo", [128, NFM, NOW], BF16, kind="ExternalOutput").ap()
        tmo = nc.dram_tensor("tmo", [NOW, NTMC], BF16, kind="ExternalOutput").ap()
        tmf = nc.dram_tensor("tmf", [NOW, NTMF], F32, kind="ExternalOutput").ap()
    else:
        nc = fx.nc
        x = fx.x_src
        w_in = fx.ext("w_in", [D, CIN], F32, True)
        ng = fx.ext("ng", [128, KC], F32, True)
        gains = fx.ext("gains", [128, NG], F32, True)
        tabs = fx.ext("tabs", [128, 4, NOW], BF16)
        cmat = fx.ext("cmat", [128, 5, 128], BF16)
        sel = fx.ext("sel", [16, 8, 128], BF16) if parity == 1 else None
        tmf = fx.internal("tmf", [NOW, NTMF], F32)
        fmq_d = fx.internal("fmq", [128, NQS[parity], NOW], BF16)
        kown_fm = fx.internal("kown_fm", [NK[parity] * 128, NOW], BF16)
        kown_v = fx.internal("kown_v", [NV[parity] * 128, NO * 129], BF16)
        tmz_d = fx.internal("tmz", [NOW, 2048], BF16)
        kown_fm_v = kown_fm.rearrange("(s d) n -> d s n", d=128)
        kown_v_v = kown_v.rearrange("(u p) (j d) -> p u j d", p=128, d=129)

    with ExitStack() as es:
        P = Prog(nc) if fx is None else fx.P
        hT = sb(nc, es, "hT", [128, KC, NOW], BF16)
        ng_t = sb(nc, es, "ng_t", [128, KC], F32)
        gains_t = sb(nc, es, "gains_t", [128, NG], F32)
        tabs_t = sb(nc, es, "tabs_t", [128, 4, NOW], BF16)
        cmat_t = sb(nc, es, "cmat_t", [128, 5, 128], BF16)
        eps_t = sb(nc, es, "eps_t", [128, 1], F32)
        xin = Pool(nc, es, "xin", 2, [128, D], F32)
        xs = Pool(nc, es, "xs", 1, [128, D], BF16)
        junk = sb(nc, es, "junk", [128, D], BF16)
        ssp = Pool(nc, es, "ss", 4, [128, 1], F32)
        wst = Pool(nc, es, "wst", 2, [128, KC // 2, 512], F32)
        wbf = Pool(nc, es, "wbf", 2, [128, KC, 512], BF16)
        psq = Pool(nc, es, "psq", 3, [128, 512], F32, psum=True)
        psb = Pool(nc, es, "psb", 2, [128, 512], F32, psum=True)
        psc = Pool(nc, es, "psc", 2, [128, 512], F32, psum=True)
        pst = Pool(nc, es, "pst", 1, [128, 8, 128], BF16, psum=True)
        sqb = Pool(nc, es, "sqb", 2, [128, 512], BF16)
        f1 = Pool(nc, es, "f1", 2, [128, 512], F32)
        f2 = Pool(nc, es, "f2", 2, [128, 512], F32)
        f3 = Pool(nc, es, "f3", 2, [128, 512], F32)
        qn = Pool(nc, es, "qn", 2, [128, 512], BF16)
        stg = Pool(nc, es, "stg", 5, [128, 512], BF16)
        stf = Pool(nc, es, "stf", 2, [128, 32], F32)
        stv = Pool(nc, es, "stv", 2, [128, 4, 129], BF16) if fx is not None else None
        if fx is not None:
            for t_ in stv.t:
                P.memset('pool', t_[:, :, 128:129], 1.0, w=[('stv', stv.t.index(t_))])
        wrep = sb(nc, es, "wrep", [128, KC, 128], BF16) if parity == 1 else None
        sel_t = sb(nc, es, "sel_t", [16, 8, 128], BF16) if parity == 1 else None
        aw16 = sb(nc, es, "aw16", [16, NOW], BF16) if parity == 1 else None
        if parity == 1:
            P.dma(sel_t[:], sel, w=['sel'])

        ident = cmat_t[:, 0, :]
        P.dma(ng_t[:], ng, w=['ng'])
        P.dma(gains_t[:], gains, w=['gains'])
        P.dma(tabs_t[:], tabs, w=['tabs'])
        P.dma(cmat_t[:], cmat, w=['cmat'])
        P.memset('dve', eps_t[:], EPS, w=['eps'])

        import os
        _p1 = os.environ.get('A_P1', 'full')
        for j in range(NO if not _p1.startswith('one') else 1):
            xt, xk = xin.next()
            P.dma(xt[:], x[j * 128:(j + 1) * 128, :], w=[xk])
            st, sk = ssp.next()
            P.act(junk[:], xt[:], AF.Square, r=[xk], w=['junk', sk], accum_out=st[:])
            s2, s2k = ssp.next()
            P.act(s2[:], st[:], AF.Sqrt, r=[sk, 'eps'], w=[s2k], scale=1.0 / D, bias=eps_t[:])
            P.op('dve', lambda e, a=s2: e.reciprocal(a[:], a[:]), r=[s2k], w=[s2k])
            xb, xbk = xs.next()
            P.ts('dve', xb[:], xt[:], s2[:], None, ALU.mult, r=[xk, s2k], w=[xbk])
            if _p1 == 'notr':
                continue
            for half in range(2 if _p1 != 'oneh' else 1):
                pt, ptk = pst.next()
                for q in range(8):
                    kc = half * 8 + q
                    P.tr(pt[:, q, :], xb[:, kc * 128:(kc + 1) * 128], ident, r=[xbk, 'cmat'], w=[ptk])
                for q in range(8):
                    kc = half * 8 + q
                    eng = 'dve'
                    if eng == 'pool':
                        P.act(hT[:, kc, j * 128:(j + 1) * 128], pt[:, q, :], AF.Copy, r=[ptk, 'ng'],
                              w=[('hT', j, kc)], scale=ng_t[:, kc:kc + 1])
                    else:
                        P.ts('dve', hT[:, kc, j * 128:(j + 1) * 128], pt[:, q, :], ng_t[:, kc:kc + 1], None,
                             ALU.mult, r=[ptk, 'ng'], w=[('hT', j, kc)])

        def hkeys(tiles):
            return [('hT', j, kc) for j in tiles for kc in range(KC)]

        def fm_unit(it, tg, wb, wbk, coff, post, gidx, slot, onesb, rotT, cosT, sinT):
            tiles = range(4 * tg, 4 * tg + 4)
            cols = slice(tg * 512, (tg + 1) * 512)
            pa, pak = psq.next()
            for kc in range(KC):
                if it[0] == 'fmrep':
                    lhsT = wrep[:, kc, :]
                    rk = [('wrep', kc)]
                else:
                    lhsT = wb[:, kc, coff:coff + 128]
                    rk = [(wbk, kc)]
                P.mm(pa[:], lhsT, hT[:, kc, cols], start=(kc == 0), stop=(kc == KC - 1),
                     r=rk + [('hT', j, kc) for j in tiles], w=[pak])
            so, sok = stg.next()
            if post == 'raw':
                P.act(so[:], pa[:], AF.Copy, r=[pak], w=[sok])
            elif post in ('nr', 'n'):
                sq, sqk = sqb.next()
                P.act(sq[:], pa[:], AF.Square, r=[pak], w=[sqk])
                yield
                pb, pbk = psb.next()
                P.mm(pb[:], onesb, sq[:], r=[sqk, 'cmat'], w=[pbk])
                sd, sdk = f1.next()
                P.act(sd[:], pb[:], AF.Sqrt, r=[pbk, 'eps'], w=[sdk], bias=eps_t[:])
                P.op('dve', lambda e, a=sd: e.reciprocal(a[:], a[:]), r=[sdk], w=[sdk])
                if post == 'n':
                    P.stt('dve', so[:], pa[:], gains_t[:, gidx:gidx + 1], sd[:], ALU.mult, ALU.mult,
                          r=[pak, sdk, 'gains'], w=[sok])
                else:
                    q_, qk = qn.next()
                    P.stt('dve', q_[:], pa[:], gains_t[:, gidx:gidx + 1], sd[:], ALU.mult, ALU.mult,
                          r=[pak, sdk, 'gains'], w=[qk])
                    yield
                    pc, pck = psc.next()
                    P.mm(pc[:], rotT, q_[:], r=[qk, 'cmat'], w=[pck])
                    t1, t1k = f2.next()
                    P.tt('pool', t1[:], q_[:], cosT[:, cols], ALU.mult, r=[qk, 'tabs'], w=[t1k])
                    t2, t2k = f3.next()
                    P.tt('dve', t2[:], pc[:], sinT[:, cols], ALU.mult, r=[pck, 'tabs'], w=[t2k])
                    P.tt('pool', so[:], t1[:], t2[:], ALU.add, r=[t1k, t2k], w=[sok])
            elif post == 'r':
                q_, qk = qn.next()
                P.act(q_[:], pa[:], AF.Copy, r=[pak], w=[qk])
                yield
                pc, pck = psc.next()
                P.mm(pc[:], rotT, q_[:], r=[qk, 'cmat'], w=[pck])
                t1, t1k = f2.next()
                P.tt('pool', t1[:], q_[:], cosT[:, cols], ALU.mult, r=[qk, 'tabs'], w=[t1k])
                t2, t2k = f3.next()
                P.tt('dve', t2[:], pc[:], sinT[:, cols], ALU.mult, r=[pck, 'tabs'], w=[t2k])
                P.tt('pool', so[:], t1[:], t2[:], ALU.add, r=[t1k, t2k], w=[sok])
            elif post == 'rw':
                cidx = it[6]
                q_, qk = qn.next()
                P.act(q_[:], pa[:], AF.Copy, r=[pak], w=[qk])
                yield
                pc, pck = psc.next()
                P.mm(pc[:], rotT, q_[:], r=[qk, 'cmat'], w=[pck])
                pw, pwk = psb.next()
                P.mm(pw[:], sel_t[:, cidx, :], aw16[:, cols], r=['sel', ('aw16', tg)], w=[pwk])
                t1, t1k = f2.next()
                P.tt('pool', t1[:], q_[:], cosT[:, cols], ALU.mult, r=[qk, 'tabs'], w=[t1k])
                t2, t2k = f3.next()
                P.tt('dve', t2[:], pc[:], sinT[:, cols], ALU.mult, r=[pck, 'tabs'], w=[t2k])
                P.tt('pool', t1[:], t1[:], t2[:], ALU.add, r=[t1k, t2k], w=[t1k])
                P.tt('dve', so[:], t1[:], pw[:], ALU.mult, r=[t1k, pwk], w=[sok])
            if fx is None:
                P.dma(fmo[:, slot, cols], so[:], r=[sok], w=[('fmo', slot, tg)])
            else:
                kind_, idx_ = fm_map(parity, slot)
                dst_ = fmq_d[:, idx_, cols] if kind_ == 'q' else kown_fm_v[:, idx_, cols]
                P.dma(dst_, so[:], r=[sok], w=[('fmo', slot, tg)])
            return
            yield

        fm_live = []

        def fm_push(g_):
            next(g_, None)
            for o_ in list(fm_live):
                try:
                    next(o_)
                except StopIteration:
                    fm_live.remove(o_)
            fm_live.append(g_)

        def fm_flush():
            while fm_live:
                for o_ in list(fm_live):
                    try:
                        next(o_)
                    except StopIteration:
                        fm_live.remove(o_)

        funcs = {'copy': AF.Copy, 'silu': AF.Silu, 'sigmoid': AF.Sigmoid, 'sign': AF.Sign}
        import os
        _lim = int(os.environ.get('A_GROUPS', '999'))
        for gi, g in enumerate(groups):
            if gi >= _lim:
                break
            n = g['n']
            wb, wbk = wbf.next()
            wv = w_in[:, g['c0']:g['c0'] + n].rearrange("(kc p) c -> p kc c", p=128)
            for hf in range(2):
                ws, wsk = wst.next()
                P.dma(ws[:, :, 0:n], wv[:, hf * 8:(hf + 1) * 8, :], w=[wsk])
                for q in range(8):
                    kc = hf * 8 + q
                    eng = ('dve', 'pool')[kc % 2]
                    P.cp(eng, wb[:, kc, 0:n], ws[:, q, 0:n], r=[wsk], w=[(wbk, kc)])
            wkeys = [(wbk, kc) for kc in range(KC)]
            for it in g['items']:
                if it[0] == 'wrep':
                    fm_flush()
                    for kc in range(KC):
                        for hh in range(2):
                            P.cp('pool', wrep[:, kc, hh * 64:(hh + 1) * 64], wb[:, kc, 0:64],
                                 r=[(wbk, kc)], w=[('wrep', kc)])
                    for tg in range(NTG):
                        cols = slice(tg * 512, (tg + 1) * 512)
                        pa, pak = psq.next()
                        for kc in range(KC):
                            P.mm(pa[0:16, :], wb[:, kc, 64:80], hT[:, kc, cols], start=(kc == 0), stop=(kc == KC - 1),
                                 r=[(wbk, kc)] + [('hT', j, kc) for j in range(4 * tg, 4 * tg + 4)], w=[pak])
                        P.act(aw16[:, cols], pa[0:16, :], AF.Abs, r=[pak], w=[('aw16', tg)], scale=1.0 / 32.0)
                    continue
                if it[0] in ('fm', 'fmrep'):
                    _, coff, post, gidx, variant, slot = it[:6]
                    vi = 0 if variant == 128 else 1
                    onesb = cmat_t[:, 1 + vi, :]
                    rotT = cmat_t[:, 3 + vi, :]
                    cosT = tabs_t[:, 2 * vi, :]
                    sinT = tabs_t[:, 2 * vi + 1, :]
                    for tg in range(NTG):
                        fm_push(fm_unit(it, tg, wb, wbk, coff, post, gidx, slot, onesb, rotT, cosT, sinT))
                elif it[0] == 'tm':
                    fm_flush()
                    _, coff, nn, func, dst, dcol = it
                    for j in range(NO):
                        pa, pak = psq.next()
                        for kc in range(KC):
                            P.mm(pa[:, 0:nn], hT[:, kc, j * 128:(j + 1) * 128], wb[:, kc, coff:coff + nn],
                                 start=(kc == 0), stop=(kc == KC - 1), r=[(wbk, kc), ('hT', j, kc)], w=[pak])
                        if dst == 'b' and fx is not None and tm_map(parity, dcol)[0] == 'v':
                            _, u0_, nu_ = tm_map(parity, dcol)
                            so, sok = stv.next()
                            P.act(so[:, 0:nu_, 0:128], pa[:, 0:nn].rearrange("p (u d) -> p u d", d=128), funcs[func],
                                  r=[pak], w=[sok])
                            P.dma(kown_v_v[:, u0_:u0_ + nu_, j, :], so[:, 0:nu_, :], r=[sok], w=[('tmo', dcol, j)])
                        elif dst == 'b':
                            so, sok = stg.next()
                            P.act(so[:, 0:nn], pa[:, 0:nn], funcs[func], r=[pak], w=[sok])
                            if fx is None:
                                P.dma(tmo[j * 128:(j + 1) * 128, dcol:dcol + nn], so[:, 0:nn], r=[sok], w=[('tmo', dcol, j)])
                            else:
                                zc_ = tm_map(parity, dcol)[1]
                                P.dma(tmz_d[j * 128:(j + 1) * 128, zc_:zc_ + nn], so[:, 0:nn], r=[sok], w=[('tmo', dcol, j)])
                        else:
                            so, sok = stf.next()
                            P.act(so[:, 0:nn], pa[:, 0:nn], funcs[func], r=[pak], w=[sok])
                            P.dma(tmf[j * 128:(j + 1) * 128, dcol:dcol + nn], so[:, 0:nn], r=[sok], w=[('tmf', dcol, j)])
        fm_flush()
        if fx is None:
            P.emit(es)
    return nc


def cmat_host():
    ident = np.eye(128, dtype=np.float32).astype(NPBF)
    return np.stack([ident, ones_block(128), ones_block(64), rot_matrix_T(128), rot_matrix_T(64)], axis=1)


def tabs_host(S, r):
    pos = own_positions(S, r)
    c128, s128 = rope_tables_fm(pos, 128)
    c64, s64 = rope_tables_fm(pos, 64)
    return np.stack([c128, s128, c64, s64], axis=1)


def geom(S):
    NT = S // 128
    NO = NT // 4
    NOW = NO * 128
    NC = NO // 4
    n_cmp = (S - 32) // 16 + 1
    NCT = (n_cmp + 127) // 128
    n_sel = S // 64
    return NT, NO, NOW, NC, n_cmp, NCT, n_sel


class BCtx:
    pass


def b_common(nc, es, P, S, parity, fx=None):
    NT, NO, NOW, NC, n_cmp, NCT, n_sel = geom(S)
    c = BCtx()
    c.S, c.NT, c.NO, c.NOW, c.NC = S, NT, NO, NOW, NC
    c.fx = fx
    NQ = 16 if parity == 0 else 24
    NZ = 2048
    if fx is None:
        dt = nc.dram_tensor
        c.ext = lambda name, shape, dtype, per_layer=False: dt(name, shape, dtype, kind="ExternalInput").ap()
        c.x = dt("x", [NOW, D], F32, kind="ExternalInput").ap()
        c.xo = dt("xo", [NOW, D], F32, kind="ExternalOutput").ap()
        c.fmq = dt("fmq", [128, NQ, NOW], BF16, kind="ExternalInput").ap()
        c.tmz = dt("tmz", [NOW, NZ], BF16, kind="ExternalInput").ap()
    else:
        c.ext = fx.ext
        c.x = fx.x_src
        c.xo = fx.x_dst
        c.fmq = fx.internal("fmq", [128, NQ, NOW], BF16)
        c.tmz = fx.internal("tmz", [NOW, NZ], BF16)
        c.tmf = fx.internal("tmf", [NOW, 24 if parity == 0 else 16], F32)
        c.kall_fm = fx.internal("kall_fm", [4 * NK[parity] * 128, NOW], BF16)
        c.kall_v = fx.internal("kall_v", [4 * NV[parity] * 128, NO * 129], BF16)
        c.kall_fm_v = c.kall_fm.rearrange("(r s d) (j p) -> d s j r p", r=4, d=128, p=128)
        c.kall_v_v = c.kall_v.rearrange("(r u p) (j d) -> p u j r d", r=4, p=128, d=129)
    c.w_out = c.ext("w_out", [D, D], F32, True)
    c.mem = c.ext("mem", [256, D], F32)
    c.w_kv = c.ext("w_kv", [D, 1024], F32, True)
    c.memg = c.ext("memg", [128, KC + 1], F32, True)
    c.cmat = c.ext("cmat", [128, 5, 128], BF16)
    c.dm = c.ext("dm", [128, 4, 128], BF16)

    def load_k(slot, kg_unfused):
        if fx is None:
            P.dma(c.bufK[:, 0:S], kg_unfused[:, slot, :], w=['bufK'])
        else:
            dv = c.bufK[:, 0:S].rearrange("d (j r p) -> d j r p", r=4, p=128)
            for r_ in range(4):
                P.dma(dv[:, :, r_, :], c.kall_fm_v[:, slot, :, r_, :], w=['bufK'])
    c.load_k = load_k

    def load_v(unit, vg_unfused):
        if fx is None:
            P.dma(c.bufV[:, 0:NT * 129], vg_unfused[unit], w=['bufV'])
        else:
            dv = c.bufV[:, 0:NT * 129].rearrange("p (j r d) -> p j r d", r=4, d=129)
            for r_ in range(4):
                P.dma(dv[:, :, r_, :], c.kall_v_v[:, unit, :, r_, :], w=['bufV'])
    c.load_v = load_v

    c.cmat_t = sb(nc, es, "cmat_t", [128, 5, 128], BF16)
    c.dm_t = sb(nc, es, "dm_t", [128, 4, 128], BF16)
    c.memg_t = sb(nc, es, "memg_t", [128, KC + 1], F32)
    c.eps_t = sb(nc, es, "eps_t", [128, 1], F32)
    c.y_all = sb(nc, es, "y_all", [128, NO, D], BF16)
    KCOLS = max(S, 8192)
    VCOLS = max(NT * 129, 8192)
    c.big = sb(nc, es, "big", [128, KCOLS + VCOLS], BF16)
    c.bufK = c.big[:, 0:KCOLS]
    c.bufV = c.big[:, KCOLS:KCOLS + VCOLS]
    c.qT = sb(nc, es, "qT", [128, 4, NOW], BF16)
    c.wst = Pool(nc, es, "wst", 1, [128, KC // 2, 512], F32)
    c.memT = c.bufV[:, 0:KC * 256].rearrange("p (k m) -> p k m", m=256)
    c.MKT = sb(nc, es, "MKT", [128, 4, 256], BF16)
    c.MVx = sb(nc, es, "MVx", [128, 2, 4, 129], BF16)
    c.stp = Pool(nc, es, "stp", 2, [128, 512], F32, psum=True)
    c.obank = [es.enter_context(nc.psum_tensor(_uname(f"ob{i}"), [128, 512], F32)) for i in range(4)]
    c.psm = Pool(nc, es, "psm", 1, [128, 512], F32, psum=True)
    c.pst = Pool(nc, es, "pst", 1, [128, 8, 128], BF16, psum=True)
    c.Ep = Pool(nc, es, "Ep", 3, [128, 512], BF16)
    c.f1 = Pool(nc, es, "f1", 2, [128, 512], F32)
    c.b1 = Pool(nc, es, "b1", 2, [128, 512], BF16)
    c.sm = Pool(nc, es, "sm", 8, [128, 4], F32)
    c.zp = Pool(nc, es, "zp", 3, [128, 512], BF16)

    def zt_for(j, col0, n):
        t, k = c.zp.next()
        P.dma(t[:, 0:n], c.tmz[j * 128:(j + 1) * 128, col0:col0 + n], w=[k])
        return t, k
    c.zt_for = zt_for
    c.xp = Pool(nc, es, "xp", 2, [128, 512], F32)
    c.ident = c.cmat_t[:, 0, :]
    c.ones128 = c.cmat_t[:, 1, :]
    c.ones64 = c.cmat_t[:, 2, :]
    c.rot128 = c.cmat_t[:, 3, :]

    P.dma(c.cmat_t[:], c.cmat, w=['cmat'])
    P.dma(c.dm_t[:], c.dm, w=['dm'])
    P.dma(c.memg_t[:], c.memg, w=['memg'])
    P.memset('dve', c.eps_t[:], EPS, w=['eps'])
    return c


def fm_norm(P, c, pa, pak, ncols, gain_ap, gain_key, out_ap, out_keys, ones=None):
    ones = c.ones128 if ones is None else ones
    sq, sqk = c.b1.next()
    P.act(sq[:, 0:ncols], pa[:, 0:ncols], AF.Square, r=[pak], w=[sqk])
    pb, pbk = c.psm.next()
    P.mm(pb[:, 0:ncols], ones, sq[:, 0:ncols], r=[sqk, 'cmat'], w=[pbk])
    sd, sdk = c.f1.next()
    P.act(sd[:, 0:ncols], pb[:, 0:ncols], AF.Sqrt, r=[pbk, 'eps'], w=[sdk], bias=c.eps_t[:])
    P.op('dve', lambda e: e.reciprocal(sd[:, 0:ncols], sd[:, 0:ncols]), r=[sdk], w=[sdk])
    P.stt('dve', out_ap, pa[:, 0:ncols], gain_ap, sd[:, 0:ncols], ALU.mult, ALU.mult,
          r=[pak, sdk, gain_key], w=out_keys)


def load_cast_w(P, c, dst, dst_key, w_dram_view, n):
    for hf in range(2):
        ws, wsk = c.wst.next()
        P.dma(ws[:, :, 0:n], w_dram_view[:, hf * 8:(hf + 1) * 8, :], w=[wsk])
        for q in range(8):
            kc = hf * 8 + q
            eng = ('dve', 'pool')[kc % 2]
            P.cp(eng, dst[:, kc, 0:n], ws[:, q, 0:n], r=[wsk], w=[dst_key])


def b_memory_kv(P, c):
    for mt in range(2):
        wst0, xk = c.wst.next()
        xt = wst0[:].rearrange("p a b -> p (a b)")[:, 0:D]
        P.dma(xt[:], c.mem[mt * 128:(mt + 1) * 128, :], w=[xk])
        xb, xbk = c.y_all[:, 0, :], ('y', 0)
        st, sk = c.sm.next()
        P.act(xb[:], xt[:], AF.Square, r=[xk], w=[xbk, sk], accum_out=st[:, 0:1])
        P.act(st[:, 1:2], st[:, 0:1], AF.Sqrt, r=[sk, 'eps'], w=[sk], scale=1.0 / D, bias=c.eps_t[:])
        P.op('dve', lambda e, a=st: e.reciprocal(a[:, 1:2], a[:, 1:2]), r=[sk], w=[sk])
        P.ts('dve', xb[:], xt[:], st[:, 1:2], None, ALU.mult, r=[xk, sk], w=[xbk])
        for half in range(2):
            pt, ptk = c.pst.next()
            for q in range(8):
                kc = half * 8 + q
                P.tr(pt[:, q, :], xb[:, kc * 128:(kc + 1) * 128], c.ident, r=[xbk, 'cmat'], w=[ptk])
            for q in range(8):
                kc = half * 8 + q
                P.ts('dve', c.memT[:, kc, mt * 128:(mt + 1) * 128], pt[:, q, :], c.memg_t[:, kc:kc + 1], None,
                     ALU.mult, r=[ptk, 'memg'], w=['bufV'])
    wv = c.w_kv.rearrange("(kc p) c -> p kc c", p=128)
    wk = c.bufK[:, 0:KC * 512].rearrange("p (k c) -> p k c", c=512)
    load_cast_w(P, c, wk, 'bufK', wv[:, :, 0:512], 512)
    for h in range(4):
        pa, pak = c.stp.next()
        for kc in range(KC):
            P.mm(pa[:, 0:256], wk[:, kc, h * 128:(h + 1) * 128], c.memT[:, kc, :], start=(kc == 0), stop=(kc == KC - 1),
                 r=['bufK', 'bufV'], w=[pak])
        fm_norm(P, c, pa, pak, 256, c.memg_t[:, KC:KC + 1], 'memg', c.MKT[:, h, :], ['MKT'])
    load_cast_w(P, c, wk, 'bufK', wv[:, :, 512:1024], 512)
    P.memset('pool', c.MVx[:, :, :, 128:129], 1.0, w=['MVx'])
    for mt in range(2):
        pa, pak = c.stp.next()
        for kc in range(KC):
            P.mm(pa[:], c.memT[:, kc, mt * 128:(mt + 1) * 128], wk[:, kc, :], start=(kc == 0), stop=(kc == KC - 1),
                 r=['bufK', 'bufV'], w=[pak])
        P.act(c.MVx[:, mt, :, 0:128], pa[:].rearrange("p (h d) -> p h d", d=128), AF.Copy, r=[pak], w=['MVx'])


def attend(P, c, steps, scale):
    prev = None

    def do_pv(st_):
        E, Ek = st_['E']
        for (ec0, oap, okey, rhs, rkeys, first, last) in st_['pv']:
            P.mm(oap, E[:, ec0:ec0 + 128], rhs, start=first, stop=last, r=[Ek] + rkeys, w=[okey])

    for st_ in steps:
        ps, psk = c.stp.next()
        c0, n = st_['c0'], st_['n']
        nm = len(st_['masks'])
        P.mm(ps[:, c0:c0 + n], st_['k'], st_['q'], start=True, stop=(nm == 0), r=st_['kkeys'] + st_['qkeys'], w=[psk])
        for mi, (mc0, mn, ml, mr, mk) in enumerate(st_['masks']):
            P.mm(ps[:, mc0:mc0 + mn], ml, mr, start=False, stop=(mi == nm - 1), r=mk, w=[psk])
        E, Ek = c.Ep.next()
        P.act(E[:, c0:c0 + n], ps[:, c0:c0 + n], AF.Exp, r=[psk], w=[Ek], scale=scale)
        st_['E'] = (E, Ek)
        if prev is not None:
            do_pv(prev)
        prev = st_
    if prev is not None:
        do_pv(prev)


def b_mem_attention(P, c, mq_slot0, ycol0, zcol0):
    NO, NC = c.NO, c.NC
    P.dma(c.qT[:], c.fmq[:, mq_slot0:mq_slot0 + 4, :], w=['qT'])
    scale = 128 ** -0.5
    for h in range(4):
        for ch in range(NC):
            steps = []
            for mt in range(2):
                pv = [(i * 128, c.obank[i][:, 0:129], ('ob', i), c.MVx[:, mt, h, :], ['MVx'], mt == 0, mt == 1)
                      for i in range(4)]
                steps.append(dict(k=c.MKT[:, h, mt * 128:(mt + 1) * 128], kkeys=['MKT'],
                                  q=c.qT[:, h, ch * 512:(ch + 1) * 512], qkeys=['qT'], c0=0, n=512, masks=[], pv=pv))
            attend(P, c, steps, scale)
            for i in range(4):
                j = ch * 4 + i
                zt, ztk = c.zt_for(j, zcol0 + h * 128, 128)
                s_, sk = c.sm.next()
                P.ts('dve', s_[:, 0:1], c.obank[i][:, 128:129], 1e-30, None, ALU.max, r=[('ob', i)], w=[sk])
                P.op('dve', lambda e, a=s_: e.reciprocal(a[:, 0:1], a[:, 0:1]), r=[sk], w=[sk])
                P.stt('dve', c.y_all[:, j, ycol0 + h * 128:ycol0 + (h + 1) * 128], c.obank[i][:, 0:128], s_[:, 0:1],
                      zt[:, 0:128], ALU.mult, ALU.mult,
                      r=[('ob', i), sk, ztk], w=[('y', j)])


def b_out_proj(P, c):
    NO = c.NO
    for j in range(NO):
        for half in range(2):
            pt, ptk = c.pst.next()
            for q in range(8):
                kc = half * 8 + q
                P.tr(pt[:, q, :], c.y_all[:, j, kc * 128:(kc + 1) * 128], c.ident, r=[('y', j), 'cmat'], w=[ptk])
            P.cp('dve', c.y_all[:, j, half * 1024:(half + 1) * 1024], pt[:].rearrange("p a b -> p (a b)"),
                 r=[ptk], w=[('y', j)])
    wv = c.w_out.rearrange("(kc p) c -> p kc c", p=128)
    wbs = [(c.bufK[:, 0:KC * 512].rearrange("p (k c) -> p k c", c=512), 'bufK'),
           (c.bufV[:, 0:KC * 512].rearrange("p (k c) -> p k c", c=512), 'bufV')]
    for cg in range(4):
        wb, wbk = wbs[cg % 2]
        load_cast_w(P, c, wb, wbk, wv[:, :, cg * 512:(cg + 1) * 512], 512)
        for j in range(NO):
            pa, pak = c.stp.next()
            for kc in range(KC):
                P.mm(pa[:], c.y_all[:, j, kc * 128:(kc + 1) * 128], wb[:, kc, :], start=(kc == 0), stop=(kc == KC - 1),
                     r=[('y', j), wbk], w=[pak])
            xt, xk = c.xp.next()
            P.dma(xt[:], c.x[j * 128:(j + 1) * 128, cg * 512:(cg + 1) * 512], w=[xk])
            P.tt('dve', xt[:], xt[:], pa[:], ALU.add, r=[xk, pak], w=[xk])
            P.dma(c.xo[j * 128:(j + 1) * 128, cg * 512:(cg + 1) * 512], xt[:], r=[xk], w=[('xo', j, cg)])


def build_B_even(S, fx=None):
    NT, NO, NOW, NC, n_cmp, NCT, n_sel = geom(S)
    NCP = NCT * 128
    nc = bass.Bass("TRN2", target_bir_lowering=False) if fx is None else fx.nc
    with ExitStack() as es:
        P = Prog(nc) if fx is None else fx.P
        c = b_common(nc, es, P, S, 0, fx)
        if fx is None:
            kg = c.ext("kg", [128, 12, S], BF16)
            vg = c.ext("vg", [8, 128, NT * 129], BF16)
            ag = c.ext("ag", [NOW, 24], F32)
        else:
            kg = vg = None
            ag = c.tmf
        cw1 = c.ext("cw1", [2, 4096, 256], F32, True)
        cw2 = c.ext("cw2", [128, 2, 2, 128], F32, True)
        peT = c.ext("peT", [128, 2, 32], BF16, True)
        ctab = c.ext("ctab", [128, 2, NCP], BF16)
        ovm = c.ext("ovm", [128, NCT, 128], BF16)
        expand = c.ext("expand", [128, 32, 128], BF16)
        wm = c.ext("wm", [128, 8, 128], BF16)
        cm = c.ext("cm", [NO, 128, NCT, 128], BF16)
        visfrc = c.ext("visfrc", [NO, 128, 2, 128], F32)
        gv = c.ext("gv", [128, 388], F32, True)

        cw2_f = sb(nc, es, "cw2_f", [128, 2, 2, 128], F32)
        cw2_t = sb(nc, es, "cw2_t", [128, 2, 2, 128], BF16)
        peT_t = sb(nc, es, "peT_t", [128, 2, 32], BF16)
        ctab_t = sb(nc, es, "ctab_t", [128, 2, NCP], BF16)
        ovm_t = sb(nc, es, "ovm_t", [128, NCT, 128], BF16)
        expand_t = sb(nc, es, "expand_t", [128, 32, 128], BF16)
        wm_t = sb(nc, es, "wm_t", [128, 8, 128], BF16)
        gv_t = sb(nc, es, "gv_t", [128, 388], F32)
        KCc = sb(nc, es, "KCc", [128, 2, NCP], BF16)
        VCx = sb(nc, es, "VCx", [128, 2, NCT, 257], BF16)
        hid = sb(nc, es, "hid", [128, 2, NCP], BF16)
        cbias = sb(nc, es, "cbias", [128, 2], F32)
        lam = sb(nc, es, "lam", [128, 4], F32)
        cmp_ = Pool(nc, es, "cmp", 2, [128, NCT, 128], BF16)
        vfp = Pool(nc, es, "vfp", 2, [128, 2, 128], F32)
        agp = Pool(nc, es, "agp", 2, [128, 24], F32)
        acc = Pool(nc, es, "acc", 1, [128, 512], F32)
        imp = Pool(nc, es, "imp", 2, [128, 128], F32)
        imp2 = Pool(nc, es, "imp2", 2, [128, 128], F32)
        m8 = Pool(nc, es, "m8", 2, [128, 16], F32)
        seln = Pool(nc, es, "seln", 2, [128, 128], BF16)
        selT = Pool(nc, es, "selT", 2, [128, 128], BF16)
        kwp = Pool(nc, es, "kwp", 1, [128, 8, 128], BF16)
        vwp = Pool(nc, es, "vwp", 1, [128, 8, 129], BF16)
        dtmp = Pool(nc, es, "dtmp", 2, [128, 4, 128], F32)

        for (t, src, k) in ((cw2_f, cw2, 'cw2f'), (peT_t, peT, 'peT'), (ctab_t, ctab, 'ctab'), (ovm_t, ovm, 'ovm'),
                            (expand_t, expand, 'expand'), (wm_t, wm, 'wm'), (gv_t, gv, 'gv')):
            P.dma(t[:], src, w=[k])
        P.cp('dve', cw2_t[:], cw2_f[:], r=['cw2f'], w=['cw2'])
        lv = gv_t[:, 130:130 + 256]
        j1, j1k = c.f1.next()
        P.tt('dve', j1[:, 0:64], lv[:, 0:64], lv[:, 64:128], ALU.mult, r=['gv'], w=[j1k])
        P.tt('dve', j1[:, 64:128], lv[:, 128:192], lv[:, 192:256], ALU.mult, r=['gv'], w=[j1k])
        P.op('dve', lambda e: e.reduce_sum(lam[:, 0:1], j1[:, 0:64], AX.X), r=[j1k], w=['lam'])
        P.op('dve', lambda e: e.reduce_sum(lam[:, 1:2], j1[:, 64:128], AX.X), r=[j1k], w=['lam'])
        P.act(lam[:, 0:2], lam[:, 0:2], AF.Exp, r=['lam'], w=['lam'])
        P.tt('dve', lam[:, 2:3], lam[:, 0:1], lam[:, 1:2], ALU.subtract, r=['lam'], w=['lam'])
        P.tt('dve', lam[:, 3:4], lam[:, 2:3], gv_t[:, 386:387], ALU.add, r=['lam', 'gv'], w=['lam'])
        P.ts('dve', lam[:, 3:4], lam[:, 3:4], -1.0, None, ALU.mult, r=['lam'], w=['lam'])

        b_memory_kv(P, c)

        P.memset('pool', hid[:], 0.0, w=['hid'])
        P.memset('pool', KCc[:], 0.0, w=['KCc'])
        P.memset('pool', VCx[:], 0.0, w=['VCx'])
        P.memset('pool', VCx[:, :, :, 256:257], 1.0, w=['VCx'])
        for g in range(2):
            for ct in range(NCT):
                P.cp('pool', VCx[:, g, ct, 128:256], ovm_t[:, ct, :], r=['ovm'], w=['VCx'])
        w1v = c.bufV[:, 0:32 * 256].rearrange("p (l h) -> p l h", h=256)
        for kv in range(2):
            wsrc = cw1[kv].rearrange("(l p) h -> p l h", p=128)
            for q4 in range(4):
                ws, wsk = c.wst.next()
                wsv = ws[:].rearrange("p a b -> p (a b)")[:, 0:8 * 256].rearrange("p (l h) -> p l h", h=256)
                P.dma(wsv, wsrc[:, q4 * 8:(q4 + 1) * 8, :], w=[wsk])
                P.cp('dve', w1v[:, q4 * 8:q4 * 8 + 4, :], wsv[:, 0:4, :], r=[wsk], w=['bufV'])
                P.cp('pool', w1v[:, q4 * 8 + 4:q4 * 8 + 8, :], wsv[:, 4:8, :], r=[wsk], w=['bufV'])
            for half in range(2):
                pb, pbk = c.psm.next()
                for l in range(32):
                    P.mm(pb[:, 0:1], w1v[:, l, half * 128:(half + 1) * 128], peT_t[:, kv, l:l + 1],
                         start=(l == 0), stop=(l == 31), r=['bufV', 'peT'], w=[pbk])
                P.cp('dve', cbias[:, half:half + 1], pb[:, 0:1], r=[pbk], w=['cbias'])
            for g in range(2):
                c.load_k(2 * kv + g, kg)
                for half in range(2):
                    pa, pak = c.stp.next()
                    for l in range(32):
                        P.mm(pa[:, 0:n_cmp], w1v[:, l, half * 128:(half + 1) * 128],
                             c.bufK[:, l:l + 16 * (n_cmp - 1) + 1:16], start=(l == 0), stop=(l == 31),
                             r=['bufV', 'bufK'], w=[pak])
                    P.act(hid[:, half, 0:n_cmp], pa[:, 0:n_cmp], AF.Silu, r=[pak, 'cbias'], w=['hid'],
                          bias=cbias[:, half:half + 1])
                if kv == 0:
                    pa, pak = c.stp.next()
                    for half in range(2):
                        P.mm(pa[:, 0:NCP], cw2_t[:, 0, half, :], hid[:, half, :], start=(half == 0), stop=(half == 1),
                             r=['cw2', 'hid'], w=[pak])
                    q_, qk = c.b1.next()
                    fm_norm(P, c, pa, pak, NCP, gv_t[:, 0:1], 'gv', q_[:, 0:NCP], [qk])
                    pc, pck = c.psm.next()
                    P.mm(pc[:, 0:NCP], c.rot128, q_[:, 0:NCP], r=[qk, 'cmat'], w=[pck])
                    t1, t1k = c.f1.next()
                    P.tt('pool', t1[:, 0:NCP], q_[:, 0:NCP], ctab_t[:, 0, :], ALU.mult, r=[qk, 'ctab'], w=[t1k])
                    t2, t2k = c.f1.next()
                    P.tt('dve', t2[:, 0:NCP], pc[:, 0:NCP], ctab_t[:, 1, :], ALU.mult, r=[pck, 'ctab'], w=[t2k])
                    P.tt('pool', KCc[:, g, :], t1[:, 0:NCP], t2[:, 0:NCP], ALU.add, r=[t1k, t2k], w=['KCc'])
                else:
                    for ct in range(NCT):
                        pa, pak = c.stp.next()
                        for half in range(2):
                            P.mm(pa[:, 0:128], hid[:, half, ct * 128:(ct + 1) * 128], cw2_t[:, 1, half, :],
                                 start=(half == 0), stop=(half == 1), r=['cw2', 'hid'], w=[pak])
                        P.act(VCx[:, g, ct, 0:128], pa[:, 0:128], AF.Copy, r=[pak], w=['VCx'])

        sc = 128 ** -0.5
        ob = c.obank
        for g in range(2):
            P.dma(c.qT[:], c.fmq[:, g * 4:(g + 1) * 4, :], w=['qT'])
            c.load_k(4 + g, kg)
            c.load_v(g, vg)
            vS = c.bufV[:, 0:NT * 129].rearrange("p (t d) -> p t d", d=129)
            for j in range(NO):
                qj = c.qT[:, :, j * 128:(j + 1) * 128]
                cmj, cmk = cmp_.next()
                P.dma(cmj[:], cm[j], w=[cmk])
                vf, vfk = vfp.next()
                P.dma(vf[:], visfrc[j], w=[vfk])
                agt, agk = agp.next()
                P.dma(agt[:], ag[j * 128:(j + 1) * 128, :], w=[agk])
                a_, ak = acc.next()
                steps = []
                for ct in range(NCT):
                    masks = [(h * 128, 128, c.ident, cmj[:, ct, :], ['cmat', cmk]) for h in range(4)]
                    pv = [(h * 128, ob[h][:, 0:257], ('ob', h), VCx[:, g, ct, :], ['VCx'], ct == 0, ct == NCT - 1)
                          for h in range(4)]
                    steps.append(dict(k=KCc[:, g, ct * 128:(ct + 1) * 128], kkeys=['KCc'], q=qj, qkeys=['qT'],
                                      c0=0, n=512, masks=masks, pv=pv))
                attend(P, c, steps, sc)
                im, imk = imp.next()
                for h in range(4):
                    s_, sk = c.sm.next()
                    P.ts('dve', s_[:, 0:1], ob[h][:, 256:257], 1e-30, None, ALU.max, r=[('ob', h)], w=[sk])
                    P.op('dve', lambda e, a=s_: e.reciprocal(a[:, 0:1], a[:, 0:1]), r=[sk], w=[sk])
                    P.tt('dve', s_[:, 1:2], s_[:, 0:1], agt[:, (g * 4 + h) * 3:(g * 4 + h) * 3 + 1], ALU.mult,
                         r=[sk, agk], w=[sk])
                    P.ts('dve', a_[:, h * 128:(h + 1) * 128], ob[h][:, 0:128], s_[:, 1:2], None, ALU.mult,
                         r=[('ob', h), sk], w=[ak])
                    if h == 0:
                        P.ts('dve', im[:], ob[h][:, 128:256], s_[:, 0:1], None, ALU.mult, r=[('ob', h), sk], w=[imk])
                    else:
                        P.stt('dve', im[:], ob[h][:, 128:256], s_[:, 0:1], im[:], ALU.mult, ALU.add,
                              r=[('ob', h), sk, imk], w=[imk])
                P.tt('dve', im[:], im[:], vf[:, 0, :], ALU.mult, r=[imk, vfk], w=[imk])
                P.tt('dve', im[:], im[:], vf[:, 1, :], ALU.add, r=[imk, vfk], w=[imk])
                mx, mxk = m8.next()
                P.op('dve', lambda e, a=mx, b=im: e.max(a[:, 0:8], b[:]), r=[imk], w=[mxk])
                i2, i2k = imp2.next()
                P.op('dve', lambda e, a=mx, b=im, o=i2: e.match_replace(o[:], a[:, 0:8], b[:], -3.0e38),
                     r=[imk, mxk], w=[i2k])
                P.op('dve', lambda e, a=mx, b=i2: e.max(a[:, 8:16], b[:]), r=[i2k], w=[mxk])
                sn, snk = seln.next()
                P.ts('dve', sn[:], im[:], mx[:, 15:16], NEG, ALU.is_lt, ALU.mult, r=[imk, mxk], w=[snk])
                pt, ptk = c.pst.next()
                P.tr(pt[:, 0, :], sn[:], c.ident, r=[snk, 'cmat'], w=[ptk])
                sT, sTk = selT.next()
                P.cp('dve', sT[:], pt[:, 0, :], r=[ptk], w=[sTk])
                u0 = 4 if j == 0 else 0
                kw, kwk = kwp.next()
                vw, vwk = vwp.next()
                t_lo = 4 * j - 4 + u0
                if fx is None:
                    P.dma(kw[:, u0:8, :], kg[:, 6 + g, t_lo * 128:(4 * j + 4) * 128].rearrange("p (u s) -> p u s", s=128), w=[kwk])
                    P.dma(vw[:, u0:8, :], vg[2 + g][:, t_lo * 129:(4 * j + 4) * 129].rearrange("p (u s) -> p u s", s=129), w=[vwk])
                else:
                    if j > 0:
                        P.dma(kw[:, 0:4, :], c.kall_fm_v[:, 6 + g, j - 1], w=[kwk])
                        P.dma(vw[:, 0:4, :], c.kall_v_v[:, 2 + g, j - 1], w=[vwk])
                    P.dma(kw[:, 4:8, :], c.kall_fm_v[:, 6 + g, j], w=[kwk])
                    P.dma(vw[:, 4:8, :], c.kall_v_v[:, 2 + g, j], w=[vwk])
                steps = []
                for u in range(u0, 8):
                    masks = [(h * 128, 128, c.ident, wm_t[:, u, :], ['cmat', 'wm']) for h in range(4)]
                    pv = [(h * 128, ob[h][:, 0:129], ('ob', h), vw[:, u, :], [vwk], u == u0, u == 7) for h in range(4)]
                    steps.append(dict(k=kw[:, u, :], kkeys=[kwk], q=qj, qkeys=['qT'], c0=0, n=512, masks=masks, pv=pv))
                attend(P, c, steps, sc)
                for h in range(4):
                    s_, sk = c.sm.next()
                    P.ts('dve', s_[:, 0:1], ob[h][:, 128:129], 1e-30, None, ALU.max, r=[('ob', h)], w=[sk])
                    P.op('dve', lambda e, a=s_: e.reciprocal(a[:, 0:1], a[:, 0:1]), r=[sk], w=[sk])
                    P.tt('dve', s_[:, 1:2], s_[:, 0:1], agt[:, (g * 4 + h) * 3 + 2:(g * 4 + h) * 3 + 3], ALU.mult,
                         r=[sk, agk], w=[sk])
                    P.stt('dve', a_[:, h * 128:(h + 1) * 128], ob[h][:, 0:128], s_[:, 1:2], a_[:, h * 128:(h + 1) * 128],
                          ALU.mult, ALU.add, r=[('ob', h), sk, ak], w=[ak])
                steps = []
                nk = 4 * j + 4
                for kt in range(nk):
                    hrows = slice((kt // 32) * 64, (kt // 32) * 64 + 64)
                    masks = [(h * 128, 128, expand_t[hrows, kt % 32, :], sT[hrows, :], ['expand', sTk]) for h in range(4)]
                    if kt >= 4 * j:
                        masks += [(h * 128, 128, c.ident, c.dm_t[:, kt - 4 * j, :], ['cmat', 'dm']) for h in range(4)]
                    pv = [(h * 128, ob[h][:, 0:129], ('ob', h), vS[:, kt, :], ['bufV'], kt == 0, kt == nk - 1)
                          for h in range(4)]
                    steps.append(dict(k=c.bufK[:, kt * 128:(kt + 1) * 128], kkeys=['bufK'], q=qj, qkeys=['qT'],
                                      c0=0, n=512, masks=masks, pv=pv))
                attend(P, c, steps, sc)
                zt, ztk = c.zt_for(j, g * 512, 512)
                for h in range(4):
                    s_, sk = c.sm.next()
                    P.ts('dve', s_[:, 0:1], ob[h][:, 128:129], 1e-30, None, ALU.max, r=[('ob', h)], w=[sk])
                    P.op('dve', lambda e, a=s_: e.reciprocal(a[:, 0:1], a[:, 0:1]), r=[sk], w=[sk])
                    P.tt('dve', s_[:, 1:2], s_[:, 0:1], agt[:, (g * 4 + h) * 3 + 1:(g * 4 + h) * 3 + 2], ALU.mult,
                         r=[sk, agk], w=[sk])
                    P.stt('dve', a_[:, h * 128:(h + 1) * 128], ob[h][:, 0:128], s_[:, 1:2], a_[:, h * 128:(h + 1) * 128],
                          ALU.mult, ALU.add, r=[('ob', h), sk, ak], w=[ak])
                P.tt('dve', c.y_all[:, j, g * 512:(g + 1) * 512], a_[:], zt[:], ALU.mult, r=[ak, ztk], w=[('y', j)])

        sc64 = 64 ** -0.5
        for h in range(4):
            P.dma(c.qT[:, 0, :], c.fmq[:, 8 + h, :], w=['qT'])
            c.load_k(8 + h, kg)
            c.load_v(4 + h, vg)
            vS = c.bufV[:, 0:NT * 129].rearrange("p (t d) -> p t d", d=129)
            for ch in range(NC):
                j0 = 4 * ch
                dts = []
                for m in range(2):
                    rows = slice(m * 64, (m + 1) * 64)
                    steps = []
                    nk = 4 * j0 + 16
                    for kt in range(nk):
                        a = 0 if kt < 4 * j0 else (kt - 4 * j0) // 4
                        masks = []
                        if kt >= 4 * j0:
                            u = kt - 4 * (j0 + a)
                            masks = [(a * 128, 128, c.ident, c.dm_t[:, u, :], ['cmat', 'dm'])]
                        pv = []
                        for i in range(a, 4):
                            last_kt = 4 * (j0 + i) + 3
                            pv.append((i * 128, ob[i][:, 0:129], ('ob', i), vS[:, kt, :], ['bufV'], kt == 0, kt == last_kt))
                        steps.append(dict(k=c.bufK[rows, kt * 128:(kt + 1) * 128], kkeys=['bufK'],
                                          q=c.qT[rows, 0, ch * 512 + a * 128:(ch + 1) * 512], qkeys=['qT'],
                                          c0=a * 128, n=512 - a * 128, masks=masks, pv=pv))
                    attend(P, c, steps, sc64)
                    d_, dk = dtmp.next()
                    dts.append((d_, dk))
                    for i in range(4):
                        s_, sk = c.sm.next()
                        P.ts('dve', s_[:, 0:1], ob[i][:, 128:129], 1e-30, None, ALU.max, r=[('ob', i)], w=[sk])
                        P.op('dve', lambda e, a=s_: e.reciprocal(a[:, 0:1], a[:, 0:1]), r=[sk], w=[sk])
                        if m == 1:
                            P.tt('dve', s_[:, 0:1], s_[:, 0:1], lam[:, 3:4], ALU.mult, r=[sk, 'lam'], w=[sk])
                        P.ts('dve', d_[:, i, :], ob[i][:, 0:128], s_[:, 0:1], None, ALU.mult, r=[('ob', i), sk], w=[dk])
                (d0, d0k), (d1, d1k) = dts
                P.tt('pool', d0[:], d0[:], d1[:], ALU.add, r=[d0k, d1k], w=[d0k])
                for i in range(4):
                    j = j0 + i
                    s_, sk = c.sm.next()
                    jb, jbk = c.b1.next()
                    P.act(jb[:, 0:128], d0[:, i, :], AF.Square, r=[d0k], w=[jbk, sk], accum_out=s_[:, 0:1])
                    P.act(s_[:, 1:2], s_[:, 0:1], AF.Sqrt, r=[sk, 'eps'], w=[sk], scale=1.0 / 128, bias=c.eps_t[:])
                    P.op('dve', lambda e, a=s_: e.reciprocal(a[:, 1:2], a[:, 1:2]), r=[sk], w=[sk])
                    zt, ztk = c.zt_for(j, 1024 + h * 128, 128)
                    t1, t1k = c.f1.next()
                    P.stt('dve', t1[:, 0:128], d0[:, i, :], s_[:, 1:2], gv_t[:, 2:130], ALU.mult, ALU.mult,
                          r=[d0k, sk, 'gv'], w=[t1k])
                    P.stt('dve', c.y_all[:, j, 1024 + h * 128:1024 + (h + 1) * 128], t1[:, 0:128], gv_t[:, 387:388],
                          zt[:, 0:128], ALU.mult, ALU.mult, r=[t1k, ztk, 'gv'], w=[('y', j)])

        b_mem_attention(P, c, 12, 1536, 1536)
        b_out_proj(P, c)
        if fx is None:
            P.emit(es)
    return nc


def mask_consts(S, r):
    NT, NO, NOW, NC, n_cmp, NCT, n_sel = geom(S)
    s = np.arange(128)[:, None]
    t = np.arange(128)[None, :]
    dm = np.zeros((128, 4, 128), np.float32)
    for u in range(4):
        if u == r:
            dm[:, u, :] = np.where(s <= t, 0.0, NEG)
        elif u > r:
            dm[:, u, :] = NEG
    wm = np.zeros((128, 8, 128), np.float32)
    for u in range(8):
        delta = (r + 4 - u) * 128 + t - s
        wm[:, u, :] = np.where((delta >= 0) & (delta < 512), 0.0, NEG)
    cm = np.zeros((NO, 128, NCT, 128), np.float32)
    vf = np.zeros((NO, 128, 2, 128), np.float32)
    jj = np.arange(128)[None, :]
    for j in range(NO):
        tg = (4 * j + r) * 128 + np.arange(128)
        for ct in range(NCT):
            cidx = ct * 128 + np.arange(128)
            vis = (cidx[:, None] < n_cmp) & ((16 * cidx[:, None] + 31) <= tg[None, :])
            cm[j, :, ct, :] = np.where(vis, 0.0, NEG)
        cur = (tg // 64)[:, None]
        visible = (jj <= cur) & (jj < n_sel)
        forced = (jj == 0) | (jj >= cur - 1)
        vf[j, :, 0, :] = np.where(visible & ~forced, 1.0, 0.0)
        vf[j, :, 1, :] = np.where(~visible, -1e30, np.where(forced, 1000.0 + jj, 0.0))
    return dm.astype(NPBF), wm.astype(NPBF), cm.astype(NPBF), vf


def even_consts(S):
    NT, NO, NOW, NC, n_cmp, NCT, n_sel = geom(S)
    NCP = NCT * 128
    cpos = 16 * np.arange(NCP) + 31
    cc, cs = rope_tables_fm(cpos, 128)
    ctab = np.stack([cc, cs], axis=1)
    cmp_start = np.arange(n_cmp) * 16
    sel_start = np.arange(n_sel) * 64
    ov = np.clip(np.minimum(cmp_start[:, None] + 32, sel_start[None, :] + 64)
                 - np.maximum(cmp_start[:, None], sel_start[None, :]), 0, None) / 32.0
    ovp = np.zeros((NCP, 128), np.float32)
    ovp[:n_cmp, :n_sel] = ov
    ovm = ovp.reshape(NCT, 128, 128).transpose(1, 0, 2).copy().astype(NPBF)
    expand = np.zeros((128, 32, 128), np.float32)
    for k in range(32):
        for s in range(128):
            expand[2 * k + s // 64, k, s] = 1.0
            expand[64 + 2 * k + s // 64, k, s] = 1.0
    return ctab, ovm, expand.astype(NPBF)


def fm_gather(fmos, slots, S):
    NT = S // 128
    NO = NT // 4
    out = np.zeros((128, len(slots), NT, 128), dtype=fmos[0].dtype)
    for r in range(4):
        v = fmos[r][:, slots, :].reshape(128, len(slots), NO, 128)
        out[:, :, r::4, :] = v
    return out.reshape(128, len(slots), S)


def tm_gather_vext(tmos, col0, S):
    NT = S // 128
    NO = NT // 4
    out = np.ones((128, NT, 129), dtype=tmos[0].dtype)
    for r in range(4):
        v = tmos[r][:, col0:col0 + 128].reshape(NO, 128, 128)
        out[:, r::4, 0:128] = v.transpose(1, 0, 2)
    return out.reshape(128, NT * 129)


def memg_host(mem_norm_gain, mem_k_gain):
    m = np.zeros((128, KC + 1), np.float32)
    m[:, :KC] = mem_norm_gain.reshape(KC, 128).T
    m[:, KC] = mem_k_gain
    return m


def a_inputs(parity, S, x_own, w_in, norm_gain, gain_cols, r):
    m = dict(x=np.ascontiguousarray(x_own), w_in=w_in, ng=np.ascontiguousarray(norm_gain.reshape(KC, 128).T),
             gains=np.ascontiguousarray(np.stack(gain_cols, axis=1).astype(np.float32)),
             tabs=tabs_host(S, r), cmat=cmat_host())
    if parity == 1:
        sel = np.zeros((16, 8, 128), np.float32)
        for c in range(8):
            for mm_ in range(128):
                sel[2 * c + mm_ // 64, c, mm_] = 1
        m['sel'] = sel.astype(NPBF)
    return m


def b_even_inputs(S, r, x_own, fmo_b, tmo_b, tmf_own, w_out, mem_b, mem_norm_gain, w_kv, mem_qk_gain,
                  nsa_qk_gain, cmp_pos, cmp_w1, cmp_w2, subln, lam_vecs, shared, lambda_init):
    fmo = fmo_b[r]
    tmo = tmo_b[r]
    dm, wm, cm, vf = mask_consts(S, r)
    gv = np.zeros((128, 388), np.float32)
    gv[:, 0] = nsa_qk_gain[1]
    gv[:, 2:130] = subln[None, :]
    gv[:, 130:386] = lam_vecs.reshape(1, 256)
    gv[:, 386] = lambda_init
    gv[:, 387] = 1.0 - lambda_init
    m = dict(x=np.ascontiguousarray(x_own), w_out=w_out, mem=mem_b, w_kv=w_kv,
             memg=memg_host(mem_norm_gain, mem_qk_gain[1]), cmat=cmat_host(), dm=dm,
             fmq=np.ascontiguousarray(np.concatenate([fmo[:, 0:8], fmo[:, 16:20], fmo[:, 24:28]], axis=1)),
             tmz=np.ascontiguousarray(np.concatenate([tmo[:, 512:1536], tmo[:, 2048:2560], tmo[:, 2560:3072]], axis=1)),
             kg=shared['kg'], vg=shared['vg'], ag=np.ascontiguousarray(tmf_own[:, :24]),
             cw1=cmp_w1, cw2=np.ascontiguousarray(cmp_w2.reshape(2, 2, 128, 128).transpose(2, 0, 1, 3)),
             peT=np.ascontiguousarray(cmp_pos.transpose(2, 0, 1)).astype(NPBF),
             ctab=shared['ctab'], ovm=shared['ovm'], expand=shared['expand'], wm=wm, cm=cm, visfrc=vf, gv=gv)
    return m


def b_even_shared(S, fmo_b, tmo_b):
    ctab, ovm, expand = even_consts(S)
    kg = fm_gather(fmo_b, [8, 9, 10, 11, 12, 13, 14, 15, 20, 21, 22, 23], S)
    vcols = [0, 128, 256, 384, 1536, 1664, 1792, 1920]
    vg = np.stack([tm_gather_vext(tmo_b, c0, S) for c0 in vcols], axis=0)
    return dict(kg=kg, vg=vg, ctab=ctab, ovm=ovm, expand=expand)


def build_B_odd(S, fx=None):
    NT, NO, NOW, NC, n_cmp, NCT, n_sel = geom(S)
    TOPK = min(256, S // 4)
    NIT = 16
    nc = bass.Bass("TRN2", target_bir_lowering=False) if fx is None else fx.nc
    with ExitStack() as es:
        P = Prog(nc) if fx is None else fx.P
        c = b_common(nc, es, P, S, 1, fx)
        if fx is None:
            kg = c.ext("kg", [128, 5, S], BF16)
            vg = c.ext("vg", [4, 128, NT * 129], BF16)
            sg = c.ext("sg", [NOW, 16], F32)
            mscr = nc.dram_tensor("mscr", [NO, 128, S], BF16, kind="Internal").ap()
        else:
            kg = vg = None
            sg = c.tmf
            mscr = fx.internal("mscr", [NO, 128, S], BF16)
        dmt = c.ext("dmt", [128, 512], F32)

        IKT = c.qT[:].rearrange("p a b -> p (a b)")[:, 0:S]
        dmt_t = sb(nc, es, "dmt_t", [128, 512], F32)
        mnp = Pool(nc, es, "mnp", 2, [128, S], BF16)
        iqp = Pool(nc, es, "iqp", 2, [128, 8, 128], BF16)
        sgp = Pool(nc, es, "sgp", 2, [128, 16], F32)
        rp = Pool(nc, es, "rp", 7, [128, 512], BF16)
        dgp = Pool(nc, es, "dgp", 2, [128, 16, 128], BF16)
        thr = Pool(nc, es, "thr", 2, [128, 8], F32)
        scr_bufs = [c.big[:, 0:2 * S].bitcast(F32)]
        scr_guard = [['bufK', 'bufV']]
        if NO >= 2:
            scr_bufs.append(c.y_all[:, 0:NO // 2, :].rearrange("p a b -> p (a b)")[:, 0:2 * S].bitcast(F32))
            scr_guard.append([('y', j_) for j_ in range(NO // 2)])

        P.dma(dmt_t[:], dmt, w=['dmt'])
        b_memory_kv(P, c)
        if fx is None:
            P.dma(IKT, kg[:, 4, :], w=['qT'])
        else:
            dv_ = IKT.rearrange("d (j r p) -> d j r p", r=4, p=128)
            for r_ in range(4):
                P.dma(dv_[:, :, r_, :], c.kall_fm_v[:, 4, :, r_, :], w=['qT'])
        for sb_, sg_ in zip(scr_bufs, scr_guard):
            P.memset('dve', sb_[:, 0:1], 0.0, w=sg_)

        psl = [(c.stp.t[0], ('stp', 0)), (c.stp.t[1], ('stp', 1)), (c.obank[0], ('ob', 0)), (c.obank[1], ('ob', 1)), (c.psm.t[0], ('psm', 0))]
        psl = psl + [(c.pst.t[0][:].rearrange("p a b -> p (a b)").bitcast(F32), ('pst', 0))]
        accl = [(c.obank[2], ('ob', 2)), (c.obank[3], ('ob', 3))]
        psi = [0]
        acci = [0]
        for j in range(NO):
            L = (4 * j + 4) * 128
            sbi = j % len(scr_bufs)
            scr_t = scr_bufs[sbi]
            BB = scr_guard[sbi]
            iq, iqk = iqp.next()
            P.dma(iq[:], c.fmq[:, 12:20, j * 128:(j + 1) * 128], w=[iqk])
            sgt, sgk = sgp.next()
            P.dma(sgt[:], sg[j * 128:(j + 1) * 128, :], w=[sgk])
            dg, dgk = dgp.next()
            for hh in range(16):
                P.ts('pool', dg[:, hh, :], c.ident, sgt[:, hh:hh + 1], None, ALU.mult,
                     r=['cmat', sgk], w=[(dgk, hh)])
            pend = []

            def do_item(item):
                hh_, R__, Rk_, acc_ps_, acck_, cols_, kc5_ = item
                P.mm(acc_ps_[:], dg[:, hh_, :], R__[:], start=(hh_ == 0), stop=(hh_ == 15),
                     r=[(dgk, hh_), Rk_], w=[acck_])
                if hh_ == 15:
                    P.act(scr_t[:, cols_], acc_ps_[:], AF.Copy, r=[acck_] + BB, w=[('scr', sbi, kc5_)])
            for kc5 in range(L // 512):
                cols = slice(kc5 * 512, (kc5 + 1) * 512)
                acc_ps, acck = accl[acci[0] % 2]
                acci[0] += 1
                for hp in range(8):
                    pair = []
                    for hh in (2 * hp, 2 * hp + 1):
                        rows = slice((hh % 2) * 64, (hh % 2) * 64 + 64)
                        ps, psk = psl[psi[0] % len(psl)]
                        psi[0] += 1
                        P.mm(ps[:], iq[rows, hh // 2, :], IKT[rows, cols], r=[iqk, 'qT'], w=[psk])
                        pair.append((hh, ps, psk))
                    for hh, ps, psk in pair:
                        R_, Rk = rp.next()
                        P.act(R_[:], ps[:], AF.Relu, r=[psk], w=[Rk])
                        pend.append((hh, R_, Rk, acc_ps, acck, cols, kc5))
                    while len(pend) > 4:
                        do_item(pend.pop(0))
            while pend:
                do_item(pend.pop(0))
            last = L // 512 - 1
            P.tt('dve', scr_t[:, L - 512:L], scr_t[:, L - 512:L], dmt_t[:], ALU.add, r=[('scr', sbi, last), 'dmt'] + BB,
                 w=[('scr', sbi, last)])
            skeys = [('scr', sbi, k) for k in range(L // 512)] + BB
            th, thk = thr.next()
            mn, mnk = mnp.next()
            P.op('dve', lambda e, a=th, LL=L, sc_=scr_t: e.reduce_max(a[:, 0:1], sc_[:, 0:LL], AX.X), r=skeys, w=[thk])
            P.ts('dve', th[:, 1:2], th[:, 0:1], -16.0, None, ALU.add, r=[thk], w=[thk])
            for k in range(NIT):
                ck = 8.0 / (2 ** k)
                P.ts('dve', th[:, 2:3], th[:, 1:2], ck, None, ALU.add, r=[thk], w=[thk])
                P.op('dve', lambda e, a=th, m_=mn, LL=L, sc_=scr_t: e.tensor_scalar(m_[:, 0:LL], sc_[:, 0:LL], a[:, 2:3], None,
                                                                      ALU.is_ge, ALU.add, accum_out=a[:, 3:4]),
                     r=skeys + [thk], w=[thk, mnk])
                P.ts('dve', th[:, 4:5], th[:, 3:4], float(TOPK) - 0.5, ck, ALU.is_ge, ALU.mult, r=[thk], w=[thk])
                P.tt('dve', th[:, 1:2], th[:, 1:2], th[:, 4:5], ALU.add, r=[thk], w=[thk])
            P.ts('dve', mn[:, 0:L], scr_t[:, 0:L], th[:, 1:2], NEG, ALU.is_lt, ALU.mult, r=skeys + [thk], w=[mnk])
            P.dma(mscr[j][:, 0:L], mn[:, 0:L], r=[mnk], w=[('mscr', j)])

        sc = 128 ** -0.5
        ob = c.obank
        for g in range(4):
            P.dma(c.qT[:, 0:3, :], c.fmq[:, 3 * g:3 * g + 3, :], w=['qT'])
            c.load_k(g, kg)
            c.load_v(g, vg)
            vS = c.bufV[:, 0:NT * 129].rearrange("p (t d) -> p t d", d=129)
            for j in range(NO):
                L = (4 * j + 4) * 128
                mn, mnk = mnp.next()
                P.dma(mn[:, 0:L], mscr[j][:, 0:L], r=[('mscr', j)], w=[mnk])
                qj = c.qT[:, 0:3, j * 128:(j + 1) * 128]
                steps = []
                nk = 4 * j + 4
                for kt in range(nk):
                    masks = [(h * 128, 128, mn[:, kt * 128:(kt + 1) * 128], c.ident, [mnk, 'cmat']) for h in range(3)]
                    pv = [(h * 128, ob[h][:, 0:129], ('ob', h), vS[:, kt, :], ['bufV'], kt == 0, kt == nk - 1)
                          for h in range(3)]
                    steps.append(dict(k=c.bufK[:, kt * 128:(kt + 1) * 128], kkeys=['bufK'], q=qj, qkeys=['qT'],
                                      c0=0, n=384, masks=masks, pv=pv))
                attend(P, c, steps, sc)
                zt, ztk = c.zt_for(j, g * 384, 384)
                for h in range(3):
                    s_, sk = c.sm.next()
                    P.ts('dve', s_[:, 0:1], ob[h][:, 128:129], 1e-30, None, ALU.max, r=[('ob', h)], w=[sk])
                    P.op('dve', lambda e, a=s_: e.reciprocal(a[:, 0:1], a[:, 0:1]), r=[sk], w=[sk])
                    col = (g * 3 + h) * 128
                    P.stt('dve', c.y_all[:, j, col:col + 128], ob[h][:, 0:128], s_[:, 0:1], zt[:, h * 128:(h + 1) * 128],
                          ALU.mult, ALU.mult, r=[('ob', h), sk, ztk], w=[('y', j)])

        b_mem_attention(P, c, 20, 1536, 1536)
        b_out_proj(P, c)
        if fx is None:
            P.emit(es)
    return nc


def b_odd_shared(S, fmo_b, tmo_b):
    kg = fm_gather(fmo_b, [12, 13, 14, 15, 24], S)
    vg = np.stack([tm_gather_vext(tmo_b, h * 128, S) for h in range(4)], axis=0)
    return dict(kg=kg, vg=vg)


def b_odd_inputs(S, r, x_own, fmo_b, tmo_b, tmf_own, w_out, mem_b, mem_norm_gain, w_kv, mem_qk_gain, shared):
    fmo = fmo_b[r]
    tmo = tmo_b[r]
    dm, wm, cm, vf = mask_consts(S, r)
    t = np.arange(128)[:, None]
    dmt = np.zeros((128, 4, 128), np.float32)
    s = np.arange(128)[None, :]
    for u in range(4):
        if u == r:
            dmt[:, u, :] = np.where(s <= t, 0.0, -1e30)
        elif u > r:
            dmt[:, u, :] = -1e30
    m = dict(x=np.ascontiguousarray(x_own), w_out=w_out, mem=mem_b, w_kv=w_kv,
             memg=memg_host(mem_norm_gain, mem_qk_gain[1]), cmat=cmat_host(), dm=dm,
             fmq=np.ascontiguousarray(np.concatenate([fmo[:, 0:12], fmo[:, 16:24], fmo[:, 25:29]], axis=1)),
             tmz=np.ascontiguousarray(tmo[:, 512:2560]),
             kg=shared['kg'], vg=shared['vg'], sg=np.ascontiguousarray(tmf_own[:, :16]),
             dmt=dmt.reshape(128, 512))
    return m


_PROGS = {}


def _prog(name, S):
    key = (name, S)
    if key not in _PROGS:
        if name == 'A0':
            _PROGS[key] = build_A(0, S)
        elif name == 'A1':
            _PROGS[key] = build_A(1, S)
        elif name == 'B0':
            _PROGS[key] = build_B_even(S)
        else:
            _PROGS[key] = build_B_odd(S)
    return _PROGS[key]


def run_layer(S, layer, xb, inp):
    parity = layer % 2
    e = layer // 2
    f32 = lambda a: np.ascontiguousarray(np.asarray(a, dtype=np.float32))
    mg = f32(inp['mem_qk_gain'][layer])
    if parity == 0:
        g = f32(inp['nsa_qk_gain'][e])
        dg = f32(inp['diff_qk_gain'][e])
        gcols = [g[0], g[2], g[3], np.tile(dg[0], 2), np.tile(dg[1], 2), mg[0]]
        w_in = f32(inp['even_w_in'][e])
    else:
        g = f32(inp['dsa_qk_gain'][e])
        gcols = [g[0], g[1], mg[0]]
        w_in = f32(inp['odd_w_in'][e])
    ng = f32(inp['norm_gain'][layer])
    owns = [[np.ascontiguousarray(xb[b][own_positions(S, r)]) for r in range(4)] for b in range(2)]
    maps = [a_inputs(parity, S, owns[cc // 4][cc % 4], w_in, ng, gcols, cc % 4) for cc in range(8)]
    resA = run_bass_kernel_spmd(_prog('A%d' % parity, S), maps, core_ids=list(range(8))).results
    w_out = f32(inp['w_out'][layer])
    w_kv = f32(inp['mem_w_kv'][layer])
    mng = f32(inp['mem_norm_gain'])
    maps = []
    for b in range(2):
        fmo_b = [np.asarray(resA[b * 4 + r]['fmo']) for r in range(4)]
        tmo_b = [np.asarray(resA[b * 4 + r]['tmo']) for r in range(4)]
        tmf_b = [np.asarray(resA[b * 4 + r]['tmf']) for r in range(4)]
        mem_b = f32(inp['mem'][b])
        if parity == 0:
            shared = b_even_shared(S, fmo_b, tmo_b)
            lambda_init = 0.8 - 0.6 * math.exp(-0.3 * layer)
            for r in range(4):
                maps.append(b_even_inputs(S, r, owns[b][r], fmo_b, tmo_b, tmf_b[r], w_out, mem_b, mng, w_kv, mg,
                                          f32(inp['nsa_qk_gain'][e]), f32(inp['nsa_cmp_pos'][e]), f32(inp['nsa_cmp_w1'][e]),
                                          f32(inp['nsa_cmp_w2'][e]), f32(inp['diff_subln_gain'][e]),
                                          f32(inp['diff_lambda'][e]), shared, lambda_init))
        else:
            shared = b_odd_shared(S, fmo_b, tmo_b)
            for r in range(4):
                maps.append(b_odd_inputs(S, r, owns[b][r], fmo_b, tmo_b, tmf_b[r], w_out, mem_b, mng, w_kv, mg, shared))
    resB = run_bass_kernel_spmd(_prog('B%d' % parity, S), maps, core_ids=list(range(8))).results
    out = []
    for b in range(2):
        xn = np.empty((S, D), np.float32)
        for r in range(4):
            xn[own_positions(S, r)] = np.asarray(resB[b * 4 + r]['xo'])
        out.append(xn)
    return out


def kernel_unfused(**inputs):
    x = np.asarray(inputs['x'], dtype=np.float32)
    S = x.shape[1]
    xb = [np.ascontiguousarray(x[b]) for b in range(x.shape[0])]
    for layer in range(4):
        xb = run_layer(S, layer, xb, inputs)
    return np.stack(xb, axis=0).astype(np.float32)


def kernel(**inputs):
    return kernel_fused(inputs, 4).astype(np.float32)


def build_fused(S, depth=4):
    NT, NO, NOW, NC, n_cmp, NCT, n_sel = geom(S)
    nc = bass.Bass("TRN2", target_bir_lowering=False)
    P = Prog(nc)
    fx = FX(nc, P, S)
    x_in = nc.dram_tensor("x", [NOW, D], F32, kind="ExternalInput").ap()
    out = nc.dram_tensor("out", [NOW, D], F32, kind="ExternalOutput").ap()
    xs = [x_in] + [nc.dram_tensor(f"xs{l}", [NOW, D], F32, kind="Internal").ap() for l in range(1, depth)] + [out]
    groups = [[0, 1, 2, 3], [4, 5, 6, 7]]
    ccsrc = nc.dram_tensor("ccsrc", [256, 2048], BF16, kind="Internal").ap()
    ccdst = nc.dram_tensor("ccdst", [1024, 2048], BF16, kind="Internal").ap()
    for layer in range(depth):
        parity = layer % 2
        fx.layer = layer
        fx.x_src = xs[layer]
        fx.x_dst = xs[layer + 1]
        build_A(parity, S, fx)
        P.barrier()
        kown_fm = fx.internal("kown_fm", [NK[parity] * 128, NOW], BF16)
        kall_fm = fx.internal("kall_fm", [4 * NK[parity] * 128, NOW], BF16)
        kown_v = fx.internal("kown_v", [NV[parity] * 128, NO * 129], BF16)
        kall_v = fx.internal("kall_v", [4 * NV[parity] * 128, NO * 129], BF16)
        def gather(src_rows, dst_view, R_, C_):
            sv = ccsrc.rearrange("a b -> (a b)")[0:R_ * C_].rearrange("(r c) -> r c", c=C_)
            dv = ccdst.rearrange("a b -> (a b)")[0:4 * R_ * C_].rearrange("(r c) -> r c", c=C_)
            P.dma(sv, src_rows, r=['ccdst'], w=['ccsrc'])
            P.op('pool', lambda e, a=sv, b=dv: e.collective_compute(
                "AllGather", ALU.bypass, replica_groups=groups, ins=[a], outs=[b]), r=['ccsrc'], w=['ccdst'])
            P.dma(dst_view, dv.rearrange("(k r) c -> k r c", k=4), r=['ccdst'], w=[('kall', layer)])
        nk_, nv_ = NK[parity], NV[parity]
        kall_fm_k = kall_fm.rearrange("(k sd) n -> k sd n", k=4)
        for s0 in range(0, nk_, 2):
            ns = min(2, nk_ - s0)
            gather(kown_fm[s0 * 128:(s0 + ns) * 128, :], kall_fm_k[:, s0 * 128:(s0 + ns) * 128, :], ns * 128, NOW)
        kall_v_k = kall_v.rearrange("(k up) n -> k up n", k=4)
        for u in range(nv_):
            gather(kown_v[u * 128:(u + 1) * 128, :], kall_v_k[:, u * 128:(u + 1) * 128, :], 128, NO * 129)
        P.barrier()
        if parity == 0:
            build_B_even(S, fx)
        else:
            build_B_odd(S, fx)
        P.barrier()
    with ExitStack() as es:
        P.emit(es)
    return nc, fx


def fused_inputs(S, inp, depth=4):
    f32 = lambda a: np.ascontiguousarray(np.asarray(a, dtype=np.float32))
    x = f32(inp['x'])
    ctab, ovm, expand = even_consts(S)
    sel = np.zeros((16, 8, 128), np.float32)
    for cc_ in range(8):
        for mm_ in range(128):
            sel[2 * cc_ + mm_ // 64, cc_, mm_] = 1
    shared = dict(cmat=cmat_host(), ctab=ctab, ovm=ovm, expand=expand, sel=sel.astype(NPBF))
    mng = f32(inp['mem_norm_gain'])
    for layer in range(depth):
        e = layer // 2
        L = f"_L{layer}"
        mg = f32(inp['mem_qk_gain'][layer])
        shared['w_out' + L] = f32(inp['w_out'][layer])
        shared['w_kv' + L] = f32(inp['mem_w_kv'][layer])
        shared['memg' + L] = memg_host(mng, mg[1])
        shared['ng' + L] = np.ascontiguousarray(f32(inp['norm_gain'][layer]).reshape(KC, 128).T)
        if layer % 2 == 0:
            g = f32(inp['nsa_qk_gain'][e])
            dg = f32(inp['diff_qk_gain'][e])
            gcols = [g[0], g[2], g[3], np.tile(dg[0], 2), np.tile(dg[1], 2), mg[0]]
            shared['w_in' + L] = f32(inp['even_w_in'][e])
            lambda_init = 0.8 - 0.6 * math.exp(-0.3 * layer)
            gv = np.zeros((128, 388), np.float32)
            gv[:, 0] = g[1]
            gv[:, 2:130] = f32(inp['diff_subln_gain'][e])[None, :]
            gv[:, 130:386] = f32(inp['diff_lambda'][e]).reshape(1, 256)
            gv[:, 386] = lambda_init
            gv[:, 387] = 1.0 - lambda_init
            shared['gv' + L] = gv
            shared['cw1' + L] = f32(inp['nsa_cmp_w1'][e])
            shared['cw2' + L] = np.ascontiguousarray(f32(inp['nsa_cmp_w2'][e]).reshape(2, 2, 128, 128).transpose(2, 0, 1, 3))
            shared['peT' + L] = np.ascontiguousarray(f32(inp['nsa_cmp_pos'][e]).transpose(2, 0, 1)).astype(NPBF)
        else:
            g = f32(inp['dsa_qk_gain'][e])
            gcols = [g[0], g[1], mg[0]]
            shared['w_in' + L] = f32(inp['odd_w_in'][e])
        shared['gains' + L] = np.ascontiguousarray(np.stack(gcols, axis=1).astype(np.float32))
    maps = []
    for core in range(8):
        b, r = core // 4, core % 4
        m = dict(shared)
        m['x'] = np.ascontiguousarray(x[b][own_positions(S, r)])
        m['mem'] = f32(inp['mem'][b])
        m['tabs'] = tabs_host(S, r)
        dm, wm, cm, vf = mask_consts(S, r)
        m.update(dm=dm, wm=wm, cm=cm, visfrc=vf)
        t = np.arange(128)[:, None]
        s_ = np.arange(128)[None, :]
        dmt = np.zeros((128, 4, 128), np.float32)
        for u in range(4):
            if u == r:
                dmt[:, u, :] = np.where(s_ <= t, 0.0, -1e30)
            elif u > r:
                dmt[:, u, :] = -1e30
        m['dmt'] = dmt.reshape(128, 512)
        maps.append(m)
    return maps


_FUSED = {}


def kernel_fused(inputs, depth=4):
    x = np.asarray(inputs['x'], dtype=np.float32)
    S = x.shape[1]
    import time as _t
    t0_ = _t.time()
    key = (S, depth)
    if key not in _FUSED:
        _FUSED[key] = build_fused(S, depth)
    nc, fx = _FUSED[key]
    print("[fused] build", round(_t.time() - t0_, 1), flush=True)
    maps = fused_inputs(S, inputs, depth)
    print("[fused] inputs", round(_t.time() - t0_, 1), flush=True)
    names = set(fx.cache.keys())
    maps = [{k: v for k, v in m.items() if k == 'x' or (k in fx.ext_shapes)} for m in maps]
    res = run_bass_kernel_spmd(nc, maps, core_ids=list(range(8))).results
    print("[fused] run", round(_t.time() - t0_, 1), flush=True)
    out = np.empty_like(x)
    for core in range(8):
        b, r = core // 4, core % 4
        out[b][own_positions(S, r)] = np.asarray(res[core]['out'])
    return out
```

```python
import math
from contextlib import ExitStack
import numpy as np
import ml_dtypes
import concourse.bass as bass
import concourse.mybir as mybir
from concourse.bass_utils import run_bass_kernel_spmd

F32 = mybir.dt.float32
BF16 = mybir.dt.bfloat16
ALU = mybir.AluOpType
AF = mybir.ActivationFunctionType
AX = mybir.AxisListType
NPBF = ml_dtypes.bfloat16

D = 2048
KC = D // 128
EPS = 1e-6
NEG = -30000.0
ROPE_THETA = 10000.0

EVEN_SPLITS = (1024, 1536, 24, 1024, 512, 512, 512, 512, 512, 512)
ODD_SPLITS = (1536, 512, 512, 1024, 64, 16, 1536, 512, 512)


class Prog:
    def __init__(self, nc):
        self.nc = nc
        self.ins = []
        self.last_w = {}
        self.readers = {}
        self.pending = {}
        self.last_on = {}
        self.recent_dma = []
        self.recent_cc = []

    def barrier(self):
        deps = set(self.last_on.values()) | set(self.recent_dma[-12:]) | set(self.recent_cc[-4:])
        for e in ('pe', 'act', 'dve', 'pool', 'sp'):
            self.pending[e] = set(deps) | self.pending.get(e, set())

    def op(self, eng, fn, r=(), w=(), cc=False):
        i = len(self.ins)
        deps = set()
        if eng in self.pending:
            deps |= self.pending.pop(eng)
        self.last_on[eng] = i
        if eng == 'sp':
            self.recent_dma.append(i)
        if cc:
            self.recent_cc.append(i)
        for k in r:
            if k in self.last_w:
                deps.add(self.last_w[k])
        for k in w:
            if k in self.last_w:
                deps.add(self.last_w[k])
            deps.update(self.readers.get(k, ()))
        self.ins.append([eng, fn, deps, cc])
        for k in r:
            self.readers.setdefault(k, []).append(i)
        for k in w:
            self.last_w[k] = i
            self.readers[k] = []
        return i

    def mm(self, out, lhsT, rhs, start=True, stop=True, r=(), w=()):
        return self.op('pe', lambda e: e.matmul(out, lhsT, rhs, start=start, stop=stop), r, w)

    def tr(self, out, in_, ident, r=(), w=()):
        return self.op('pe', lambda e: e.transpose(out, in_, ident), r, w)

    def act(self, out, in_, func, r=(), w=(), **kw):
        return self.op('act', lambda e: e.activation(out, in_, func, **kw), r, w)

    def dma(self, out, in_, r=(), w=(), q='sp'):
        return self.op(q, lambda e: e.dma_start(out=out, in_=in_), r, w)

    def ts(self, eng, out, in0, s1, s2, op0, op1=None, r=(), w=(), **kw):
        if op1 is None:
            return self.op(eng, lambda e: e.tensor_scalar(out, in0, s1, None, op0, **kw), r, w)
        return self.op(eng, lambda e: e.tensor_scalar(out, in0, s1, s2, op0, op1, **kw), r, w)

    def tt(self, eng, out, in0, in1, op, r=(), w=()):
        return self.op(eng, lambda e: e.tensor_tensor(out, in0, in1, op), r, w)

    def stt(self, eng, out, in0, scalar, in1, op0, op1, r=(), w=()):
        return self.op(eng, lambda e: e.scalar_tensor_tensor(out, in0, scalar, in1, op0, op1), r, w)

    def cp(self, eng, out, in_, r=(), w=()):
        return self.op(eng, lambda e: e.tensor_copy(out, in_), r, w)

    def memset(self, eng, ap, val, r=(), w=()):
        return self.op(eng, lambda e: e.memset(ap, val), r, w)

    def emit(self, es):
        nc = self.nc
        ENGS = ['pe', 'act', 'dve', 'pool', 'sp', 'dq']
        ND = 12
        SEM_LIM = 30000
        n = len(self.ins)
        eng_of = [x[0] for x in self.ins]
        needed = [False] * n
        for i, (eng, fn, deps, cc) in enumerate(self.ins):
            if eng == 'pe':
                deps = {d for d in deps if eng_of[d] != 'pe'}
                self.ins[i][2] = deps
            for d in deps:
                needed[d] = True
        comp = [None] * n
        cnt = {e: 0 for e in ENGS}
        nsem_needed = {e: 1 for e in ENGS}
        dma_k = {'sp': 0, 'dq': 0}
        cc_k = 0
        NCC = 4
        for i, (eng, fn, deps, cc) in enumerate(self.ins):
            if cc:
                comp[i] = ('dcc', cc_k % NCC, 16 * (cc_k // NCC + 1))
                cc_k += 1
            elif eng in ('sp', 'dq'):
                k = dma_k[eng]
                dma_k[eng] += 1
                comp[i] = ('d' + eng, k % ND, 16 * (k // ND + 1))
            elif needed[i]:
                c = cnt[eng]
                cnt[eng] += 1
                comp[i] = (eng, c // SEM_LIM, c % SEM_LIM + 1)
                nsem_needed[eng] = c // SEM_LIM + 1
        sems = {}
        for e in ('pe', 'act', 'dve', 'pool'):
            for s in range(nsem_needed[e]):
                sems[(e, s)] = es.enter_context(nc.semaphore(f"s_{e}{s}"))
        for q in ('sp', 'dq'):
            if dma_k[q]:
                for s in range(ND):
                    sems[('d' + q, s)] = es.enter_context(nc.semaphore(f"s_{q}{s}"))
        if cc_k:
            for s in range(NCC):
                sems[('dcc', s)] = es.enter_context(nc.semaphore(f"s_cc{s}"))
        order = {e: [] for e in ENGS}
        for i in range(n):
            order[eng_of[i]].append(i)
        ins = self.ins
        final_dma = {q: dict() for q in ('sp', 'dq')}
        for i in range(n):
            if eng_of[i] in ('sp', 'dq'):
                c = comp[i]
                final_dma[eng_of[i]][(c[0], c[1])] = c[2]

        def run(engname, e):
            seen = {}
            dk = 0
            for i in order[engname]:
                eng, fn, deps, cc = ins[i]
                waits = {}
                for d in deps:
                    c = comp[d]
                    key = (c[0], c[1])
                    if waits.get(key, 0) < c[2]:
                        waits[key] = c[2]
                if engname in ('sp', 'dq') or cc:
                    c = comp[i]
                    if c[2] > 16:
                        key = (c[0], c[1])
                        if waits.get(key, 0) < c[2] - 16:
                            waits[key] = c[2] - 16
                for key, v in waits.items():
                    if seen.get(key, 0) >= v:
                        continue
                    seen[key] = v
                    e.wait_ge(sems[key], v)
                inst = fn(e)
                c = comp[i]
                if c is not None:
                    inst.then_inc(sems[(c[0], c[1])], 16 if (engname in ('sp', 'dq') or cc) else 1)
            if engname in ('sp', 'dq'):
                for key, v in final_dma[engname].items():
                    if seen.get(key, 0) < v:
                        e.wait_ge(sems[key], v)

        with nc.Block() as block:
            if order['sp']:
                @block.sync
                def _(e):
                    run('sp', e)
            if order['pe']:
                @block.tensor
                def _(e):
                    run('pe', e)
            if order['act'] or order['dq']:
                @block.scalar
                def _(e):
                    run('act', e)
            if order['dve']:
                @block.vector
                def _(e):
                    run('dve', e)
            if order['pool']:
                @block.gpsimd
                def _(e):
                    run('pool', e)


_UID = [0]


def _uname(name):
    _UID[0] += 1
    return f"{name}_{_UID[0]}"


class Pool:
    def __init__(self, nc, es, name, n, shape, dtype, psum=False):
        self.name = name
        name = _uname(name)
        self.n = n
        self.i = 0
        self.t = []
        for k in range(n):
            if psum:
                self.t.append(es.enter_context(nc.psum_tensor(f"{name}{k}", shape, dtype)))
            else:
                self.t.append(es.enter_context(nc.sbuf_tensor(f"{name}{k}", shape, dtype)))

    def next(self):
        k = self.i % self.n
        self.i += 1
        return self.t[k], (self.name, k)


def sb(nc, es, name, shape, dtype):
    return es.enter_context(nc.sbuf_tensor(_uname(name), shape, dtype))


class FX:
    def __init__(self, nc, P, S):
        self.nc, self.P, self.S = nc, P, S
        self.layer = 0
        self.cache = {}
        self.ext_shapes = {}

    def ext(self, name, shape, dtype, per_layer=False):
        nm = f"{name}_L{self.layer}" if per_layer else name
        if nm not in self.cache:
            self.cache[nm] = self.nc.dram_tensor(nm, shape, dtype, kind="ExternalInput").ap()
            self.ext_shapes[nm] = (name, per_layer, self.layer)
        return self.cache[nm]

    def internal(self, name, shape, dtype):
        nm = f"{name}_L{self.layer}"
        if nm not in self.cache:
            self.cache[nm] = self.nc.dram_tensor(nm, shape, dtype, kind="Internal").ap()
        return self.cache[nm]


def fm_map(parity, slot):
    if parity == 0:
        if slot < 8:
            return ('q', slot)
        if slot < 16:
            return ('k', slot - 8)
        if slot < 20:
            return ('q', 8 + slot - 16)
        if slot < 24:
            return ('k', 8 + slot - 20)
        return ('q', 12 + slot - 24)
    if slot < 12:
        return ('q', slot)
    if slot < 16:
        return ('k', slot - 12)
    if slot < 24:
        return ('q', 12 + slot - 16)
    if slot == 24:
        return ('k', 4)
    return ('q', 20 + slot - 25)


def tm_map(parity, dcol):
    if parity == 0:
        return {0: ('v', 0, 2), 256: ('v', 2, 2), 512: ('z', 0), 1024: ('z', 512), 1536: ('v', 4, 4),
                2048: ('z', 1024), 2560: ('z', 1536)}[dcol]
    return {0: ('v', 0, 4), 512: ('z', 0), 1024: ('z', 512), 1536: ('z', 1024), 2048: ('z', 1536)}[dcol]


NK = {0: 12, 1: 5}
NV = {0: 8, 1: 4}
NQS = {0: 16, 1: 24}


def own_positions(S, r):
    NT = S // 128
    NO = NT // 4
    pos = np.concatenate([np.arange(128) + (4 * j + r) * 128 for j in range(NO)])
    return pos


def rope_tables_fm(pos, dim):
    half = dim // 2
    inv = (ROPE_THETA ** (-np.arange(half, dtype=np.float32) / half)).astype(np.float32)
    ang = pos.astype(np.float32)[None, :] * inv[:, None]
    cos = np.cos(ang).astype(np.float32)
    sin = np.sin(ang).astype(np.float32)
    rows = np.arange(128) % half
    return cos[rows].astype(NPBF), sin[rows].astype(NPBF)


def rot_matrix_T(dim):
    half = dim // 2
    R = np.zeros((128, 128), np.float32)
    for m in range(128):
        blk = (m // dim) * dim
        w = m - blk
        if w < half:
            R[m, blk + w + half] = -1.0
        else:
            R[m, blk + w - half] = 1.0
    return R.T.copy().astype(NPBF)


def ones_block(dim):
    O = np.zeros((128, 128), np.float32)
    for m in range(128):
        blk = (m // dim) * dim
        O[blk:blk + dim, m] = 1.0 / dim
    return O.astype(NPBF)


def a_specs(parity):
    groups = []
    fm = [0]
    tmc = [0]
    tmf = [0]

    def fm_slot():
        fm[0] += 1
        return fm[0] - 1

    def add_fm_group(c0, nch, post, gain_idx, variant):
        for g0 in range(0, nch, 4):
            nn = min(4, nch - g0)
            items = [('fm', k * 128, post, gain_idx, variant, fm_slot()) for k in range(nn)]
            groups.append(dict(c0=c0 + g0 * 128, n=nn * 128, items=items))

    def add_tm(c0, ncols, func, dst='b'):
        for g0 in range(0, ncols, 512):
            nn = min(512, ncols - g0)
            if dst == 'b':
                dcol = tmc[0]
                tmc[0] += nn
            else:
                dcol = tmf[0]
                tmf[0] += nn
            groups.append(dict(c0=c0 + g0, n=nn, items=[('tm', 0, nn, func, dst, dcol)]))

    if parity == 0:
        c = 0
        add_fm_group(c, 8, 'nr', 0, 128); c += 1024
        add_fm_group(c, 4, 'raw', None, 128); c += 512
        groups.append(dict(c0=c, n=512, items=[('fm', 0, 'nr', 1, 128, fm_slot()), ('fm', 128, 'nr', 1, 128, fm_slot()),
                                               ('tm', 256, 256, 'copy', 'b', tmc[0])])); tmc[0] += 256; c += 512
        groups.append(dict(c0=c, n=512, items=[('fm', 0, 'nr', 2, 128, fm_slot()), ('fm', 128, 'nr', 2, 128, fm_slot()),
                                               ('tm', 256, 256, 'copy', 'b', tmc[0])])); tmc[0] += 256; c += 512
        add_tm(c, 24, 'sigmoid', 'f'); c += 24
        add_tm(c, 1024, 'silu'); c += 1024
        add_fm_group(c, 4, 'nr', 3, 64); c += 512
        add_fm_group(c, 4, 'nr', 4, 64); c += 512
        add_tm(c, 512, 'copy'); c += 512
        add_tm(c, 512, 'silu'); c += 512
        add_fm_group(c, 4, 'n', 5, 128); c += 512
        add_tm(c, 512, 'silu'); c += 512
        assert c == sum(EVEN_SPLITS)
    else:
        c = 0
        cq = c; c += 1536
        ck = c; c += 512
        cv = c; c += 512
        iq = c; c += 1024
        ik = c; c += 64
        iw = c; c += 16
        cz = c; c += 1536
        mq = c; c += 512
        mz = c; c += 512
        assert c == sum(ODD_SPLITS)
        groups.append(dict(c0=ik, n=80, items=[('tm', 64, 16, 'sign', 'f', 0), ('wrep',), ('fmrep', 0, 'r', None, 64, None)]))
        tmf[0] += 16
        add_fm_group(cq, 12, 'nr', 0, 128)
        add_fm_group(ck, 4, 'nr', 1, 128)
        add_tm(cv, 512, 'copy')
        for g0 in range(2):
            items = [('fm', k * 128, 'rw', None, 64, fm_slot(), g0 * 4 + k) for k in range(4)]
            groups.append(dict(c0=iq + g0 * 512, n=512, items=items))
        groups[0]['items'][2] = ('fmrep', 0, 'r', None, 64, fm_slot())
        add_tm(cz, 1536, 'silu')
        add_fm_group(mq, 4, 'n', 2, 128)
        add_tm(mz, 512, 'silu')
    return groups, fm[0], tmc[0], max(tmf[0], 1)


def build_A(parity, S, fx=None):
    NT = S // 128
    NO = NT // 4
    NOW = NO * 128
    NTG = NOW // 512
    groups, NFM, NTMC, NTMF = a_specs(parity)
    CIN = sum(EVEN_SPLITS) if parity == 0 else sum(ODD_SPLITS)
    NG = 6 if parity == 0 else 3
    if fx is None:
        nc = bass.Bass("TRN2", target_bir_lowering=False)
        x = nc.dram_tensor("x", [NOW, D], F32, kind="ExternalInput").ap()
        w_in = nc.dram_tensor("w_in", [D, CIN], F32, kind="ExternalInput").ap()
        ng = nc.dram_tensor("ng", [128, KC], F32, kind="ExternalInput").ap()
        gains = nc.dram_tensor("gains", [128, NG], F32, kind="ExternalInput").ap()
        tabs = nc.dram_tensor("tabs", [128, 4, NOW], BF16, kind="ExternalInput").ap()
        cmat = nc.dram_tensor("cmat", [128, 5, 128], BF16, kind="ExternalInput").ap()
        sel = nc.dram_tensor("sel", [16, 8, 128], BF16, kind="ExternalInput").ap() if parity == 1 else None
        fmo = nc.dram_tensor("fmo", [128, NFM, NOW], BF16, kind="ExternalOutput").ap()
        tmo = nc.dram_tensor("tmo", [NOW, NTMC], BF16, kind="ExternalOutput").ap()
        tmf = nc.dram_tensor("tmf", [NOW, NTMF], F32, kind="ExternalOutput").ap()
    else:
        nc = fx.nc
        x = fx.x_src
        w_in = fx.ext("w_in", [D, CIN], F32, True)
        ng = fx.ext("ng", [128, KC], F32, True)
        gains = fx.ext("gains", [128, NG], F32, True)
        tabs = fx.ext("tabs", [128, 4, NOW], BF16)
        cmat = fx.ext("cmat", [128, 5, 128], BF16)
        sel = fx.ext("sel", [16, 8, 128], BF16) if parity == 1 else None
        tmf = fx.internal("tmf", [NOW, NTMF], F32)
        fmq_d = fx.internal("fmq", [128, NQS[parity], NOW], BF16)
        kown_fm = fx.internal("kown_fm", [NK[parity] * 128, NOW], BF16)
        kown_v = fx.internal("kown_v", [NV[parity] * 128, NO * 129], BF16)
        tmz_d = fx.internal("tmz", [NOW, 2048], BF16)
        kown_fm_v = kown_fm.rearrange("(s d) n -> d s n", d=128)
        kown_v_v = kown_v.rearrange("(u p) (j d) -> p u j d", p=128, d=129)

    with ExitStack() as es:
        P = Prog(nc) if fx is None else fx.P
        hT = sb(nc, es, "hT", [128, KC, NOW], BF16)
        ng_t = sb(nc, es, "ng_t", [128, KC], F32)
        gains_t = sb(nc, es, "gains_t", [128, NG], F32)
        tabs_t = sb(nc, es, "tabs_t", [128, 4, NOW], BF16)
        cmat_t = sb(nc, es, "cmat_t", [128, 5, 128], BF16)
        eps_t = sb(nc, es, "eps_t", [128, 1], F32)
        xin = Pool(nc, es, "xin", 2, [128, D], F32)
        xs = Pool(nc, es, "xs", 1, [128, D], BF16)
        junk = sb(nc, es, "junk", [128, D], BF16)
        ssp = Pool(nc, es, "ss", 4, [128, 1], F32)
        wst = Pool(nc, es, "wst", 2, [128, KC // 2, 512], F32)
        wbf = Pool(nc, es, "wbf", 2, [128, KC, 512], BF16)
        psq = Pool(nc, es, "psq", 3, [128, 512], F32, psum=True)
        psb = Pool(nc, es, "psb", 2, [128, 512], F32, psum=True)
        psc = Pool(nc, es, "psc", 2, [128, 512], F32, psum=True)
        pst = Pool(nc, es, "pst", 1, [128, 8, 128], BF16, psum=True)
        sqb = Pool(nc, es, "sqb", 2, [128, 512], BF16)
        f1 = Pool(nc, es, "f1", 2, [128, 512], F32)
        f2 = Pool(nc, es, "f2", 2, [128, 512], F32)
        f3 = Pool(nc, es, "f3", 2, [128, 512], F32)
        qn = Pool(nc, es, "qn", 2, [128, 512], BF16)
        stg = Pool(nc, es, "stg", 5, [128, 512], BF16)
        stf = Pool(nc, es, "stf", 2, [128, 32], F32)
        stv = Pool(nc, es, "stv", 2, [128, 4, 129], BF16) if fx is not None else None
        if fx is not None:
            for t_ in stv.t:
                P.memset('pool', t_[:, :, 128:129], 1.0, w=[('stv', stv.t.index(t_))])
        wrep = sb(nc, es, "wrep", [128, KC, 128], BF16) if parity == 1 else None
        sel_t = sb(nc, es, "sel_t", [16, 8, 128], BF16) if parity == 1 else None
        aw16 = sb(nc, es, "aw16", [16, NOW], BF16) if parity == 1 else None
        if parity == 1:
            P.dma(sel_t[:], sel, w=['sel'])

        ident = cmat_t[:, 0, :]
        P.dma(ng_t[:], ng, w=['ng'])
        P.dma(gains_t[:], gains, w=['gains'])
        P.dma(tabs_t[:], tabs, w=['tabs'])
        P.dma(cmat_t[:], cmat, w=['cmat'])
        P.memset('dve', eps_t[:], EPS, w=['eps'])

        import os
        _p1 = os.environ.get('A_P1', 'full')
        for j in range(NO if not _p1.startswith('one') else 1):
            xt, xk = xin.next()
            P.dma(xt[:], x[j * 128:(j + 1) * 128, :], w=[xk])
            st, sk = ssp.next()
            P.act(junk[:], xt[:], AF.Square, r=[xk], w=['junk', sk], accum_out=st[:])
            s2, s2k = ssp.next()
            P.act(s2[:], st[:], AF.Sqrt, r=[sk, 'eps'], w=[s2k], scale=1.0 / D, bias=eps_t[:])
            P.op('dve', lambda e, a=s2: e.reciprocal(a[:], a[:]), r=[s2k], w=[s2k])
            xb, xbk = xs.next()
            P.ts('dve', xb[:], xt[:], s2[:], None, ALU.mult, r=[xk, s2k], w=[xbk])
            if _p1 == 'notr':
                continue
            for half in range(2 if _p1 != 'oneh' else 1):
                pt, ptk = pst.next()
                for q in range(8):
                    kc = half * 8 + q
                    P.tr(pt[:, q, :], xb[:, kc * 128:(kc + 1) * 128], ident, r=[xbk, 'cmat'], w=[ptk])
                for q in range(8):
                    kc = half * 8 + q
                    eng = 'dve'
                    if eng == 'pool':
                        P.act(hT[:, kc, j * 128:(j + 1) * 128], pt[:, q, :], AF.Copy, r=[ptk, 'ng'],
                              w=[('hT', j, kc)], scale=ng_t[:, kc:kc + 1])
                    else:
                        P.ts('dve', hT[:, kc, j * 128:(j + 1) * 128], pt[:, q, :], ng_t[:, kc:kc + 1], None,
                             ALU.mult, r=[ptk, 'ng'], w=[('hT', j, kc)])

        def hkeys(tiles):
            return [('hT', j, kc) for j in tiles for kc in range(KC)]

        def fm_unit(it, tg, wb, wbk, coff, post, gidx, slot, onesb, rotT, cosT, sinT):
            tiles = range(4 * tg, 4 * tg + 4)
            cols = slice(tg * 512, (tg + 1) * 512)
            pa, pak = psq.next()
            for kc in range(KC):
                if it[0] == 'fmrep':
                    lhsT = wrep[:, kc, :]
                    rk = [('wrep', kc)]
                else:
                    lhsT = wb[:, kc, coff:coff + 128]
                    rk = [(wbk, kc)]
                P.mm(pa[:], lhsT, hT[:, kc, cols], start=(kc == 0), stop=(kc == KC - 1),
                     r=rk + [('hT', j, kc) for j in tiles], w=[pak])
            so, sok = stg.next()
            if post == 'raw':
                P.act(so[:], pa[:], AF.Copy, r=[pak], w=[sok])
            elif post in ('nr', 'n'):
                sq, sqk = sqb.next()
                P.act(sq[:], pa[:], AF.Square, r=[pak], w=[sqk])
                yield
                pb, pbk = psb.next()
                P.mm(pb[:], onesb, sq[:], r=[sqk, 'cmat'], w=[pbk])
                sd, sdk = f1.next()
                P.act(sd[:], pb[:], AF.Sqrt, r=[pbk, 'eps'], w=[sdk], bias=eps_t[:])
                P.op('dve', lambda e, a=sd: e.reciprocal(a[:], a[:]), r=[sdk], w=[sdk])
                if post == 'n':
                    P.stt('dve', so[:], pa[:], gains_t[:, gidx:gidx + 1], sd[:], ALU.mult, ALU.mult,
                          r=[pak, sdk, 'gains'], w=[sok])
                else:
                    q_, qk = qn.next()
                    P.stt('dve', q_[:], pa[:], gains_t[:, gidx:gidx + 1], sd[:], ALU.mult, ALU.mult,
                          r=[pak, sdk, 'gains'], w=[qk])
                    yield
                    pc, pck = psc.next()
                    P.mm(pc[:], rotT, q_[:], r=[qk, 'cmat'], w=[pck])
                    t1, t1k = f2.next()
                    P.tt('pool', t1[:], q_[:], cosT[:, cols], ALU.mult, r=[qk, 'tabs'], w=[t1k])
                    t2, t2k = f3.next()
                    P.tt('dve', t2[:], pc[:], sinT[:, cols], ALU.mult, r=[pck, 'tabs'], w=[t2k])
                    P.tt('pool', so[:], t1[:], t2[:], ALU.add, r=[t1k, t2k], w=[sok])
            elif post == 'r':
                q_, qk = qn.next()
                P.act(q_[:], pa[:], AF.Copy, r=[pak], w=[qk])
                yield
                pc, pck = psc.next()
                P.mm(pc[:], rotT, q_[:], r=[qk, 'cmat'], w=[pck])
                t1, t1k = f2.next()
                P.tt('pool', t1[:], q_[:], cosT[:, cols], ALU.mult, r=[qk, 'tabs'], w=[t1k])
                t2, t2k = f3.next()
                P.tt('dve', t2[:], pc[:], sinT[:, cols], ALU.mult, r=[pck, 'tabs'], w=[t2k])
                P.tt('pool', so[:], t1[:], t2[:], ALU.add, r=[t1k, t2k], w=[sok])
            elif post == 'rw':
                cidx = it[6]
                q_, qk = qn.next()
                P.act(q_[:], pa[:], AF.Copy, r=[pak], w=[qk])
                yield
                pc, pck = psc.next()
                P.mm(pc[:], rotT, q_[:], r=[qk, 'cmat'], w=[pck])
                pw, pwk = psb.next()
                P.mm(pw[:], sel_t[:, cidx, :], aw16[:, cols], r=['sel', ('aw16', tg)], w=[pwk])
                t1, t1k = f2.next()
                P.tt('pool', t1[:], q_[:], cosT[:, cols], ALU.mult, r=[qk, 'tabs'], w=[t1k])
                t2, t2k = f3.next()
                P.tt('dve', t2[:], pc[:], sinT[:, cols], ALU.mult, r=[pck, 'tabs'], w=[t2k])
                P.tt('pool', t1[:], t1[:], t2[:], ALU.add, r=[t1k, t2k], w=[t1k])
                P.tt('dve', so[:], t1[:], pw[:], ALU.mult, r=[t1k, pwk], w=[sok])
            if fx is None:
                P.dma(fmo[:, slot, cols], so[:], r=[sok], w=[('fmo', slot, tg)])
            else:
                kind_, idx_ = fm_map(parity, slot)
                dst_ = fmq_d[:, idx_, cols] if kind_ == 'q' else kown_fm_v[:, idx_, cols]
                P.dma(dst_, so[:], r=[sok], w=[('fmo', slot, tg)])
            return
            yield

        fm_live = []

        def fm_push(g_):
            next(g_, None)
            for o_ in list(fm_live):
                try:
                    next(o_)
                except StopIteration:
                    fm_live.remove(o_)
            fm_live.append(g_)

        def fm_flush():
            while fm_live:
                for o_ in list(fm_live):
                    try:
                        next(o_)
                    except StopIteration:
                        fm_live.remove(o_)

        funcs = {'copy': AF.Copy, 'silu': AF.Silu, 'sigmoid': AF.Sigmoid, 'sign': AF.Sign}
        import os
        _lim = int(os.environ.get('A_GROUPS', '999'))
        for gi, g in enumerate(groups):
            if gi >= _lim:
                break
            n = g['n']
            wb, wbk = wbf.next()
            wv = w_in[:, g['c0']:g['c0'] + n].rearrange("(kc p) c -> p kc c", p=128)
            for hf in range(2):
                ws, wsk = wst.next()
                P.dma(ws[:, :, 0:n], wv[:, hf * 8:(hf + 1) * 8, :], w=[wsk])
                for q in range(8):
                    kc = hf * 8 + q
                    eng = ('dve', 'pool')[kc % 2]
                    P.cp(eng, wb[:, kc, 0:n], ws[:, q, 0:n], r=[wsk], w=[(wbk, kc)])
            wkeys = [(wbk, kc) for kc in range(KC)]
            for it in g['items']:
                if it[0] == 'wrep':
                    fm_flush()
                    for kc in range(KC):
                        for hh in range(2):
                            P.cp('pool', wrep[:, kc, hh * 64:(hh + 1) * 64], wb[:, kc, 0:64],
                                 r=[(wbk, kc)], w=[('wrep', kc)])
                    for tg in range(NTG):
                        cols = slice(tg * 512, (tg + 1) * 512)
                        pa, pak = psq.next()
                        for kc in range(KC):
                            P.mm(pa[0:16, :], wb[:, kc, 64:80], hT[:, kc, cols], start=(kc == 0), stop=(kc == KC - 1),
                                 r=[(wbk, kc)] + [('hT', j, kc) for j in range(4 * tg, 4 * tg + 4)], w=[pak])
                        P.act(aw16[:, cols], pa[0:16, :], AF.Abs, r=[pak], w=[('aw16', tg)], scale=1.0 / 32.0)
                    continue
                if it[0] in ('fm', 'fmrep'):
                    _, coff, post, gidx, variant, slot = it[:6]
                    vi = 0 if variant == 128 else 1
                    onesb = cmat_t[:, 1 + vi, :]
                    rotT = cmat_t[:, 3 + vi, :]
                    cosT = tabs_t[:, 2 * vi, :]
                    sinT = tabs_t[:, 2 * vi + 1, :]
                    for tg in range(NTG):
                        fm_push(fm_unit(it, tg, wb, wbk, coff, post, gidx, slot, onesb, rotT, cosT, sinT))
                elif it[0] == 'tm':
                    fm_flush()
                    _, coff, nn, func, dst, dcol = it
                    for j in range(NO):
                        pa, pak = psq.next()
                        for kc in range(KC):
                            P.mm(pa[:, 0:nn], hT[:, kc, j * 128:(j + 1) * 128], wb[:, kc, coff:coff + nn],
                                 start=(kc == 0), stop=(kc == KC - 1), r=[(wbk, kc), ('hT', j, kc)], w=[pak])
                        if dst == 'b' and fx is not None and tm_map(parity, dcol)[0] == 'v':
                            _, u0_, nu_ = tm_map(parity, dcol)
                            so, sok = stv.next()
                            P.act(so[:, 0:nu_, 0:128], pa[:, 0:nn].rearrange("p (u d) -> p u d", d=128), funcs[func],
                                  r=[pak], w=[sok])
                            P.dma(kown_v_v[:, u0_:u0_ + nu_, j, :], so[:, 0:nu_, :], r=[sok], w=[('tmo', dcol, j)])
                        elif dst == 'b':
                            so, sok = stg.next()
                            P.act(so[:, 0:nn], pa[:, 0:nn], funcs[func], r=[pak], w=[sok])
                            if fx is None:
                                P.dma(tmo[j * 128:(j + 1) * 128, dcol:dcol + nn], so[:, 0:nn], r=[sok], w=[('tmo', dcol, j)])
                            else:
                                zc_ = tm_map(parity, dcol)[1]
                                P.dma(tmz_d[j * 128:(j + 1) * 128, zc_:zc_ + nn], so[:, 0:nn], r=[sok], w=[('tmo', dcol, j)])
                        else:
                            so, sok = stf.next()
                            P.act(so[:, 0:nn], pa[:, 0:nn], funcs[func], r=[pak], w=[sok])
                            P.dma(tmf[j * 128:(j + 1) * 128, dcol:dcol + nn], so[:, 0:nn], r=[sok], w=[('tmf', dcol, j)])
        fm_flush()
        if fx is None:
            P.emit(es)
    return nc


def cmat_host():
    ident = np.eye(128, dtype=np.float32).astype(NPBF)
    return np.stack([ident, ones_block(128), ones_block(64), rot_matrix_T(128), rot_matrix_T(64)], axis=1)


def tabs_host(S, r):
    pos = own_positions(S, r)
    c128, s128 = rope_tables_fm(pos, 128)
    c64, s64 = rope_tables_fm(pos, 64)
    return np.stack([c128, s128, c64, s64], axis=1)


def geom(S):
    NT = S // 128
    NO = NT // 4
    NOW = NO * 128
    NC = NO // 4
    n_cmp = (S - 32) // 16 + 1
    NCT = (n_cmp + 127) // 128
    n_sel = S // 64
    return NT, NO, NOW, NC, n_cmp, NCT, n_sel


class BCtx:
    pass


def b_common(nc, es, P, S, parity, fx=None):
    NT, NO, NOW, NC, n_cmp, NCT, n_sel = geom(S)
    c = BCtx()
    c.S, c.NT, c.NO, c.NOW, c.NC = S, NT, NO, NOW, NC
    c.fx = fx
    NQ = 16 if parity == 0 else 24
    NZ = 2048
    if fx is None:
        dt = nc.dram_tensor
        c.ext = lambda name, shape, dtype, per_layer=False: dt(name, shape, dtype, kind="ExternalInput").ap()
        c.x = dt("x", [NOW, D], F32, kind="ExternalInput").ap()
        c.xo = dt("xo", [NOW, D], F32, kind="ExternalOutput").ap()
        c.fmq = dt("fmq", [128, NQ, NOW], BF16, kind="ExternalInput").ap()
        c.tmz = dt("tmz", [NOW, NZ], BF16, kind="ExternalInput").ap()
    else:
        c.ext = fx.ext
        c.x = fx.x_src
        c.xo = fx.x_dst
        c.fmq = fx.internal("fmq", [128, NQ, NOW], BF16)
        c.tmz = fx.internal("tmz", [NOW, NZ], BF16)
        c.tmf = fx.internal("tmf", [NOW, 24 if parity == 0 else 16], F32)
        c.kall_fm = fx.internal("kall_fm", [4 * NK[parity] * 128, NOW], BF16)
        c.kall_v = fx.internal("kall_v", [4 * NV[parity] * 128, NO * 129], BF16)
        c.kall_fm_v = c.kall_fm.rearrange("(r s d) (j p) -> d s j r p", r=4, d=128, p=128)
        c.kall_v_v = c.kall_v.rearrange("(r u p) (j d) -> p u j r d", r=4, p=128, d=129)
    c.w_out = c.ext("w_out", [D, D], F32, True)
    c.mem = c.ext("mem", [256, D], F32)
    c.w_kv = c.ext("w_kv", [D, 1024], F32, True)
    c.memg = c.ext("memg", [128, KC + 1], F32, True)
    c.cmat = c.ext("cmat", [128, 5, 128], BF16)
    c.dm = c.ext("dm", [128, 4, 128], BF16)

    def load_k(slot, kg_unfused):
        if fx is None:
            P.dma(c.bufK[:, 0:S], kg_unfused[:, slot, :], w=['bufK'])
        else:
            dv = c.bufK[:, 0:S].rearrange("d (j r p) -> d j r p", r=4, p=128)
            for r_ in range(4):
                P.dma(dv[:, :, r_, :], c.kall_fm_v[:, slot, :, r_, :], w=['bufK'])
    c.load_k = load_k

    def load_v(unit, vg_unfused):
        if fx is None:
            P.dma(c.bufV[:, 0:NT * 129], vg_unfused[unit], w=['bufV'])
        else:
            dv = c.bufV[:, 0:NT * 129].rearrange("p (j r d) -> p j r d", r=4, d=129)
            for r_ in range(4):
                P.dma(dv[:, :, r_, :], c.kall_v_v[:, unit, :, r_, :], w=['bufV'])
    c.load_v = load_v

    c.cmat_t = sb(nc, es, "cmat_t", [128, 5, 128], BF16)
    c.dm_t = sb(nc, es, "dm_t", [128, 4, 128], BF16)
    c.memg_t = sb(nc, es, "memg_t", [128, KC + 1], F32)
    c.eps_t = sb(nc, es, "eps_t", [128, 1], F32)
    c.y_all = sb(nc, es, "y_all", [128, NO, D], BF16)
    KCOLS = max(S, 8192)
    VCOLS = max(NT * 129, 8192)
    c.big = sb(nc, es, "big", [128, KCOLS + VCOLS], BF16)
    c.bufK = c.big[:, 0:KCOLS]
    c.bufV = c.big[:, KCOLS:KCOLS + VCOLS]
    c.qT = sb(nc, es, "qT", [128, 4, NOW], BF16)
    c.wst = Pool(nc, es, "wst", 1, [128, KC // 2, 512], F32)
    c.memT = c.bufV[:, 0:KC * 256].rearrange("p (k m) -> p k m", m=256)
    c.MKT = sb(nc, es, "MKT", [128, 4, 256], BF16)
    c.MVx = sb(nc, es, "MVx", [128, 2, 4, 129], BF16)
    c.stp = Pool(nc, es, "stp", 2, [128, 512], F32, psum=True)
    c.obank = [es.enter_context(nc.psum_tensor(_uname(f"ob{i}"), [128, 512], F32)) for i in range(4)]
    c.psm = Pool(nc, es, "psm", 1, [128, 512], F32, psum=True)
    c.pst = Pool(nc, es, "pst", 1, [128, 8, 128], BF16, psum=True)
    c.Ep = Pool(nc, es, "Ep", 3, [128, 512], BF16)
    c.f1 = Pool(nc, es, "f1", 2, [128, 512], F32)
    c.b1 = Pool(nc, es, "b1", 2, [128, 512], BF16)
    c.sm = Pool(nc, es, "sm", 8, [128, 4], F32)
    c.zp = Pool(nc, es, "zp", 3, [128, 512], BF16)

    def zt_for(j, col0, n):
        t, k = c.zp.next()
        P.dma(t[:, 0:n], c.tmz[j * 128:(j + 1) * 128, col0:col0 + n], w=[k])
        return t, k
    c.zt_for = zt_for
    c.xp = Pool(nc, es, "xp", 2, [128, 512], F32)
    c.ident = c.cmat_t[:, 0, :]
    c.ones128 = c.cmat_t[:, 1, :]
    c.ones64 = c.cmat_t[:, 2, :]
    c.rot128 = c.cmat_t[:, 3, :]

    P.dma(c.cmat_t[:], c.cmat, w=['cmat'])
    P.dma(c.dm_t[:], c.dm, w=['dm'])
    P.dma(c.memg_t[:], c.memg, w=['memg'])
    P.memset('dve', c.eps_t[:], EPS, w=['eps'])
    return c


def fm_norm(P, c, pa, pak, ncols, gain_ap, gain_key, out_ap, out_keys, ones=None):
    ones = c.ones128 if ones is None else ones
    sq, sqk = c.b1.next()
    P.act(sq[:, 0:ncols], pa[:, 0:ncols], AF.Square, r=[pak], w=[sqk])
    pb, pbk = c.psm.next()
    P.mm(pb[:, 0:ncols], ones, sq[:, 0:ncols], r=[sqk, 'cmat'], w=[pbk])
    sd, sdk = c.f1.next()
    P.act(sd[:, 0:ncols], pb[:, 0:ncols], AF.Sqrt, r=[pbk, 'eps'], w=[sdk], bias=c.eps_t[:])
    P.op('dve', lambda e: e.reciprocal(sd[:, 0:ncols], sd[:, 0:ncols]), r=[sdk], w=[sdk])
    P.stt('dve', out_ap, pa[:, 0:ncols], gain_ap, sd[:, 0:ncols], ALU.mult, ALU.mult,
          r=[pak, sdk, gain_key], w=out_keys)


def load_cast_w(P, c, dst, dst_key, w_dram_view, n):
    for hf in range(2):
        ws, wsk = c.wst.next()
        P.dma(ws[:, :, 0:n], w_dram_view[:, hf * 8:(hf + 1) * 8, :], w=[wsk])
        for q in range(8):
            kc = hf * 8 + q
            eng = ('dve', 'pool')[kc % 2]
            P.cp(eng, dst[:, kc, 0:n], ws[:, q, 0:n], r=[wsk], w=[dst_key])


def b_memory_kv(P, c):
    for mt in range(2):
        wst0, xk = c.wst.next()
        xt = wst0[:].rearrange("p a b -> p (a b)")[:, 0:D]
        P.dma(xt[:], c.mem[mt * 128:(mt + 1) * 128, :], w=[xk])
        xb, xbk = c.y_all[:, 0, :], ('y', 0)
        st, sk = c.sm.next()
        P.act(xb[:], xt[:], AF.Square, r=[xk], w=[xbk, sk], accum_out=st[:, 0:1])
        P.act(st[:, 1:2], st[:, 0:1], AF.Sqrt, r=[sk, 'eps'], w=[sk], scale=1.0 / D, bias=c.eps_t[:])
        P.op('dve', lambda e, a=st: e.reciprocal(a[:, 1:2], a[:, 1:2]), r=[sk], w=[sk])
        P.ts('dve', xb[:], xt[:], st[:, 1:2], None, ALU.mult, r=[xk, sk], w=[xbk])
        for half in range(2):
            pt, ptk = c.pst.next()
            for q in range(8):
                kc = half * 8 + q
                P.tr(pt[:, q, :], xb[:, kc * 128:(kc + 1) * 128], c.ident, r=[xbk, 'cmat'], w=[ptk])
            for q in range(8):
                kc = half * 8 + q
                P.ts('dve', c.memT[:, kc, mt * 128:(mt + 1) * 128], pt[:, q, :], c.memg_t[:, kc:kc + 1], None,
                     ALU.mult, r=[ptk, 'memg'], w=['bufV'])
    wv = c.w_kv.rearrange("(kc p) c -> p kc c", p=128)
    wk = c.bufK[:, 0:KC * 512].rearrange("p (k c) -> p k c", c=512)
    load_cast_w(P, c, wk, 'bufK', wv[:, :, 0:512], 512)
    for h in range(4):
        pa, pak = c.stp.next()
        for kc in range(KC):
            P.mm(pa[:, 0:256], wk[:, kc, h * 128:(h + 1) * 128], c.memT[:, kc, :], start=(kc == 0), stop=(kc == KC - 1),
                 r=['bufK', 'bufV'], w=[pak])
        fm_norm(P, c, pa, pak, 256, c.memg_t[:, KC:KC + 1], 'memg', c.MKT[:, h, :], ['MKT'])
    load_cast_w(P, c, wk, 'bufK', wv[:, :, 512:1024], 512)
    P.memset('pool', c.MVx[:, :, :, 128:129], 1.0, w=['MVx'])
    for mt in range(2):
        pa, pak = c.stp.next()
        for kc in range(KC):
            P.mm(pa[:], c.memT[:, kc, mt * 128:(mt + 1) * 128], wk[:, kc, :], start=(kc == 0), stop=(kc == KC - 1),
                 r=['bufK', 'bufV'], w=[pak])
        P.act(c.MVx[:, mt, :, 0:128], pa[:].rearrange("p (h d) -> p h d", d=128), AF.Copy, r=[pak], w=['MVx'])


def attend(P, c, steps, scale):
    prev = None

    def do_pv(st_):
        E, Ek = st_['E']
        for (ec0, oap, okey, rhs, rkeys, first, last) in st_['pv']:
            P.mm(oap, E[:, ec0:ec0 + 128], rhs, start=first, stop=last, r=[Ek] + rkeys, w=[okey])

    for st_ in steps:
        ps, psk = c.stp.next()
        c0, n = st_['c0'], st_['n']
        nm = len(st_['masks'])
        P.mm(ps[:, c0:c0 + n], st_['k'], st_['q'], start=True, stop=(nm == 0), r=st_['kkeys'] + st_['qkeys'], w=[psk])
        for mi, (mc0, mn, ml, mr, mk) in enumerate(st_['masks']):
            P.mm(ps[:, mc0:mc0 + mn], ml, mr, start=False, stop=(mi == nm - 1), r=mk, w=[psk])
        E, Ek = c.Ep.next()
        P.act(E[:, c0:c0 + n], ps[:, c0:c0 + n], AF.Exp, r=[psk], w=[Ek], scale=scale)
        st_['E'] = (E, Ek)
        if prev is not None:
            do_pv(prev)
        prev = st_
    if prev is not None:
        do_pv(prev)


def b_mem_attention(P, c, mq_slot0, ycol0, zcol0):
    NO, NC = c.NO, c.NC
    P.dma(c.qT[:], c.fmq[:, mq_slot0:mq_slot0 + 4, :], w=['qT'])
    scale = 128 ** -0.5
    for h in range(4):
        for ch in range(NC):
            steps = []
            for mt in range(2):
                pv = [(i * 128, c.obank[i][:, 0:129], ('ob', i), c.MVx[:, mt, h, :], ['MVx'], mt == 0, mt == 1)
                      for i in range(4)]
                steps.append(dict(k=c.MKT[:, h, mt * 128:(mt + 1) * 128], kkeys=['MKT'],
                                  q=c.qT[:, h, ch * 512:(ch + 1) * 512], qkeys=['qT'], c0=0, n=512, masks=[], pv=pv))
            attend(P, c, steps, scale)
            for i in range(4):
                j = ch * 4 + i
                zt, ztk = c.zt_for(j, zcol0 + h * 128, 128)
                s_, sk = c.sm.next()
                P.ts('dve', s_[:, 0:1], c.obank[i][:, 128:129], 1e-30, None, ALU.max, r=[('ob', i)], w=[sk])
                P.op('dve', lambda e, a=s_: e.reciprocal(a[:, 0:1], a[:, 0:1]), r=[sk], w=[sk])
                P.stt('dve', c.y_all[:, j, ycol0 + h * 128:ycol0 + (h + 1) * 128], c.obank[i][:, 0:128], s_[:, 0:1],
                      zt[:, 0:128], ALU.mult, ALU.mult,
                      r=[('ob', i), sk, ztk], w=[('y', j)])


def b_out_proj(P, c):
    NO = c.NO
    for j in range(NO):
        for half in range(2):
            pt, ptk = c.pst.next()
            for q in range(8):
                kc = half * 8 + q
                P.tr(pt[:, q, :], c.y_all[:, j, kc * 128:(kc + 1) * 128], c.ident, r=[('y', j), 'cmat'], w=[ptk])
            P.cp('dve', c.y_all[:, j, half * 1024:(half + 1) * 1024], pt[:].rearrange("p a b -> p (a b)"),
                 r=[ptk], w=[('y', j)])
    wv = c.w_out.rearrange("(kc p) c -> p kc c", p=128)
    wbs = [(c.bufK[:, 0:KC * 512].rearrange("p (k c) -> p k c", c=512), 'bufK'),
           (c.bufV[:, 0:KC * 512].rearrange("p (k c) -> p k c", c=512), 'bufV')]
    for cg in range(4):
        wb, wbk = wbs[cg % 2]
        load_cast_w(P, c, wb, wbk, wv[:, :, cg * 512:(cg + 1) * 512], 512)
        for j in range(NO):
            pa, pak = c.stp.next()
            for kc in range(KC):
                P.mm(pa[:], c.y_all[:, j, kc * 128:(kc + 1) * 128], wb[:, kc, :], start=(kc == 0), stop=(kc == KC - 1),
                     r=[('y', j), wbk], w=[pak])
            xt, xk = c.xp.next()
            P.dma(xt[:], c.x[j * 128:(j + 1) * 128, cg * 512:(cg + 1) * 512], w=[xk])
            P.tt('dve', xt[:], xt[:], pa[:], ALU.add, r=[xk, pak], w=[xk])
            P.dma(c.xo[j * 128:(j + 1) * 128, cg * 512:(cg + 1) * 512], xt[:], r=[xk], w=[('xo', j, cg)])


def build_B_even(S, fx=None):
    NT, NO, NOW, NC, n_cmp, NCT, n_sel = geom(S)
    NCP = NCT * 128
    nc = bass.Bass("TRN2", target_bir_lowering=False) if fx is None else fx.nc
    with ExitStack() as es:
        P = Prog(nc) if fx is None else fx.P
        c = b_common(nc, es, P, S, 0, fx)
        if fx is None:
            kg = c.ext("kg", [128, 12, S], BF16)
            vg = c.ext("vg", [8, 128, NT * 129], BF16)
            ag = c.ext("ag", [NOW, 24], F32)
        else:
            kg = vg = None
            ag = c.tmf
        cw1 = c.ext("cw1", [2, 4096, 256], F32, True)
        cw2 = c.ext("cw2", [128, 2, 2, 128], F32, True)
        peT = c.ext("peT", [128, 2, 32], BF16, True)
        ctab = c.ext("ctab", [128, 2, NCP], BF16)
        ovm = c.ext("ovm", [128, NCT, 128], BF16)
        expand = c.ext("expand", [128, 32, 128], BF16)
        wm = c.ext("wm", [128, 8, 128], BF16)
        cm = c.ext("cm", [NO, 128, NCT, 128], BF16)
        visfrc = c.ext("visfrc", [NO, 128, 2, 128], F32)
        gv = c.ext("gv", [128, 388], F32, True)

        cw2_f = sb(nc, es, "cw2_f", [128, 2, 2, 128], F32)
        cw2_t = sb(nc, es, "cw2_t", [128, 2, 2, 128], BF16)
        peT_t = sb(nc, es, "peT_t", [128, 2, 32], BF16)
        ctab_t = sb(nc, es, "ctab_t", [128, 2, NCP], BF16)
        ovm_t = sb(nc, es, "ovm_t", [128, NCT, 128], BF16)
        expand_t = sb(nc, es, "expand_t", [128, 32, 128], BF16)
        wm_t = sb(nc, es, "wm_t", [128, 8, 128], BF16)
        gv_t = sb(nc, es, "gv_t", [128, 388], F32)
        KCc = sb(nc, es, "KCc", [128, 2, NCP], BF16)
        VCx = sb(nc, es, "VCx", [128, 2, NCT, 257], BF16)
        hid = sb(nc, es, "hid", [128, 2, NCP], BF16)
        cbias = sb(nc, es, "cbias", [128, 2], F32)
        lam = sb(nc, es, "lam", [128, 4], F32)
        cmp_ = Pool(nc, es, "cmp", 2, [128, NCT, 128], BF16)
        vfp = Pool(nc, es, "vfp", 2, [128, 2, 128], F32)
        agp = Pool(nc, es, "agp", 2, [128, 24], F32)
        acc = Pool(nc, es, "acc", 1, [128, 512], F32)
        imp = Pool(nc, es, "imp", 2, [128, 128], F32)
        imp2 = Pool(nc, es, "imp2", 2, [128, 128], F32)
        m8 = Pool(nc, es, "m8", 2, [128, 16], F32)
        seln = Pool(nc, es, "seln", 2, [128, 128], BF16)
        selT = Pool(nc, es, "selT", 2, [128, 128], BF16)
        kwp = Pool(nc, es, "kwp", 1, [128, 8, 128], BF16)
        vwp = Pool(nc, es, "vwp", 1, [128, 8, 129], BF16)
        dtmp = Pool(nc, es, "dtmp", 2, [128, 4, 128], F32)

        for (t, src, k) in ((cw2_f, cw2, 'cw2f'), (peT_t, peT, 'peT'), (ctab_t, ctab, 'ctab'), (ovm_t, ovm, 'ovm'),
                            (expand_t, expand, 'expand'), (wm_t, wm, 'wm'), (gv_t, gv, 'gv')):
            P.dma(t[:], src, w=[k])
        P.cp('dve', cw2_t[:], cw2_f[:], r=['cw2f'], w=['cw2'])
        lv = gv_t[:, 130:130 + 256]
        j1, j1k = c.f1.next()
        P.tt('dve', j1[:, 0:64], lv[:, 0:64], lv[:, 64:128], ALU.mult, r=['gv'], w=[j1k])
        P.tt('dve', j1[:, 64:128], lv[:, 128:192], lv[:, 192:256], ALU.mult, r=['gv'], w=[j1k])
        P.op('dve', lambda e: e.reduce_sum(lam[:, 0:1], j1[:, 0:64], AX.X), r=[j1k], w=['lam'])
        P.op('dve', lambda e: e.reduce_sum(lam[:, 1:2], j1[:, 64:128], AX.X), r=[j1k], w=['lam'])
        P.act(lam[:, 0:2], lam[:, 0:2], AF.Exp, r=['lam'], w=['lam'])
        P.tt('dve', lam[:, 2:3], lam[:, 0:1], lam[:, 1:2], ALU.subtract, r=['lam'], w=['lam'])
        P.tt('dve', lam[:, 3:4], lam[:, 2:3], gv_t[:, 386:387], ALU.add, r=['lam', 'gv'], w=['lam'])
        P.ts('dve', lam[:, 3:4], lam[:, 3:4], -1.0, None, ALU.mult, r=['lam'], w=['lam'])

        b_memory_kv(P, c)

        P.memset('pool', hid[:], 0.0, w=['hid'])
        P.memset('pool', KCc[:], 0.0, w=['KCc'])
        P.memset('pool', VCx[:], 0.0, w=['VCx'])
        P.memset('pool', VCx[:, :, :, 256:257], 1.0, w=['VCx'])
        for g in range(2):
            for ct in range(NCT):
                P.cp('pool', VCx[:, g, ct, 128:256], ovm_t[:, ct, :], r=['ovm'], w=['VCx'])
        w1v = c.bufV[:, 0:32 * 256].rearrange("p (l h) -> p l h", h=256)
        for kv in range(2):
            wsrc = cw1[kv].rearrange("(l p) h -> p l h", p=128)
            for q4 in range(4):
                ws, wsk = c.wst.next()
                wsv = ws[:].rearrange("p a b -> p (a b)")[:, 0:8 * 256].rearrange("p (l h) -> p l h", h=256)
                P.dma(wsv, wsrc[:, q4 * 8:(q4 + 1) * 8, :], w=[wsk])
                P.cp('dve', w1v[:, q4 * 8:q4 * 8 + 4, :], wsv[:, 0:4, :], r=[wsk], w=['bufV'])
                P.cp('pool', w1v[:, q4 * 8 + 4:q4 * 8 + 8, :], wsv[:, 4:8, :], r=[wsk], w=['bufV'])
            for half in range(2):
                pb, pbk = c.psm.next()
                for l in range(32):
                    P.mm(pb[:, 0:1], w1v[:, l, half * 128:(half + 1) * 128], peT_t[:, kv, l:l + 1],
                         start=(l == 0), stop=(l == 31), r=['bufV', 'peT'], w=[pbk])
                P.cp('dve', cbias[:, half:half + 1], pb[:, 0:1], r=[pbk], w=['cbias'])
            for g in range(2):
                c.load_k(2 * kv + g, kg)
                for half in range(2):
                    pa, pak = c.stp.next()
                    for l in range(32):
                        P.mm(pa[:, 0:n_cmp], w1v[:, l, half * 128:(half + 1) * 128],
                             c.bufK[:, l:l + 16 * (n_cmp - 1) + 1:16], start=(l == 0), stop=(l == 31),
                             r=['bufV', 'bufK'], w=[pak])
                    P.act(hid[:, half, 0:n_cmp], pa[:, 0:n_cmp], AF.Silu, r=[pak, 'cbias'], w=['hid'],
                          bias=cbias[:, half:half + 1])
                if kv == 0:
                    pa, pak = c.stp.next()
                    for half in range(2):
                        P.mm(pa[:, 0:NCP], cw2_t[:, 0, half, :], hid[:, half, :], start=(half == 0), stop=(half == 1),
                             r=['cw2', 'hid'], w=[pak])
                    q_, qk = c.b1.next()
                    fm_norm(P, c, pa, pak, NCP, gv_t[:, 0:1], 'gv', q_[:, 0:NCP], [qk])
                    pc, pck = c.psm.next()
                    P.mm(pc[:, 0:NCP], c.rot128, q_[:, 0:NCP], r=[qk, 'cmat'], w=[pck])
                    t1, t1k = c.f1.next()
                    P.tt('pool', t1[:, 0:NCP], q_[:, 0:NCP], ctab_t[:, 0, :], ALU.mult, r=[qk, 'ctab'], w=[t1k])
                    t2, t2k = c.f1.next()
                    P.tt('dve', t2[:, 0:NCP], pc[:, 0:NCP], ctab_t[:, 1, :], ALU.mult, r=[pck, 'ctab'], w=[t2k])
                    P.tt('pool', KCc[:, g, :], t1[:, 0:NCP], t2[:, 0:NCP], ALU.add, r=[t1k, t2k], w=['KCc'])
                else:
                    for ct in range(NCT):
                        pa, pak = c.stp.next()
                        for half in range(2):
                            P.mm(pa[:, 0:128], hid[:, half, ct * 128:(ct + 1) * 128], cw2_t[:, 1, half, :],
                                 start=(half == 0), stop=(half == 1), r=['cw2', 'hid'], w=[pak])
                        P.act(VCx[:, g, ct, 0:128], pa[:, 0:128], AF.Copy, r=[pak], w=['VCx'])

        sc = 128 ** -0.5
        ob = c.obank
        for g in range(2):
            P.dma(c.qT[:], c.fmq[:, g * 4:(g + 1) * 4, :], w=['qT'])
            c.load_k(4 + g, kg)
            c.load_v(g, vg)
            vS = c.bufV[:, 0:NT * 129].rearrange("p (t d) -> p t d", d=129)
            for j in range(NO):
                qj = c.qT[:, :, j * 128:(j + 1) * 128]
                cmj, cmk = cmp_.next()
                P.dma(cmj[:], cm[j], w=[cmk])
                vf, vfk = vfp.next()
                P.dma(vf[:], visfrc[j], w=[vfk])
                agt, agk = agp.next()
                P.dma(agt[:], ag[j * 128:(j + 1) * 128, :], w=[agk])
                a_, ak = acc.next()
                steps = []
                nct_j = min(NCT, j // 4 + 1)
                for ct in range(nct_j):
                    masks = [(h * 128, 128, c.ident, cmj[:, ct, :], ['cmat', cmk]) for h in range(4)]
                    pv = [(h * 128, ob[h][:, 0:257], ('ob', h), VCx[:, g, ct, :], ['VCx'], ct == 0, ct == nct_j - 1)
                          for h in range(4)]
                    steps.append(dict(k=KCc[:, g, ct * 128:(ct + 1) * 128], kkeys=['KCc'], q=qj, qkeys=['qT'],
                                      c0=0, n=512, masks=masks, pv=pv))
                attend(P, c, steps, sc)
                im, imk = imp.next()
                for h in range(4):
                    s_, sk = c.sm.next()
                    P.ts('dve', s_[:, 0:1], ob[h][:, 256:257], 1e-30, None, ALU.max, r=[('ob', h)], w=[sk])
                    P.op('dve', lambda e, a=s_: e.reciprocal(a[:, 0:1], a[:, 0:1]), r=[sk], w=[sk])
                    P.tt('dve', s_[:, 1:2], s_[:, 0:1], agt[:, (g * 4 + h) * 3:(g * 4 + h) * 3 + 1], ALU.mult,
                         r=[sk, agk], w=[sk])
                    P.ts('dve', a_[:, h * 128:(h + 1) * 128], ob[h][:, 0:128], s_[:, 1:2], None, ALU.mult,
                         r=[('ob', h), sk], w=[ak])
                    if h == 0:
                        P.ts('dve', im[:], ob[h][:, 128:256], s_[:, 0:1], None, ALU.mult, r=[('ob', h), sk], w=[imk])
                    else:
                        P.stt('dve', im[:], ob[h][:, 128:256], s_[:, 0:1], im[:], ALU.mult, ALU.add,
                              r=[('ob', h), sk, imk], w=[imk])
                P.tt('dve', im[:], im[:], vf[:, 0, :], ALU.mult, r=[imk, vfk], w=[imk])
                P.tt('dve', im[:], im[:], vf[:, 1, :], ALU.add, r=[imk, vfk], w=[imk])
                mx, mxk = m8.next()
                P.op('dve', lambda e, a=mx, b=im: e.max(a[:, 0:8], b[:]), r=[imk], w=[mxk])
                i2, i2k = imp2.next()
                P.op('dve', lambda e, a=mx, b=im, o=i2: e.match_replace(o[:], a[:, 0:8], b[:], -3.0e38),
                     r=[imk, mxk], w=[i2k])
                P.op('dve', lambda e, a=mx, b=i2: e.max(a[:, 8:16], b[:]), r=[i2k], w=[mxk])
                sn, snk = seln.next()
                P.ts('dve', sn[:], im[:], mx[:, 15:16], NEG, ALU.is_lt, ALU.mult, r=[imk, mxk], w=[snk])
                pt, ptk = c.pst.next()
                P.tr(pt[:, 0, :], sn[:], c.ident, r=[snk, 'cmat'], w=[ptk])
                sT, sTk = selT.next()
                P.cp('dve', sT[:], pt[:, 0, :], r=[ptk], w=[sTk])
                u0 = 4 if j == 0 else 0
                kw, kwk = kwp.next()
                vw, vwk = vwp.next()
                t_lo = 4 * j - 4 + u0
                if fx is None:
                    P.dma(kw[:, u0:8, :], kg[:, 6 + g, t_lo * 128:(4 * j + 4) * 128].rearrange("p (u s) -> p u s", s=128), w=[kwk])
                    P.dma(vw[:, u0:8, :], vg[2 + g][:, t_lo * 129:(4 * j + 4) * 129].rearrange("p (u s) -> p u s", s=129), w=[vwk])
                else:
                    if j > 0:
                        P.dma(kw[:, 0:4, :], c.kall_fm_v[:, 6 + g, j - 1], w=[kwk])
                        P.dma(vw[:, 0:4, :], c.kall_v_v[:, 2 + g, j - 1], w=[vwk])
                    P.dma(kw[:, 4:8, :], c.kall_fm_v[:, 6 + g, j], w=[kwk])
                    P.dma(vw[:, 4:8, :], c.kall_v_v[:, 2 + g, j], w=[vwk])
                steps = []
                for u in range(u0, 8):
                    masks = [(h * 128, 128, c.ident, wm_t[:, u, :], ['cmat', 'wm']) for h in range(4)]
                    pv = [(h * 128, ob[h][:, 0:129], ('ob', h), vw[:, u, :], [vwk], u == u0, u == 7) for h in range(4)]
                    steps.append(dict(k=kw[:, u, :], kkeys=[kwk], q=qj, qkeys=['qT'], c0=0, n=512, masks=masks, pv=pv))
                attend(P, c, steps, sc)
                for h in range(4):
                    s_, sk = c.sm.next()
                    P.ts('dve', s_[:, 0:1], ob[h][:, 128:129], 1e-30, None, ALU.max, r=[('ob', h)], w=[sk])
                    P.op('dve', lambda e, a=s_: e.reciprocal(a[:, 0:1], a[:, 0:1]), r=[sk], w=[sk])
                    P.tt('dve', s_[:, 1:2], s_[:, 0:1], agt[:, (g * 4 + h) * 3 + 2:(g * 4 + h) * 3 + 3], ALU.mult,
                         r=[sk, agk], w=[sk])
                    P.stt('dve', a_[:, h * 128:(h + 1) * 128], ob[h][:, 0:128], s_[:, 1:2], a_[:, h * 128:(h + 1) * 128],
                          ALU.mult, ALU.add, r=[('ob', h), sk, ak], w=[ak])
                steps = []
                nk = 4 * j + 4
                for kt in range(nk):
                    hrows = slice((kt // 32) * 64, (kt // 32) * 64 + 64)
                    masks = [(h * 128, 128, expand_t[hrows, kt % 32, :], sT[hrows, :], ['expand', sTk]) for h in range(4)]
                    if kt >= 4 * j:
                        masks += [(h * 128, 128, c.ident, c.dm_t[:, kt - 4 * j, :], ['cmat', 'dm']) for h in range(4)]
                    pv = [(h * 128, ob[h][:, 0:129], ('ob', h), vS[:, kt, :], ['bufV'], kt == 0, kt == nk - 1)
                          for h in range(4)]
                    steps.append(dict(k=c.bufK[:, kt * 128:(kt + 1) * 128], kkeys=['bufK'], q=qj, qkeys=['qT'],
                                      c0=0, n=512, masks=masks, pv=pv))
                attend(P, c, steps, sc)
                zt, ztk = c.zt_for(j, g * 512, 512)
                for h in range(4):
                    s_, sk = c.sm.next()
                    P.ts('dve', s_[:, 0:1], ob[h][:, 128:129], 1e-30, None, ALU.max, r=[('ob', h)], w=[sk])
                    P.op('dve', lambda e, a=s_: e.reciprocal(a[:, 0:1], a[:, 0:1]), r=[sk], w=[sk])
                    P.tt('dve', s_[:, 1:2], s_[:, 0:1], agt[:, (g * 4 + h) * 3 + 1:(g * 4 + h) * 3 + 2], ALU.mult,
                         r=[sk, agk], w=[sk])
                    P.stt('dve', a_[:, h * 128:(h + 1) * 128], ob[h][:, 0:128], s_[:, 1:2], a_[:, h * 128:(h + 1) * 128],
                          ALU.mult, ALU.add, r=[('ob', h), sk, ak], w=[ak])
                P.tt('dve', c.y_all[:, j, g * 512:(g + 1) * 512], a_[:], zt[:], ALU.mult, r=[ak, ztk], w=[('y', j)])

        sc64 = 64 ** -0.5
        for h in range(4):
            P.dma(c.qT[:, 0, :], c.fmq[:, 8 + h, :], w=['qT'])
            c.load_k(8 + h, kg)
            c.load_v(4 + h, vg)
            vS = c.bufV[:, 0:NT * 129].rearrange("p (t d) -> p t d", d=129)
            for ch in range(NC):
                j0 = 4 * ch
                dts = []
                for m in range(2):
                    rows = slice(m * 64, (m + 1) * 64)
                    steps = []
                    nk = 4 * j0 + 16
                    for kt in range(nk):
                        a = 0 if kt < 4 * j0 else (kt - 4 * j0) // 4
                        masks = []
                        if kt >= 4 * j0:
                            u = kt - 4 * (j0 + a)
                            masks = [(a * 128, 128, c.ident, c.dm_t[:, u, :], ['cmat', 'dm'])]
                        pv = []
                        for i in range(a, 4):
                            last_kt = 4 * (j0 + i) + 3
                            pv.append((i * 128, ob[i][:, 0:129], ('ob', i), vS[:, kt, :], ['bufV'], kt == 0, kt == last_kt))
                        steps.append(dict(k=c.bufK[rows, kt * 128:(kt + 1) * 128], kkeys=['bufK'],
                                          q=c.qT[rows, 0, ch * 512 + a * 128:(ch + 1) * 512], qkeys=['qT'],
                                          c0=a * 128, n=512 - a * 128, masks=masks, pv=pv))
                    attend(P, c, steps, sc64)
                    d_, dk = dtmp.next()
                    dts.append((d_, dk))
                    for i in range(4):
                        s_, sk = c.sm.next()
                        P.ts('dve', s_[:, 0:1], ob[i][:, 128:129], 1e-30, None, ALU.max, r=[('ob', i)], w=[sk])
                        P.op('dve', lambda e, a=s_: e.reciprocal(a[:, 0:1], a[:, 0:1]), r=[sk], w=[sk])
                        if m == 1:
                            P.tt('dve', s_[:, 0:1], s_[:, 0:1], lam[:, 3:4], ALU.mult, r=[sk, 'lam'], w=[sk])
                        P.ts('dve', d_[:, i, :], ob[i][:, 0:128], s_[:, 0:1], None, ALU.mult, r=[('ob', i), sk], w=[dk])
                (d0, d0k), (d1, d1k) = dts
                P.tt('pool', d0[:], d0[:], d1[:], ALU.add, r=[d0k, d1k], w=[d0k])
                for i in range(4):
                    j = j0 + i
                    s_, sk = c.sm.next()
                    jb, jbk = c.b1.next()
                    P.act(jb[:, 0:128], d0[:, i, :], AF.Square, r=[d0k], w=[jbk, sk], accum_out=s_[:, 0:1])
                    P.act(s_[:, 1:2], s_[:, 0:1], AF.Sqrt, r=[sk, 'eps'], w=[sk], scale=1.0 / 128, bias=c.eps_t[:])
                    P.op('dve', lambda e, a=s_: e.reciprocal(a[:, 1:2], a[:, 1:2]), r=[sk], w=[sk])
                    zt, ztk = c.zt_for(j, 1024 + h * 128, 128)
                    t1, t1k = c.f1.next()
                    P.stt('dve', t1[:, 0:128], d0[:, i, :], s_[:, 1:2], gv_t[:, 2:130], ALU.mult, ALU.mult,
                          r=[d0k, sk, 'gv'], w=[t1k])
                    P.stt('dve', c.y_all[:, j, 1024 + h * 128:1024 + (h + 1) * 128], t1[:, 0:128], gv_t[:, 387:388],
                          zt[:, 0:128], ALU.mult, ALU.mult, r=[t1k, ztk, 'gv'], w=[('y', j)])

        b_mem_attention(P, c, 12, 1536, 1536)
        b_out_proj(P, c)
        if fx is None:
            P.emit(es)
    return nc


def mask_consts(S, r):
    NT, NO, NOW, NC, n_cmp, NCT, n_sel = geom(S)
    s = np.arange(128)[:, None]
    t = np.arange(128)[None, :]
    dm = np.zeros((128, 4, 128), np.float32)
    for u in range(4):
        if u == r:
            dm[:, u, :] = np.where(s <= t, 0.0, NEG)
        elif u > r:
            dm[:, u, :] = NEG
    wm = np.zeros((128, 8, 128), np.float32)
    for u in range(8):
        delta = (r + 4 - u) * 128 + t - s
        wm[:, u, :] = np.where((delta >= 0) & (delta < 512), 0.0, NEG)
    cm = np.zeros((NO, 128, NCT, 128), np.float32)
    vf = np.zeros((NO, 128, 2, 128), np.float32)
    jj = np.arange(128)[None, :]
    for j in range(NO):
        tg = (4 * j + r) * 128 + np.arange(128)
        for ct in range(NCT):
            cidx = ct * 128 + np.arange(128)
            vis = (cidx[:, None] < n_cmp) & ((16 * cidx[:, None] + 31) <= tg[None, :])
            cm[j, :, ct, :] = np.where(vis, 0.0, NEG)
        cur = (tg // 64)[:, None]
        visible = (jj <= cur) & (jj < n_sel)
        forced = (jj == 0) | (jj >= cur - 1)
        vf[j, :, 0, :] = np.where(visible & ~forced, 1.0, 0.0)
        vf[j, :, 1, :] = np.where(~visible, -1e30, np.where(forced, 1000.0 + jj, 0.0))
    return dm.astype(NPBF), wm.astype(NPBF), cm.astype(NPBF), vf


def even_consts(S):
    NT, NO, NOW, NC, n_cmp, NCT, n_sel = geom(S)
    NCP = NCT * 128
    cpos = 16 * np.arange(NCP) + 31
    cc, cs = rope_tables_fm(cpos, 128)
    ctab = np.stack([cc, cs], axis=1)
    cmp_start = np.arange(n_cmp) * 16
    sel_start = np.arange(n_sel) * 64
    ov = np.clip(np.minimum(cmp_start[:, None] + 32, sel_start[None, :] + 64)
                 - np.maximum(cmp_start[:, None], sel_start[None, :]), 0, None) / 32.0
    ovp = np.zeros((NCP, 128), np.float32)
    ovp[:n_cmp, :n_sel] = ov
    ovm = ovp.reshape(NCT, 128, 128).transpose(1, 0, 2).copy().astype(NPBF)
    expand = np.zeros((128, 32, 128), np.float32)
    for k in range(32):
        for s in range(128):
            expand[2 * k + s // 64, k, s] = 1.0
            expand[64 + 2 * k + s // 64, k, s] = 1.0
    return ctab, ovm, expand.astype(NPBF)


def fm_gather(fmos, slots, S):
    NT = S // 128
    NO = NT // 4
    out = np.zeros((128, len(slots), NT, 128), dtype=fmos[0].dtype)
    for r in range(4):
        v = fmos[r][:, slots, :].reshape(128, len(slots), NO, 128)
        out[:, :, r::4, :] = v
    return out.reshape(128, len(slots), S)


def tm_gather_vext(tmos, col0, S):
    NT = S // 128
    NO = NT // 4
    out = np.ones((128, NT, 129), dtype=tmos[0].dtype)
    for r in range(4):
        v = tmos[r][:, col0:col0 + 128].reshape(NO, 128, 128)
        out[:, r::4, 0:128] = v.transpose(1, 0, 2)
    return out.reshape(128, NT * 129)


def memg_host(mem_norm_gain, mem_k_gain):
    m = np.zeros((128, KC + 1), np.float32)
    m[:, :KC] = mem_norm_gain.reshape(KC, 128).T
    m[:, KC] = mem_k_gain
    return m


def a_inputs(parity, S, x_own, w_in, norm_gain, gain_cols, r):
    m = dict(x=np.ascontiguousarray(x_own), w_in=w_in, ng=np.ascontiguousarray(norm_gain.reshape(KC, 128).T),
             gains=np.ascontiguousarray(np.stack(gain_cols, axis=1).astype(np.float32)),
             tabs=tabs_host(S, r), cmat=cmat_host())
    if parity == 1:
        sel = np.zeros((16, 8, 128), np.float32)
        for c in range(8):
            for mm_ in range(128):
                sel[2 * c + mm_ // 64, c, mm_] = 1
        m['sel'] = sel.astype(NPBF)
    return m


def b_even_inputs(S, r, x_own, fmo_b, tmo_b, tmf_own, w_out, mem_b, mem_norm_gain, w_kv, mem_qk_gain,
                  nsa_qk_gain, cmp_pos, cmp_w1, cmp_w2, subln, lam_vecs, shared, lambda_init):
    fmo = fmo_b[r]
    tmo = tmo_b[r]
    dm, wm, cm, vf = mask_consts(S, r)
    gv = np.zeros((128, 388), np.float32)
    gv[:, 0] = nsa_qk_gain[1]
    gv[:, 2:130] = subln[None, :]
    gv[:, 130:386] = lam_vecs.reshape(1, 256)
    gv[:, 386] = lambda_init
    gv[:, 387] = 1.0 - lambda_init
    m = dict(x=np.ascontiguousarray(x_own), w_out=w_out, mem=mem_b, w_kv=w_kv,
             memg=memg_host(mem_norm_gain, mem_qk_gain[1]), cmat=cmat_host(), dm=dm,
             fmq=np.ascontiguousarray(np.concatenate([fmo[:, 0:8], fmo[:, 16:20], fmo[:, 24:28]], axis=1)),
             tmz=np.ascontiguousarray(np.concatenate([tmo[:, 512:1536], tmo[:, 2048:2560], tmo[:, 2560:3072]], axis=1)),
             kg=shared['kg'], vg=shared['vg'], ag=np.ascontiguousarray(tmf_own[:, :24]),
             cw1=cmp_w1, cw2=np.ascontiguousarray(cmp_w2.reshape(2, 2, 128, 128).transpose(2, 0, 1, 3)),
             peT=np.ascontiguousarray(cmp_pos.transpose(2, 0, 1)).astype(NPBF),
             ctab=shared['ctab'], ovm=shared['ovm'], expand=shared['expand'], wm=wm, cm=cm, visfrc=vf, gv=gv)
    return m


def b_even_shared(S, fmo_b, tmo_b):
    ctab, ovm, expand = even_consts(S)
    kg = fm_gather(fmo_b, [8, 9, 10, 11, 12, 13, 14, 15, 20, 21, 22, 23], S)
    vcols = [0, 128, 256, 384, 1536, 1664, 1792, 1920]
    vg = np.stack([tm_gather_vext(tmo_b, c0, S) for c0 in vcols], axis=0)
    return dict(kg=kg, vg=vg, ctab=ctab, ovm=ovm, expand=expand)


def build_B_odd(S, fx=None):
    NT, NO, NOW, NC, n_cmp, NCT, n_sel = geom(S)
    TOPK = min(256, S // 4)
    NIT = 16
    nc = bass.Bass("TRN2", target_bir_lowering=False) if fx is None else fx.nc
    with ExitStack() as es:
        P = Prog(nc) if fx is None else fx.P
        c = b_common(nc, es, P, S, 1, fx)
        if fx is None:
            kg = c.ext("kg", [128, 5, S], BF16)
            vg = c.ext("vg", [4, 128, NT * 129], BF16)
            sg = c.ext("sg", [NOW, 16], F32)
            mscr = nc.dram_tensor("mscr", [NO, 128, S], BF16, kind="Internal").ap()
        else:
            kg = vg = None
            sg = c.tmf
            mscr = fx.internal("mscr", [NO, 128, S], BF16)
        dmt = c.ext("dmt", [128, 512], F32)

        IKT = c.qT[:].rearrange("p a b -> p (a b)")[:, 0:S]
        dmt_t = sb(nc, es, "dmt_t", [128, 512], F32)
        mnp = Pool(nc, es, "mnp", 2, [128, S], BF16)
        iqp = Pool(nc, es, "iqp", 2, [128, 8, 128], BF16)
        sgp = Pool(nc, es, "sgp", 2, [128, 16], F32)
        rp = Pool(nc, es, "rp", 7, [128, 512], BF16)
        dgp = Pool(nc, es, "dgp", 2, [128, 16, 128], BF16)
        thr = Pool(nc, es, "thr", 2, [128, 8], F32)
        scr_bufs = [c.big[:, 0:2 * S].bitcast(F32)]
        scr_guard = [['bufK', 'bufV']]
        if NO >= 2:
            scr_bufs.append(c.y_all[:, 0:NO // 2, :].rearrange("p a b -> p (a b)")[:, 0:2 * S].bitcast(F32))
            scr_guard.append([('y', j_) for j_ in range(NO // 2)])

        P.dma(dmt_t[:], dmt, w=['dmt'])
        b_memory_kv(P, c)
        if fx is None:
            P.dma(IKT, kg[:, 4, :], w=['qT'])
        else:
            dv_ = IKT.rearrange("d (j r p) -> d j r p", r=4, p=128)
            for r_ in range(4):
                P.dma(dv_[:, :, r_, :], c.kall_fm_v[:, 4, :, r_, :], w=['qT'])
        for sb_, sg_ in zip(scr_bufs, scr_guard):
            P.memset('dve', sb_[:, 0:1], 0.0, w=sg_)

        psl = [(c.stp.t[0], ('stp', 0)), (c.stp.t[1], ('stp', 1)), (c.obank[0], ('ob', 0)), (c.obank[1], ('ob', 1)), (c.psm.t[0], ('psm', 0))]
        psl = psl + [(c.pst.t[0][:].rearrange("p a b -> p (a b)").bitcast(F32), ('pst', 0))]
        accl = [(c.obank[2], ('ob', 2)), (c.obank[3], ('ob', 3))]
        psi = [0]
        acci = [0]
        for j in range(NO):
            L = (4 * j + 4) * 128
            sbi = j % len(scr_bufs)
            scr_t = scr_bufs[sbi]
            BB = scr_guard[sbi]
            iq, iqk = iqp.next()
            P.dma(iq[:], c.fmq[:, 12:20, j * 128:(j + 1) * 128], w=[iqk])
            sgt, sgk = sgp.next()
            P.dma(sgt[:], sg[j * 128:(j + 1) * 128, :], w=[sgk])
            dg, dgk = dgp.next()
            for hh in range(16):
                P.ts('pool', dg[:, hh, :], c.ident, sgt[:, hh:hh + 1], None, ALU.mult,
                     r=['cmat', sgk], w=[(dgk, hh)])
            pend = []

            def do_item(item):
                hh_, R__, Rk_, acc_ps_, acck_, cols_, kc5_ = item
                P.mm(acc_ps_[:], dg[:, hh_, :], R__[:], start=(hh_ == 0), stop=(hh_ == 15),
                     r=[(dgk, hh_), Rk_], w=[acck_])
                if hh_ == 15:
                    P.act(scr_t[:, cols_], acc_ps_[:], AF.Copy, r=[acck_] + BB, w=[('scr', sbi, kc5_)])
            for kc5 in range(L // 512):
                cols = slice(kc5 * 512, (kc5 + 1) * 512)
                acc_ps, acck = accl[acci[0] % 2]
                acci[0] += 1
                for hp in range(8):
                    pair = []
                    for hh in (2 * hp, 2 * hp + 1):
                        rows = slice((hh % 2) * 64, (hh % 2) * 64 + 64)
                        ps, psk = psl[psi[0] % len(psl)]
                        psi[0] += 1
                        P.mm(ps[:], iq[rows, hh // 2, :], IKT[rows, cols], r=[iqk, 'qT'], w=[psk])
                        pair.append((hh, ps, psk))
                    for hh, ps, psk in pair:
                        R_, Rk = rp.next()
                        P.act(R_[:], ps[:], AF.Relu, r=[psk], w=[Rk])
                        pend.append((hh, R_, Rk, acc_ps, acck, cols, kc5))
                    while len(pend) > 4:
                        do_item(pend.pop(0))
            while pend:
                do_item(pend.pop(0))
            last = L // 512 - 1
            P.tt('dve', scr_t[:, L - 512:L], scr_t[:, L - 512:L], dmt_t[:], ALU.add, r=[('scr', sbi, last), 'dmt'] + BB,
                 w=[('scr', sbi, last)])
            skeys = [('scr', sbi, k) for k in range(L // 512)] + BB
            th, thk = thr.next()
            mn, mnk = mnp.next()
            P.op('dve', lambda e, a=th, LL=L, sc_=scr_t: e.reduce_max(a[:, 0:1], sc_[:, 0:LL], AX.X), r=skeys, w=[thk])
            P.ts('dve', th[:, 1:2], th[:, 0:1], -16.0, None, ALU.add, r=[thk], w=[thk])
            for k in range(NIT):
                ck = 8.0 / (2 ** k)
                P.ts('dve', th[:, 2:3], th[:, 1:2], ck, None, ALU.add, r=[thk], w=[thk])
                P.op('dve', lambda e, a=th, m_=mn, LL=L, sc_=scr_t: e.tensor_scalar(m_[:, 0:LL], sc_[:, 0:LL], a[:, 2:3], None,
                                                                      ALU.is_ge, ALU.add, accum_out=a[:, 3:4]),
                     r=skeys + [thk], w=[thk, mnk])
                P.ts('dve', th[:, 4:5], th[:, 3:4], float(TOPK) - 0.5, ck, ALU.is_ge, ALU.mult, r=[thk], w=[thk])
                P.tt('dve', th[:, 1:2], th[:, 1:2], th[:, 4:5], ALU.add, r=[thk], w=[thk])
            P.ts('dve', mn[:, 0:L], scr_t[:, 0:L], th[:, 1:2], NEG, ALU.is_lt, ALU.mult, r=skeys + [thk], w=[mnk])
            P.dma(mscr[j][:, 0:L], mn[:, 0:L], r=[mnk], w=[('mscr', j)])

        sc = 128 ** -0.5
        ob = c.obank
        for g in range(4):
            P.dma(c.qT[:, 0:3, :], c.fmq[:, 3 * g:3 * g + 3, :], w=['qT'])
            c.load_k(g, kg)
            c.load_v(g, vg)
            vS = c.bufV[:, 0:NT * 129].rearrange("p (t d) -> p t d", d=129)
            for j in range(NO):
                L = (4 * j + 4) * 128
                mn, mnk = mnp.next()
                P.dma(mn[:, 0:L], mscr[j][:, 0:L], r=[('mscr', j)], w=[mnk])
                qj = c.qT[:, 0:3, j * 128:(j + 1) * 128]
                steps = []
                nk = 4 * j + 4
                for kt in range(nk):
                    masks = [(h * 128, 128, mn[:, kt * 128:(kt + 1) * 128], c.ident, [mnk, 'cmat']) for h in range(3)]
                    pv = [(h * 128, ob[h][:, 0:129], ('ob', h), vS[:, kt, :], ['bufV'], kt == 0, kt == nk - 1)
                          for h in range(3)]
                    steps.append(dict(k=c.bufK[:, kt * 128:(kt + 1) * 128], kkeys=['bufK'], q=qj, qkeys=['qT'],
                                      c0=0, n=384, masks=masks, pv=pv))
                attend(P, c, steps, sc)
                zt, ztk = c.zt_for(j, g * 384, 384)
                for h in range(3):
                    s_, sk = c.sm.next()
                    P.ts('dve', s_[:, 0:1], ob[h][:, 128:129], 1e-30, None, ALU.max, r=[('ob', h)], w=[sk])
                    P.op('dve', lambda e, a=s_: e.reciprocal(a[:, 0:1], a[:, 0:1]), r=[sk], w=[sk])
                    col = (g * 3 + h) * 128
                    P.stt('dve', c.y_all[:, j, col:col + 128], ob[h][:, 0:128], s_[:, 0:1], zt[:, h * 128:(h + 1) * 128],
                          ALU.mult, ALU.mult, r=[('ob', h), sk, ztk], w=[('y', j)])

        b_mem_attention(P, c, 20, 1536, 1536)
        b_out_proj(P, c)
        if fx is None:
            P.emit(es)
    return nc


def b_odd_shared(S, fmo_b, tmo_b):
    kg = fm_gather(fmo_b, [12, 13, 14, 15, 24], S)
    vg = np.stack([tm_gather_vext(tmo_b, h * 128, S) for h in range(4)], axis=0)
    return dict(kg=kg, vg=vg)


def b_odd_inputs(S, r, x_own, fmo_b, tmo_b, tmf_own, w_out, mem_b, mem_norm_gain, w_kv, mem_qk_gain, shared):
    fmo = fmo_b[r]
    tmo = tmo_b[r]
    dm, wm, cm, vf = mask_consts(S, r)
    t = np.arange(128)[:, None]
    dmt = np.zeros((128, 4, 128), np.float32)
    s = np.arange(128)[None, :]
    for u in range(4):
        if u == r:
            dmt[:, u, :] = np.where(s <= t, 0.0, -1e30)
        elif u > r:
            dmt[:, u, :] = -1e30
    m = dict(x=np.ascontiguousarray(x_own), w_out=w_out, mem=mem_b, w_kv=w_kv,
             memg=memg_host(mem_norm_gain, mem_qk_gain[1]), cmat=cmat_host(), dm=dm,
             fmq=np.ascontiguousarray(np.concatenate([fmo[:, 0:12], fmo[:, 16:24], fmo[:, 25:29]], axis=1)),
             tmz=np.ascontiguousarray(tmo[:, 512:2560]),
             kg=shared['kg'], vg=shared['vg'], sg=np.ascontiguousarray(tmf_own[:, :16]),
             dmt=dmt.reshape(128, 512))
    return m


_PROGS = {}


def _prog(name, S):
    key = (name, S)
    if key not in _PROGS:
        if name == 'A0':
            _PROGS[key] = build_A(0, S)
        elif name == 'A1':
            _PROGS[key] = build_A(1, S)
        elif name == 'B0':
            _PROGS[key] = build_B_even(S)
        else:
            _PROGS[key] = build_B_odd(S)
    return _PROGS[key]


def run_layer(S, layer, xb, inp):
    parity = layer % 2
    e = layer // 2
    f32 = lambda a: np.ascontiguousarray(np.asarray(a, dtype=np.float32))
    mg = f32(inp['mem_qk_gain'][layer])
    if parity == 0:
        g = f32(inp['nsa_qk_gain'][e])
        dg = f32(inp['diff_qk_gain'][e])
        gcols = [g[0], g[2], g[3], np.tile(dg[0], 2), np.tile(dg[1], 2), mg[0]]
        w_in = f32(inp['even_w_in'][e])
    else:
        g = f32(inp['dsa_qk_gain'][e])
        gcols = [g[0], g[1], mg[0]]
        w_in = f32(inp['odd_w_in'][e])
    ng = f32(inp['norm_gain'][layer])
    owns = [[np.ascontiguousarray(xb[b][own_positions(S, r)]) for r in range(4)] for b in range(2)]
    maps = [a_inputs(parity, S, owns[cc // 4][cc % 4], w_in, ng, gcols, cc % 4) for cc in range(8)]
    resA = run_bass_kernel_spmd(_prog('A%d' % parity, S), maps, core_ids=list(range(8))).results
    w_out = f32(inp['w_out'][layer])
    w_kv = f32(inp['mem_w_kv'][layer])
    mng = f32(inp['mem_norm_gain'])
    maps = []
    for b in range(2):
        fmo_b = [np.asarray(resA[b * 4 + r]['fmo']) for r in range(4)]
        tmo_b = [np.asarray(resA[b * 4 + r]['tmo']) for r in range(4)]
        tmf_b = [np.asarray(resA[b * 4 + r]['tmf']) for r in range(4)]
        mem_b = f32(inp['mem'][b])
        if parity == 0:
            shared = b_even_shared(S, fmo_b, tmo_b)
            lambda_init = 0.8 - 0.6 * math.exp(-0.3 * layer)
            for r in range(4):
                maps.append(b_even_inputs(S, r, owns[b][r], fmo_b, tmo_b, tmf_b[r], w_out, mem_b, mng, w_kv, mg,
                                          f32(inp['nsa_qk_gain'][e]), f32(inp['nsa_cmp_pos'][e]), f32(inp['nsa_cmp_w1'][e]),
                                          f32(inp['nsa_cmp_w2'][e]), f32(inp['diff_subln_gain'][e]),
                                          f32(inp['diff_lambda'][e]), shared, lambda_init))
        else:
            shared = b_odd_shared(S, fmo_b, tmo_b)
            for r in range(4):
                maps.append(b_odd_inputs(S, r, owns[b][r], fmo_b, tmo_b, tmf_b[r], w_out, mem_b, mng, w_kv, mg, shared))
    resB = run_bass_kernel_spmd(_prog('B%d' % parity, S), maps, core_ids=list(range(8))).results
    out = []
    for b in range(2):
        xn = np.empty((S, D), np.float32)
        for r in range(4):
            xn[own_positions(S, r)] = np.asarray(resB[b * 4 + r]['xo'])
        out.append(xn)
    return out


def kernel_unfused(**inputs):
    x = np.asarray(inputs['x'], dtype=np.float32)
    S = x.shape[1]
    xb = [np.ascontiguousarray(x[b]) for b in range(x.shape[0])]
    for layer in range(4):
        xb = run_layer(S, layer, xb, inputs)
    return np.stack(xb, axis=0).astype(np.float32)


def kernel(**inputs):
    return kernel_fused(inputs, 4).astype(np.float32)


def build_fused(S, depth=4):
    NT, NO, NOW, NC, n_cmp, NCT, n_sel = geom(S)
    nc = bass.Bass("TRN2", target_bir_lowering=False)
    P = Prog(nc)
    fx = FX(nc, P, S)
    x_in = nc.dram_tensor("x", [NOW, D], F32, kind="ExternalInput").ap()
    out = nc.dram_tensor("out", [NOW, D], F32, kind="ExternalOutput").ap()
    xs = [x_in] + [nc.dram_tensor(f"xs{l}", [NOW, D], F32, kind="Internal").ap() for l in range(1, depth)] + [out]
    groups = [[0, 1, 2, 3], [4, 5, 6, 7]]
    ccsrc = nc.dram_tensor("ccsrc", [256, 2048], BF16, kind="Internal").ap()
    ccdst = nc.dram_tensor("ccdst", [1024, 2048], BF16, kind="Internal").ap()
    for layer in range(depth):
        parity = layer % 2
        fx.layer = layer
        fx.x_src = xs[layer]
        fx.x_dst = xs[layer + 1]
        build_A(parity, S, fx)
        P.barrier()
        kown_fm = fx.internal("kown_fm", [NK[parity] * 128, NOW], BF16)
        kall_fm = fx.internal("kall_fm", [4 * NK[parity] * 128, NOW], BF16)
        kown_v = fx.internal("kown_v", [NV[parity] * 128, NO * 129], BF16)
        kall_v = fx.internal("kall_v", [4 * NV[parity] * 128, NO * 129], BF16)
        def gather(src_rows, dst_view, R_, C_):
            sv = ccsrc.rearrange("a b -> (a b)")[0:R_ * C_].rearrange("(r c) -> r c", c=C_)
            dv = ccdst.rearrange("a b -> (a b)")[0:4 * R_ * C_].rearrange("(r c) -> r c", c=C_)
            P.dma(sv, src_rows, r=['ccdst'], w=['ccsrc'])
            P.op('pool', lambda e, a=sv, b=dv: e.collective_compute(
                "AllGather", ALU.bypass, replica_groups=groups, ins=[a], outs=[b]), r=['ccsrc'], w=['ccdst'])
            P.dma(dst_view, dv.rearrange("(k r) c -> k r c", k=4), r=['ccdst'], w=[('kall', layer)])
        nk_, nv_ = NK[parity], NV[parity]
        kall_fm_k = kall_fm.rearrange("(k sd) n -> k sd n", k=4)
        for s0 in range(0, nk_, 2):
            ns = min(2, nk_ - s0)
            gather(kown_fm[s0 * 128:(s0 + ns) * 128, :], kall_fm_k[:, s0 * 128:(s0 + ns) * 128, :], ns * 128, NOW)
        kall_v_k = kall_v.rearrange("(k up) n -> k up n", k=4)
        for u in range(nv_):
            gather(kown_v[u * 128:(u + 1) * 128, :], kall_v_k[:, u * 128:(u + 1) * 128, :], 128, NO * 129)
        P.barrier()
        if parity == 0:
            build_B_even(S, fx)
        else:
            build_B_odd(S, fx)
        P.barrier()
    with ExitStack() as es:
        P.emit(es)
    return nc, fx


def fused_inputs(S, inp, depth=4):
    f32 = lambda a: np.ascontiguousarray(np.asarray(a, dtype=np.float32))
    x = f32(inp['x'])
    ctab, ovm, expand = even_consts(S)
    sel = np.zeros((16, 8, 128), np.float32)
    for cc_ in range(8):
        for mm_ in range(128):
            sel[2 * cc_ + mm_ // 64, cc_, mm_] = 1
    shared = dict(cmat=cmat_host(), ctab=ctab, ovm=ovm, expand=expand, sel=sel.astype(NPBF))
    mng = f32(inp['mem_norm_gain'])
    for layer in range(depth):
        e = layer // 2
        L = f"_L{layer}"
        mg = f32(inp['mem_qk_gain'][layer])
        shared['w_out' + L] = f32(inp['w_out'][layer])
        shared['w_kv' + L] = f32(inp['mem_w_kv'][layer])
        shared['memg' + L] = memg_host(mng, mg[1])
        shared['ng' + L] = np.ascontiguousarray(f32(inp['norm_gain'][layer]).reshape(KC, 128).T)
        if layer % 2 == 0:
            g = f32(inp['nsa_qk_gain'][e])
            dg = f32(inp['diff_qk_gain'][e])
            gcols = [g[0], g[2], g[3], np.tile(dg[0], 2), np.tile(dg[1], 2), mg[0]]
            shared['w_in' + L] = f32(inp['even_w_in'][e])
            lambda_init = 0.8 - 0.6 * math.exp(-0.3 * layer)
            gv = np.zeros((128, 388), np.float32)
            gv[:, 0] = g[1]
            gv[:, 2:130] = f32(inp['diff_subln_gain'][e])[None, :]
            gv[:, 130:386] = f32(inp['diff_lambda'][e]).reshape(1, 256)
            gv[:, 386] = lambda_init
            gv[:, 387] = 1.0 - lambda_init
            shared['gv' + L] = gv
            shared['cw1' + L] = f32(inp['nsa_cmp_w1'][e])
            shared['cw2' + L] = np.ascontiguousarray(f32(inp['nsa_cmp_w2'][e]).reshape(2, 2, 128, 128).transpose(2, 0, 1, 3))
            shared['peT' + L] = np.ascontiguousarray(f32(inp['nsa_cmp_pos'][e]).transpose(2, 0, 1)).astype(NPBF)
        else:
            g = f32(inp['dsa_qk_gain'][e])
            gcols = [g[0], g[1], mg[0]]
            shared['w_in' + L] = f32(inp['odd_w_in'][e])
        shared['gains' + L] = np.ascontiguousarray(np.stack(gcols, axis=1).astype(np.float32))
    maps = []
    for core in range(8):
        b, r = core // 4, core % 4
        m = dict(shared)
        m['x'] = np.ascontiguousarray(x[b][own_positions(S, r)])
        m['mem'] = f32(inp['mem'][b])
        m['tabs'] = tabs_host(S, r)
        dm, wm, cm, vf = mask_consts(S, r)
        m.update(dm=dm, wm=wm, cm=cm, visfrc=vf)
        t = np.arange(128)[:, None]
        s_ = np.arange(128)[None, :]
        dmt = np.zeros((128, 4, 128), np.float32)
        for u in range(4):
            if u == r:
                dmt[:, u, :] = np.where(s_ <= t, 0.0, -1e30)
            elif u > r:
                dmt[:, u, :] = -1e30
        m['dmt'] = dmt.reshape(128, 512)
        maps.append(m)
    return maps


_FUSED = {}


def kernel_fused(inputs, depth=4):
    x = np.asarray(inputs['x'], dtype=np.float32)
    S = x.shape[1]
    import time as _t
    t0_ = _t.time()
    key = (S, depth)
    if key not in _FUSED:
        _FUSED[key] = build_fused(S, depth)
    nc, fx = _FUSED[key]
    print("[fused] build", round(_t.time() - t0_, 1), flush=True)
    maps = fused_inputs(S, inputs, depth)
    print("[fused] inputs", round(_t.time() - t0_, 1), flush=True)
    names = set(fx.cache.keys())
    maps = [{k: v for k, v in m.items() if k == 'x' or (k in fx.ext_shapes)} for m in maps]
    res = run_bass_kernel_spmd(nc, maps, core_ids=list(range(8))).results
    print("[fused] run", round(_t.time() - t0_, 1), flush=True)
    out = np.empty_like(x)
    for core in range(8):
        b, r = core // 4, core % 4
        out[b][own_positions(S, r)] = np.asarray(res[core]['out'])
    return out
```
